# Optimizing a Trainium2 kernel written in Bass

```python
import jax, jax.numpy as jnp
from jax import lax
import numpy as np

D_MODEL = 1024
BATCH = 2
SEQ = 8192
DEPTH = 4

MIX_WIDTH = D_MODEL
HGRN_WIDTH = MIX_WIDTH // 2
HGRN_HEADS = 4
HGRN_HEAD_DIM = HGRN_WIDTH // HGRN_HEADS
CONV_WIDTH = MIX_WIDTH - HGRN_WIDTH
CONV_GROUPS = 8
CONV_K = 3
FFN_CONV_K = 3
D_FF = 2816
CHUNK = 64
EPS = 1e-6
IN_COLS = 4 * HGRN_WIDTH + 3 * CONV_WIDTH

kernel_name = "hgrn2_shortconv_parallel_hybrid"


def rmsnorm(x, g):
    xf = x.astype(jnp.float32)
    y = xf * lax.rsqrt(jnp.mean(xf * xf, axis=-1, keepdims=True) + EPS)
    return (y * g.astype(jnp.float32)).astype(x.dtype)


def causal_dwconv(x, w):
    k = w.shape[0]
    c = x.shape[-1]
    return lax.conv_general_dilated(
        x, w[:, None, :].astype(x.dtype), window_strides=(1,), padding=[(k - 1, 0)],
        dimension_numbers=("NWC", "WIO", "NWC"), feature_group_count=c)


def to_chunks(t):
    b, t_len, _ = t.shape
    t = t.reshape(b, t_len // CHUNK, CHUNK, HGRN_HEADS, HGRN_HEAD_DIM)
    return t.transpose(1, 0, 3, 2, 4)


def from_chunks(t):
    n, b, h, c, d = t.shape
    return t.transpose(1, 0, 3, 2, 4).reshape(b, n * c, h * d)


def hgrn2_chunkwise(q, k, v, log_f):
    bsz = q.shape[0]
    causal = jnp.tril(jnp.ones((CHUNK, CHUNK), dtype=bool))

    def step(state, inp):
        qc, kc, vc, gc = inp
        b = jnp.cumsum(gc, axis=-2)
        diff = b[..., :, None, :] - b[..., None, :, :]
        decay = jnp.exp(jnp.where(causal[:, :, None], diff, -jnp.inf))
        scores = jnp.einsum('bhtk,bhsk,bhtsk->bhts', qc, kc, decay)
        o = (jnp.einsum('bhts,bhsv->bhtv', scores, vc)
             + jnp.einsum('bhtk,bhkv->bhtv', qc * jnp.exp(b), state))
        b_last = b[..., -1:, :]
        state = (jnp.exp(b_last[..., 0, :])[..., None] * state
                 + jnp.einsum('bhsk,bhsv->bhkv', kc * jnp.exp(b_last - b), vc))
        return state, o

    s0 = jnp.zeros((bsz, HGRN_HEADS, HGRN_HEAD_DIM, HGRN_HEAD_DIM), jnp.float32)
    _, o = lax.scan(step, s0, (to_chunks(q), to_chunks(k), to_chunks(v), to_chunks(log_f)))
    return from_chunks(o)


def hgrn2_mixer(q_pre, f_pre, i_pre, g_pre, lb, norm_g):
    dt = q_pre.dtype
    fp = f_pre.astype(jnp.float32)
    lbf = lb.astype(jnp.float32)
    q = jax.nn.silu(q_pre.astype(jnp.float32)) * (HGRN_HEAD_DIM ** -0.5)
    log_f = jnp.logaddexp(jnp.log(lbf), jnp.log1p(-lbf) + jax.nn.log_sigmoid(fp))
    k = (1.0 - lbf) * jax.nn.sigmoid(-fp)
    v = i_pre.astype(jnp.float32)
    o = hgrn2_chunkwise(q, k, v, log_f)
    bsz, t_len, _ = o.shape
    oh = o.reshape(bsz, t_len, HGRN_HEADS, HGRN_HEAD_DIM)
    oh = oh * lax.rsqrt(jnp.mean(oh * oh, axis=-1, keepdims=True) + EPS)
    o = oh.reshape(bsz, t_len, HGRN_WIDTH) * norm_g.astype(jnp.float32)
    return (o * jax.nn.silu(g_pre.astype(jnp.float32))).astype(dt)


def setup_inputs(seed: int = 0) -> dict:
    key = jax.random.key(seed)
    ks = jax.random.split(key, 14)
    f32 = jnp.float32

    def gain(k, shape):
        return 1.0 + 0.05 * jax.random.normal(k, shape, f32)

    return {
        "x": jax.random.normal(ks[0], (BATCH, SEQ, D_MODEL), f32),
        "lb_param": jax.random.normal(ks[1], (DEPTH, HGRN_WIDTH), f32),
        "w_in": jax.random.normal(ks[2], (DEPTH, D_MODEL, IN_COLS), f32) * D_MODEL ** -0.5,
        "w_out": jax.random.normal(ks[3], (DEPTH, MIX_WIDTH, D_MODEL), f32) * MIX_WIDTH ** -0.5,
        "conv_w": jax.random.normal(ks[4], (DEPTH, CONV_K, CONV_WIDTH), f32) * CONV_K ** -0.5,
        "ffn_w_up": jax.random.normal(ks[5], (DEPTH, D_MODEL, 2 * D_FF), f32) * D_MODEL ** -0.5,
        "ffn_conv_w": jax.random.normal(ks[6], (DEPTH, FFN_CONV_K, 2 * D_FF), f32) * FFN_CONV_K ** -0.5,
        "ffn_w_down": jax.random.normal(ks[7], (DEPTH, D_FF, D_MODEL), f32) * D_FF ** -0.5,
        "hgrn_norm_g": gain(ks[8], (DEPTH, HGRN_WIDTH)),
        "pre_mix_g": gain(ks[9], (DEPTH, D_MODEL)),
        "post_mix_g": gain(ks[10], (DEPTH, D_MODEL)),
        "pre_ffn_g": gain(ks[11], (DEPTH, D_MODEL)),
        "post_ffn_g": gain(ks[12], (DEPTH, D_MODEL)),
    }


def reference(x, lb_param, w_in, w_out, conv_w, ffn_w_up, ffn_conv_w, ffn_w_down,
              hgrn_norm_g, pre_mix_g, post_mix_g, pre_ffn_g, post_ffn_g):
    lb_all = jnp.cumsum(jax.nn.softmax(lb_param.astype(jnp.float32), axis=0), axis=0)
    lb_all = lb_all - lb_all[0:1]
    split_pts = [HGRN_WIDTH * j for j in range(1, 5)] + [4 * HGRN_WIDTH + CONV_WIDTH * j for j in range(1, 3)]

    for l in range(DEPTH):
        h = rmsnorm(x, pre_mix_g[l])
        z = jnp.einsum('btd,dc->btc', h, w_in[l])
        q_p, f_p, i_p, g_p, gate_b, gate_c, conv_h = jnp.split(z, split_pts, axis=-1)
        mix_a = hgrn2_mixer(q_p, f_p, i_p, g_p, lb_all[l], hgrn_norm_g[l])
        mix_b = gate_b * causal_dwconv(gate_c * conv_h, conv_w[l])
        mix = jnp.einsum('btc,cd->btd', jnp.concatenate([mix_a, mix_b], axis=-1), w_out[l])
        x = x + rmsnorm(mix, post_mix_g[l])
        h = rmsnorm(x, pre_ffn_g[l])
        u = causal_dwconv(jnp.einsum('btd,df->btf', h, ffn_w_up[l]), ffn_conv_w[l])
        gate, val = jnp.split(u, 2, axis=-1)
        y = jnp.einsum('btf,fd->btd', jax.nn.silu(gate) * val, ffn_w_down[l])
        x = x + rmsnorm(y, post_ffn_g[l])
    return x
```

```python
import os
from contextlib import ExitStack

import numpy as np
import ml_dtypes

import concourse.bass as bass
import concourse.mybir as mybir
from concourse.bass_utils import run_bass_kernel_spmd

F32 = mybir.dt.float32
BF16 = mybir.dt.bfloat16
AF = mybir.ActivationFunctionType
ALU = mybir.AluOpType

D = 1024
T = 2048
TB = 512
NTB = T // TB
DEPTH = 4
NH = 4
DFF = 2816
NFC = DFF // 128
EPS = 1e-6
QSCALE = 128 ** -0.5

P_LBP = 0
P_NG = P_LBP + 16
P_GPM = P_NG + 16
P_GQM = P_GPM + 32
P_GPF = P_GQM + 32
P_GQF = P_GPF + 32
P_CW = P_GQF + 32
P_FCW = P_CW + 48
P_SEL = P_FCW + 528
P_MLT = P_SEL + 4
P_OMM = P_MLT + 4
P_MA = P_OMM + 4
P_MB = P_MA + 1
NPAR = P_MB + 1
C_ONES = 0
C_ID = 128
C_MASK = 256
C_RST = 768
C_ONE = 1280
NCB = 1792

NSLOT = 6


class TT:
    __slots__ = ("w", "r")

    def __init__(self):
        self.w = None
        self.r = {}


class Tile:
    def __init__(self, t):
        self.t = t
        self.units = {}

    def u(self, key=0):
        x = self.units.get(key)
        if x is None:
            x = self.units[key] = TT()
        return x

    def __getitem__(self, idx):
        return self.t[idx]


class Sched:
    ENG = ("pe", "act", "dve", "pool", "sp")

    def __init__(self, nc, es):
        self.nc = nc
        self.q = {e: [] for e in self.ENG}
        self.cnt = {e: 0 for e in self.ENG}
        self.waited = {e: {} for e in self.ENG}
        self.sem = {e: es.enter_context(nc.semaphore("c_" + e)) for e in self.ENG}
        self.dsem = [[es.enter_context(nc.semaphore("d%d" % i)), 0] for i in range(24)]
        self.drr = 0
        self.es = es
        self.cccnt = 0
        self.semh = dict(self.sem)
        for i, (h, _) in enumerate(self.dsem):
            self.semh["d%d" % i] = h
        self.nops = 0
        self.enabled = True
        self.seen = {}
        self.cap = None

    def _wait(self, eng, key, val):
        if eng == "pe" and key == "pe":
            return
        if key in self.cnt:
            assert val <= self.cnt[key], ("wait on future signal", eng, key, val, self.cnt[key])
        if self.waited[eng].get(key, 0) >= val:
            return
        self.waited[eng][key] = val
        if self.seen.get(key, 0) < val:
            self.seen[key] = val
        self.q[eng].append(("w", key, val))

    def _deps(self, eng, reads, writes):
        for t in reads:
            if t.w is not None:
                self._wait(eng, *t.w)
        for t in writes:
            if t.w is not None:
                self._wait(eng, *t.w)
            for k, v in t.r.items():
                self._wait(eng, k, v)

    def _mark(self, sig, reads, writes):
        k, v = sig
        for t in reads:
            if t.r.get(k, 0) < v:
                t.r[k] = v
        for t in writes:
            t.w = sig
            t.r = {}

    def capture(self, f):
        assert self.cap is None
        self.cap = []
        try:
            f()
        finally:
            lst, self.cap = self.cap, None
        return lst

    def op(self, eng, fn, r=(), w=(), signal=True):
        if not self.enabled:
            return
        if self.cap is not None:
            self.cap.append((eng, lambda: self.op(eng, fn, r, w, signal)))
            return
        self._deps(eng, r, w)
        self.nops += 1
        if signal:
            self.cnt[eng] += 1
            sig = (eng, self.cnt[eng])
            self.q[eng].append(("i", fn, eng, 1))
        else:
            sig = (eng, self.cnt[eng] + 1)
            self.q[eng].append(("i", fn, None, 0))
        self._mark(sig, r, w)

    def dma(self, eng, fn, r=(), w=()):
        if not self.enabled:
            return None
        if self.cap is not None:
            self.cap.append((eng, lambda: self.dma(eng, fn, r, w)))
            return None
        self._deps(eng, r, w)
        i = self.drr
        self.drr = (self.drr + 1) % len(self.dsem)
        key = "d%d" % i
        prev = self.dsem[i][1]
        if prev and self.seen.get(key, 0) < prev:
            self._wait(eng, key, prev)
        self.dsem[i][1] = prev + 16
        sig = (key, prev + 16)
        self.q[eng].append(("i", fn, key, 16))
        self._mark(sig, r, w)
        return sig

    def collective(self, fn, r=(), w=()):
        if not self.enabled:
            return
        if self.cap is not None:
            self.cap.append(("pool", lambda: self.collective(fn, r, w)))
            return
        self._deps("pool", r, w)
        self.cccnt += 1
        key = "cc%d" % self.cccnt
        self.semh[key] = self.es.enter_context(self.nc.semaphore(key))
        sig = (key, 1)
        self.q["pool"].append(("i", fn, key, 1))
        self._mark(sig, r, w)

    def barrier(self):
        if not self.enabled:
            return
        cur = [(e, self.cnt[e]) for e in self.ENG if self.cnt[e]]
        cur += [("d%d" % i, c) for i, (_, c) in enumerate(self.dsem) if c]
        cur += [("cc%d" % i, 1) for i in range(1, self.cccnt + 1)]
        for e in self.ENG:
            for k, v in cur:
                if k == e:
                    continue
                self._wait(e, k, v)

    def emit(self):
        nc = self.nc
        engs = {"pe": "tensor", "act": "scalar", "dve": "vector", "pool": "gpsimd", "sp": "sync"}
        with nc.Block() as block:
            for e in self.ENG:
                items = self.q[e]

                def body(eng, items=items):
                    for it in items:
                        if it[0] == "w":
                            eng.wait_ge(self.semh[it[1]], it[2])
                        else:
                            ins = it[1](eng)
                            if it[2] is not None:
                                ins.then_inc(self.semh[it[2]], it[3])

                getattr(block, engs[e])(body)


def build_program(nlayers=DEPTH, dbg=None, lbase=0):
    nc = bass.Bass("TRN2", target_bir_lowering=False)
    xin = nc.dram_tensor("xin", [128, 8 * T], F32, kind="ExternalInput").ap()
    w_in = nc.dram_tensor("w_in", [nlayers * 28 * 128, 1024], F32, kind="ExternalInput").ap()
    w_out = nc.dram_tensor("w_out", [nlayers * 8 * 128, 1024], F32, kind="ExternalInput").ap()
    w_up = nc.dram_tensor("w_up", [nlayers * 44 * 128, 1024], F32, kind="ExternalInput").ap()
    w_dn = nc.dram_tensor("w_dn", [nlayers * 8 * 128, NFC * 128], F32, kind="ExternalInput").ap()
    par_d = nc.dram_tensor("par", [128, NPAR], F32, kind="ExternalInput").ap()
    cb_d = nc.dram_tensor("cb", [128, NCB], BF16, kind="ExternalInput").ap()
    out = nc.dram_tensor("out", [128, 8 * T], F32, kind="ExternalOutput").ap()

    es = ExitStack()
    with es:
        S = Sched(nc, es)

        uid = [0]

        def sb(name, shape, dt, stack=es):
            uid[0] += 1
            return Tile(stack.enter_context(nc.sbuf_tensor("%s_%d" % (name, uid[0]), shape, dt)))

        def ps(name, shape, dt, stack=es):
            uid[0] += 1
            return Tile(stack.enter_context(nc.psum_tensor("%s_%d" % (name, uid[0]), shape, dt)))

        xT = sb("xT", [128, 8 * T], F32)
        par = sb("par", [128, NPAR], F32)
        cb = sb("cb", [128, NCB], BF16)
        lbt = sb("lbt", [128, 64], F32)
        wring = [sb("wr%d" % i, [128, 1024], BF16) for i in range(NSLOT)]
        sqr = [sb("sq%d" % i, [128, TB], BF16) for i in range(2)]
        lnv = [sb("lnv%d" % i, [128, TB], F32) for i in range(1)]
        rstd = [sb("rstd%d" % i, [128, TB], F32) for i in range(2)]
        xs_h = sb("xs_h", [128, 16], F32)
        xg_h = sb("xg_h", [128, 64], F32)
        hal = sb("hal", [128, 16], F32)
        pring = [ps("pb%d" % i, [128, TB], F32) for i in range(5)]
        pnorm = [ps("pn%d" % i, [128, TB], F32) for i in range(2)]
        ptr = ps("ptr", [128, 1024], BF16)
        st = {"wr": 0, "pr": 0, "sq": 0, "ln": 0, "ex": 0}

        ones_bf = cb[:, C_ONES:C_ONES + 128]
        ident_bf = cb[:, C_ID:C_ID + 128]

        def pcol(off):
            return par[:, off:off + 1]

        def wslot():
            i = st["wr"]
            st["wr"] = (i + 1) % NSLOT
            return wring[i]

        def pbank():
            i = st["pr"]
            st["pr"] = (i + 1) % len(pring)
            return pring[i]

        def load_w(src_ap):
            sl = wslot()
            ncol = src_ap.shape[1]
            S.dma("pool", lambda e, sl=sl, src_ap=src_ap, ncol=ncol: e.dma_start(out=sl[:, 0:ncol], in_=src_ap), w=[sl.u()])
            return sl

        def mm(out_ap, lhsT, rhs, start, stop, r, w, signal):
            S.op("pe", lambda e: e.matmul(out_ap, lhsT=lhsT, rhs=rhs, start=start, stop=stop), r=r, w=w, signal=signal)

        def act(out_ap, in_ap, func, r, w, scale=None, bias=None):
            kw = {}
            if scale is not None:
                kw["scale"] = scale
            if bias is not None:
                kw["bias"] = bias
            S.op("act", lambda e: e.activation(out=out_ap, in_=in_ap, func=func, **kw), r=r, w=w)

        def tsc(out_ap, in0, s1, s2, op0, op1, r, w, eng="dve"):
            if s2 is None:
                S.op(eng, lambda e: e.tensor_scalar(out=out_ap, in0=in0, scalar1=s1, scalar2=None, op0=op0), r=r, w=w)
            else:
                S.op(eng, lambda e: e.tensor_scalar(out=out_ap, in0=in0, scalar1=s1, scalar2=s2, op0=op0, op1=op1), r=r, w=w)

        def stt(out_ap, in0, sc, in1, op0, op1, r, w):
            S.op("dve", lambda e: e.scalar_tensor_tensor(out=out_ap, in0=in0, scalar=sc, in1=in1, op0=op0, op1=op1), r=r, w=w)

        def tt(out_ap, in0, in1, op, r, w, eng="dve"):
            S.op(eng, lambda e: e.tensor_tensor(out=out_ap, in0=in0, in1=in1, op=op), r=r, w=w)

        def cpy(out_ap, in_ap, r, w, eng="dve"):
            S.op(eng, lambda e: e.tensor_copy(out_ap, in_ap), r=r, w=w)

        def exchange(src, ncols, tag, runits=None):
            i = st["ex"]
            st["ex"] += 1
            agin = nc.dram_tensor("agin%d" % i, [128, ncols], F32)
            agout = nc.dram_tensor("agout%d" % i, [4 * 128, ncols], F32)
            gi, go = TT(), TT()
            S.dma("sp", lambda e: e.dma_start(out=agin.ap()[:, :], in_=src[:, 0:ncols]), r=(runits or [src.u()]), w=[gi])
            S.collective(lambda e: e.collective_compute("AllGather", ALU.bypass, replica_groups=[[0, 1, 2, 3], [4, 5, 6, 7]],
                                                        ins=[agin.ap().opt()], outs=[agout.ap().opt()]), r=[gi], w=[go])
            return agout, go

        def exchange_recv(agout, go, dst, ncols, nr=4):
            S.dma("sp", lambda e: e.dma_start(out=dst[:, 0:nr * ncols].rearrange("p (r c) -> p r c", r=nr),
                                              in_=agout.ap()[0:nr * 128, :].rearrange("(r p) c -> p r c", p=128)), r=[go], w=[dst.u()])

        def norm_stats(src_fn, nk, tb_key, scale, pn):
            for k in range(nk):
                ap, un = src_fn(k)
                sq = sqr[st["sq"]]
                st["sq"] = (st["sq"] + 1) % 2
                act(sq[:, :], ap, AF.Square, r=un, w=[sq.u()])
                mm(pn[:, :], ones_bf, sq[:, :], k == 0, k == nk - 1, r=[sq.u(), cb.u()], w=[pn.u()], signal=True)
            return finish_rstd(pn, scale)

        def finish_rstd(pn, scale):
            i = st["ln"]
            st["ln"] = (i + 1) % 2
            lv, rs = lnv[0], rstd[i]
            act(lv[:, :], pn[:, :], AF.Ln, r=[pn.u()], w=[lv.u()], scale=scale, bias=EPS)
            act(rs[:, :], lv[:, :], AF.Exp, r=[lv.u()], w=[rs.u()], scale=-0.5)
            return rs

        for k in range(8):
            for tb in range(NTB):
                c0 = k * T + tb * TB
                S.dma("sp", lambda e, c0=c0: e.dma_start(out=xT[:, c0:c0 + TB], in_=xin[:, c0:c0 + TB]), w=[xT.u((k, tb))])
        S.dma("sp", lambda e: e.dma_start(out=par[:, :], in_=par_d[:, :]), w=[par.u()])
        S.dma("sp", lambda e: e.dma_start(out=cb[:, :], in_=cb_d[:, :]), w=[cb.u()])
        L = lambda l: lbt[:, 48 + 0:48 + 4]
        mx = lbt[:, 48:52]
        tt(mx, par[:, P_LBP:P_LBP + 4], par[:, P_LBP + 4:P_LBP + 8], ALU.max, r=[par.u()], w=[lbt.u()])
        tt(mx, mx, par[:, P_LBP + 8:P_LBP + 12], ALU.max, r=[lbt.u()], w=[lbt.u()])
        tt(mx, mx, par[:, P_LBP + 12:P_LBP + 16], ALU.max, r=[lbt.u()], w=[lbt.u()])
        for l in range(4):
            tt(lbt[:, 4 * l:4 * l + 4], par[:, P_LBP + 4 * l:P_LBP + 4 * l + 4], mx, ALU.subtract, r=[par.u(), lbt.u()], w=[lbt.u()])
        act(lbt[:, 0:16], lbt[:, 0:16], AF.Exp, r=[lbt.u()], w=[lbt.u()])
        sm = lbt[:, 52:56]
        tt(sm, lbt[:, 0:4], lbt[:, 4:8], ALU.add, r=[lbt.u()], w=[lbt.u()])
        tt(sm, sm, lbt[:, 8:12], ALU.add, r=[lbt.u()], w=[lbt.u()])
        tt(sm, sm, lbt[:, 12:16], ALU.add, r=[lbt.u()], w=[lbt.u()])
        S.op("dve", lambda e: e.reciprocal(sm, sm), r=[lbt.u()], w=[lbt.u()])
        for l in range(4):
            tt(lbt[:, 4 * l:4 * l + 4], lbt[:, 4 * l:4 * l + 4], sm, ALU.mult, r=[lbt.u()], w=[lbt.u()])
        tt(lbt[:, 8:12], lbt[:, 8:12], lbt[:, 4:8], ALU.add, r=[lbt.u()], w=[lbt.u()])
        tt(lbt[:, 12:16], lbt[:, 12:16], lbt[:, 8:12], ALU.add, r=[lbt.u()], w=[lbt.u()])
        S.op("dve", lambda e: e.memset(lbt[:, 0:4], 0.0), w=[lbt.u()])
        tsc(lbt[:, 16:32], lbt[:, 0:16], -1.0, 1.0, ALU.mult, ALU.add, r=[lbt.u()], w=[lbt.u()])
        tsc(lbt[:, 32:48], lbt[:, 16:32], -1.0, None, ALU.mult, None, r=[lbt.u()], w=[lbt.u()])

        HW = T + 2

        class _Stop(Exception):
            pass

        def bp(l, name):
            if dbg == (l, name):
                S.barrier()
                S.enabled = False

        for l in range(nlayers):
          try:
            lp = l + lbase
            bp(l, "pre")
            with ExitStack() as mixs:
                mixT = sb("mixT", [128, 8 * T], BF16, mixs)
                hts = ExitStack()
                hT = sb("hT", [128, 8 * HW], BF16, hts)

                def prenorm(hT, hw, gofs, tbs, toff):
                    for tb in tbs:
                        rs = norm_stats(lambda k: (xT[:, k * T + tb * TB:k * T + (tb + 1) * TB], [xT.u((k, tb))]), 8, tb, 1.0 / D, pnorm[tb % 2])
                        for k in range(8):
                            c0 = k * hw + 2 + (tb - toff) * TB
                            stt(hT[:, c0:c0 + TB], xT[:, k * T + tb * TB:k * T + (tb + 1) * TB], pcol(gofs + 8 * lp + k), rs[:, :],
                                ALU.mult, ALU.mult, r=[xT.u((k, tb)), rs.u(), par.u()], w=[hT.u((k, tb - toff))])

                def halo_exchange(hT, hw, last_c0, last_key):
                    for k in range(8):
                        cpy(xs_h[:, 2 * k:2 * k + 2], hT[:, k * hw + last_c0:k * hw + last_c0 + 2], r=[hT.u((k, last_key))], w=[xs_h.u()])
                    ag, go = exchange(xs_h, 16, "h")
                    exchange_recv(ag, go, xg_h, 16)
                    tsc(hal[:, :], xg_h[:, 0:16], pcol(P_SEL + 0), None, ALU.mult, None, r=[xg_h.u(), par.u()], w=[hal.u()])
                    for i in range(1, 4):
                        stt(hal[:, :], xg_h[:, 16 * i:16 * i + 16], pcol(P_SEL + i), hal[:, :], ALU.mult, ALU.add,
                            r=[xg_h.u(), hal.u(), par.u()], w=[hal.u()])
                    for k in range(8):
                        cpy(hT[:, k * hw:k * hw + 2], hal[:, 2 * k:2 * k + 2], r=[hal.u()], w=[hT.u((k, "halo"))])

                prenorm(hT, HW, P_GPM, range(NTB), 0)
                bp(l, "norm")
                halo_exchange(hT, HW, 2 + T - 2, NTB - 1)
                bp(l, "halo")

                def hrhs(k, tb):
                    c0 = k * HW + 2 + tb * TB
                    return hT[:, c0:c0 + TB]

                def inproj_fm(wsl, tb, bank):
                    for k in range(8):
                        mm(bank[:, :], wsl[:, k * 128:(k + 1) * 128], hrhs(k, tb), k == 0, k == 7,
                           r=[wsl.u(), hT.u((k, tb))], w=[bank.u()], signal=(k == 7))

                def inproj_halo(wsl, bank, col):
                    for k in range(8):
                        mm(bank[:, col:col + 2], wsl[:, k * 128:(k + 1) * 128], hT[:, k * HW:k * HW + 2], k == 0, k == 7,
                           r=[wsl.u(), hT.u((k, "halo"))], w=[bank.u()], signal=(k == 7))

                with ExitStack() as mxs:
                    with ExitStack() as hs:
                        tq = sb("tq", [128, TB], F32, hs)
                        tsig = sb("tsig", [128, TB], F32, hs)
                        tk = sb("tk", [128, TB], F32, hs)
                        tb_ = sb("tb_", [128, TB], F32, hs)
                        tB = sb("tB", [128, TB], F32, hs)
                        tbm = sb("tbm", [128, TB], F32, hs)
                        tEp = sb("tEp", [128, TB], F32, hs)
                        qt = [sb("qt%d" % i, [128, TB], BF16, hs) for i in range(3)]
                        kt = [sb("kt%d" % i, [128, TB], BF16, hs) for i in range(2)]
                        kh = [sb("kh%d" % i, [128, TB], BF16, hs) for i in range(2)]
                        vtok = [sb("vtok%d" % i, [128, TB], BF16, hs) for i in range(3)]
                        ktokA = sb("ktokA", [128, TB], BF16, hs)
                        ktokB = sb("ktokB", [128, TB], BF16, hs)
                        PT = [sb("PT%d" % i, [128, TB], BF16, hs) for i in range(2)]
                        Stl = [sb("Stl%d" % i, [128, 8 * 128], BF16, hs) for i in range(2)]
                        scal = [sb("scal%d" % i, [128, 24], F32, hs) for i in range(2)]
                        oloc2 = [sb("oloc%d" % i, [128, T], F32, hs) for i in range(2)]
                        Sst2 = [sb("Sst%d" % i, [128, 128], F32, hs) for i in range(2)]
                        Bprev = sb("Bprev", [128, 1], F32, hs)
                        xs_s = sb("xs_s", [128, 132], F32, hs)
                        xg_s = sb("xg_s", [128, 3 * 132], F32, hs)
                        acc = sb("acc", [128, 128], F32, hs)
                        avec = sb("avec", [128, 4], F32, hs)
                        Sin = sb("Sin", [128, 128], BF16, hs)

                        stages = {}
                        def head_setup(h):
                            wq = load_w(w_in[(l * 28 + 0 + h) * 128:(l * 28 + 1 + h) * 128, :])
                            wf = load_w(w_in[(l * 28 + 4 + h) * 128:(l * 28 + 5 + h) * 128, :])
                            wi = load_w(w_in[(l * 28 + 8 + h) * 128:(l * 28 + 9 + h) * 128, :])
                            wg = load_w(w_in[(l * 28 + 12 + h) * 128:(l * 28 + 13 + h) * 128, :])
                            lb_c = lbt[:, 4 * lp + h:4 * lp + h + 1]
                            oml_c = lbt[:, 16 + 4 * lp + h:16 + 4 * lp + h + 1]
                            noml_c = lbt[:, 32 + 4 * lp + h:32 + 4 * lp + h + 1]

                            def stageA(tb, h=h, wq=wq, wf=wf, wi=wi, wg=wg, lb_c=lb_c, oml_c=oml_c, noml_c=noml_c):
                                g_ = h * NTB + tb
                                q_, k_, v_, sc_ = qt[g_ % 3], kt[g_ % 2], vtok[g_ % 3], scal[g_ % 2]
                                if tb == 0:
                                    S.op("dve", lambda e: e.memset(Bprev[:, :], 0.0), w=[Bprev.u()])
                                bf, bq, bg, bv = pbank(), pbank(), pbank(), pbank()
                                inproj_fm(wf, tb, bf)
                                inproj_fm(wq, tb, bq)
                                inproj_fm(wg, tb, bg)
                                for tt_ in range(4):
                                    for k in range(8):
                                        c0 = k * HW + 2 + tb * TB + tt_ * 128
                                        mm(bv[:, tt_ * 128:(tt_ + 1) * 128], hT[:, c0:c0 + 128], wi[:, k * 128:(k + 1) * 128], k == 0, k == 7,
                                           r=[wi.u(), hT.u((k, tb))], w=[bv.u()], signal=(k == 7))
                                act(tsig[:, :], bf[:, :], AF.Sigmoid, r=[bf.u()], w=[tsig.u()])
                                act(tEp[:, :], tsig[:, :], AF.Ln, r=[tsig.u(), lbt.u()], w=[tEp.u()], scale=oml_c, bias=lb_c)
                                act(tq[:, :], bq[:, :], AF.Silu, r=[bq.u()], w=[tq.u()])
                                act(mixT[:, h * T + tb * TB:h * T + (tb + 1) * TB], bg[:, :], AF.Silu, r=[bg.u()], w=[mixT.u((h, tb))])
                                S.op("dve", lambda e: e.tensor_tensor_scan(out=tb_[:, :], data0=cb[:, C_RST:C_RST + TB], data1=tEp[:, :], initial=0.0,
                                                                           op0=ALU.mult, op1=ALU.add), r=[cb.u(), tEp.u()], w=[tb_.u()])
                                b3 = tb_[:, :].rearrange("p (c t) -> p c t", t=64)
                                bm3 = tbm[:, :].rearrange("p (c t) -> p c t", t=64)
                                tt(bm3, b3, b3[:, :, 31:32].to_broadcast([128, 8, 64]), ALU.subtract, r=[tb_.u()], w=[tbm.u()])
                                tsc(tk[:, :], tsig[:, :], noml_c, oml_c, ALU.mult, ALU.add, r=[tsig.u(), lbt.u()], w=[tk.u()])
                                act(sc_[:, 8:16].rearrange("p (c o) -> p c o", o=1), bm3[:, :, 63:64], AF.Exp, r=[tbm.u()], w=[sc_.u()])
                                act(tsig[:, :], tbm[:, :], AF.Exp, r=[tbm.u()], w=[tsig.u()])
                                act(tbm[:, :], tbm[:, :], AF.Exp, r=[tbm.u()], w=[tbm.u()], scale=-1.0)
                                tt(tk[:, :], tk[:, :], tbm[:, :], ALU.mult, r=[tk.u(), tbm.u()], w=[tk.u()])
                                cpy(k_[:, :], tk[:, :], r=[tk.u()], w=[k_.u()])
                                kh_ = kh[g_ % 2]
                                tt(kh_[:, :].rearrange("p (c t) -> p c t", t=64), tk[:, :].rearrange("p (c t) -> p c t", t=64),
                                   sc_[:, 8:16].rearrange("p (c o) -> p c o", o=1).to_broadcast([128, 8, 64]), ALU.mult,
                                   r=[tk.u(), sc_.u()], w=[kh_.u()])
                                stt(q_[:, :], tq[:, :], QSCALE, tsig[:, :], ALU.mult, ALU.mult, r=[tq.u(), tsig.u()], w=[q_.u()])
                                act(sc_[:, 0:8].rearrange("p (c o) -> p c o", o=1), b3[:, :, 63:64], AF.Exp, r=[tb_.u()], w=[sc_.u()])
                                act(sc_[:, 16:24].rearrange("p (c o) -> p c o", o=1), b3[:, :, 31:32], AF.Exp, r=[tb_.u()], w=[sc_.u()])
                                S.op("dve", lambda e: e.tensor_tensor_scan(out=tB[:, :], data0=cb[:, C_ONE:C_ONE + TB], data1=tEp[:, :], initial=Bprev[:, 0:1],
                                                                           op0=ALU.mult, op1=ALU.add), r=[cb.u(), tEp.u(), Bprev.u()], w=[tB.u()])
                                cpy(Bprev[:, :], tB[:, TB - 1:TB], r=[tB.u()], w=[Bprev.u()])
                                act(tB[:, :], tB[:, :], AF.Exp, r=[tB.u()], w=[tB.u()])
                                stt(mixT[:, (4 + h) * T + tb * TB:(4 + h) * T + (tb + 1) * TB], tq[:, :], QSCALE, tB[:, :], ALU.mult, ALU.mult, r=[tq.u(), tB.u()], w=[mixT.u((4 + h, tb))])
                                if tb == NTB - 1:
                                    cpy(xs_s[:, 128:129], tB[:, TB - 1:TB], r=[tB.u()], w=[xs_s.u("d")])
                                cpy(v_[:, :], bv[:, :], r=[bv.u()], w=[v_.u()])

                            def stageB(tb, h=h):
                                g_ = h * NTB + tb
                                q_, k_, v_, sc_ = qt[g_ % 3], kt[g_ % 2], vtok[g_ % 3], scal[g_ % 2]
                                PT_, Stl_ = PT[g_ % 2], Stl[g_ % 2]
                                if tb == 0:
                                    S.op("dve", lambda e: e.memset(Sst2[0][:, :], 0.0), w=[Sst2[0].u()])
                                kh_ = kh[g_ % 2]
                                for j in range(4):
                                    S.op("pe", lambda e, j=j, kh_=kh_: e.transpose(ptr[:, j * 128:(j + 1) * 128], kh_[:, j * 128:(j + 1) * 128], ident_bf),
                                         r=[kh_.u(), cb.u()], w=[ptr.u()], signal=(j == 3))
                                tsc(ktokA[:, :], ptr[:, 0:TB], pcol(P_MA), None, ALU.mult, None, r=[ptr.u(), par.u()], w=[ktokA.u()])
                                tsc(ktokB[:, :], ptr[:, 0:TB], pcol(P_MB), None, ALU.mult, None, r=[ptr.u(), par.u()], w=[ktokB.u()])
                                bs = pbank()
                                for j in range(4):
                                    mm(bs[:, j * 128:(j + 1) * 128], k_[:, j * 128:(j + 1) * 128], q_[:, j * 128:(j + 1) * 128], True, True,
                                       r=[k_.u(), q_.u()], w=[bs.u()], signal=(j == 3))
                                tt(PT_[:, :], bs[:, :], cb[:, C_MASK:C_MASK + TB], ALU.mult, r=[bs.u(), cb.u()], w=[PT_.u()])
                                bu = [pbank(), pbank()]
                                for c in range(8):
                                    jj = c // 2
                                    b_ = bu[c // 4]
                                    kx = ktokB if c % 2 else ktokA
                                    mm(b_[:, (c % 4) * 128:(c % 4 + 1) * 128], kx[:, jj * 128:(jj + 1) * 128], v_[:, jj * 128:(jj + 1) * 128],
                                       True, True, r=[kx.u(), v_.u()], w=[b_.u()], signal=(c % 4 == 3))
                                for c in range(8):
                                    b_ = bu[c // 4]
                                    Sa, Sb = Sst2[c % 2], Sst2[(c + 1) % 2]
                                    stt(Sb[:, :], Sa[:, :], sc_[:, c:c + 1], b_[:, (c % 4) * 128:(c % 4 + 1) * 128], ALU.mult, ALU.add,
                                        r=[Sa.u(), sc_.u(), b_.u()], w=[Sb.u()])
                                    act(Stl_[:, c * 128:(c + 1) * 128], Sa[:, :], AF.Copy, r=[Sa.u(), sc_.u()], w=[Stl_.u(c)], scale=sc_[:, 16 + c:17 + c])
                                if tb == NTB - 1:
                                    cpy(xs_s[:, 0:128], Sst2[0][:, :], r=[Sst2[0].u()], w=[xs_s.u("s")])
                                    stages[("ag", h)] = exchange(xs_s, 132, "s", [xs_s.u("d"), xs_s.u("s")])

                            def stageC(tb, h=h):
                                g_ = h * NTB + tb
                                oloc = oloc2[h % 2]
                                q_, v_ = qt[g_ % 3], vtok[g_ % 3]
                                PT_, Stl_ = PT[g_ % 2], Stl[g_ % 2]
                                bo = pbank()
                                for c in range(8):
                                    jj = c // 2
                                    mm(bo[:, c * 64:(c + 1) * 64], v_[:, jj * 128:(jj + 1) * 128], PT_[:, c * 64:(c + 1) * 64], True, False,
                                       r=[v_.u(), PT_.u()], w=[bo.u()], signal=False)
                                    mm(bo[:, c * 64:(c + 1) * 64], Stl_[:, c * 128:(c + 1) * 128], q_[:, c * 64:(c + 1) * 64], False, True,
                                       r=[Stl_.u(c), q_.u()], w=[bo.u()], signal=(c == 7))
                                cpy(oloc[:, tb * TB:(tb + 1) * TB], bo[:, :], r=[bo.u()], w=[oloc.u(tb)])

                            def epilogue(h=h):
                                oloc = oloc2[h % 2]
                                ag, go = stages[("ag", h)]
                                exchange_recv(ag, go, xg_s, 132, 3)
                                G = lambda i, a, b: xg_s[:, 132 * i + a:132 * i + b]
                                ru = [xg_s.u(), par.u()]
                                tsc(acc[:, :], G(0, 0, 128), pcol(P_MLT + 0), None, ALU.mult, None, r=ru, w=[acc.u()])
                                for i in (1, 2):
                                    tsc(avec[:, i:i + 1], G(i, 128, 129), pcol(P_MLT + i), pcol(P_OMM + i), ALU.mult, ALU.add, r=ru, w=[avec.u()])
                                    tsc(acc[:, :], acc[:, :], avec[:, i:i + 1], None, ALU.mult, None, r=[acc.u(), avec.u()], w=[acc.u()])
                                    stt(acc[:, :], G(i, 0, 128), pcol(P_MLT + i), acc[:, :], ALU.mult, ALU.add, r=ru + [acc.u()], w=[acc.u()])
                                cpy(Sin[:, :], acc[:, :], r=[acc.u()], w=[Sin.u()])
                                for tb in range(NTB):
                                    bc = pbank()
                                    mm(bc[:, :], Sin[:, :], mixT[:, (4 + h) * T + tb * TB:(4 + h) * T + (tb + 1) * TB], True, True, r=[Sin.u(), mixT.u((4 + h, tb))], w=[bc.u()], signal=True)
                                    o = oloc[:, tb * TB:(tb + 1) * TB]
                                    tt(o, o, bc[:, :], ALU.add, r=[oloc.u(tb), bc.u()], w=[oloc.u(tb)])
                                    rs = norm_stats(lambda k: (o, [oloc.u(tb)]), 1, tb, 1.0 / 128, pnorm[tb % 2])
                                    stt(o, o, pcol(P_NG + 4 * lp + h), rs[:, :], ALU.mult, ALU.mult, r=[oloc.u(tb), rs.u(), par.u()], w=[oloc.u(tb)])
                                    mc = h * T + tb * TB
                                    tt(mixT[:, mc:mc + TB], o, mixT[:, mc:mc + TB], ALU.mult, r=[oloc.u(tb), mixT.u((h, tb))], w=[mixT.u((h, tb))])
                            return stageA, stageB, stageC, epilogue

                        NG = NH * NTB
                        hfun = {}
                        for s_ in range(NG + 2):
                            LA, LB, LC = [], [], []
                            if s_ < NG:
                                if s_ % NTB == 0:
                                    hfun[s_ // NTB] = head_setup(s_ // NTB)
                                LA = S.capture(lambda: hfun[s_ // NTB][0](s_ % NTB))
                            if 0 <= s_ - 1 < NG:
                                LB = S.capture(lambda: hfun[(s_ - 1) // NTB][1]((s_ - 1) % NTB))
                            if 0 <= s_ - 2 < NG:
                                LC = S.capture(lambda: hfun[(s_ - 2) // NTB][2]((s_ - 2) % NTB))
                            for e_, th_ in LA:
                                if e_ == "pe":
                                    th_()
                            ep = [x for x in LA if x[0] != "pe"]
                            i_, j_ = 0, 0
                            while i_ < len(ep) or j_ < len(LB):
                                if i_ < len(ep):
                                    ep[i_][1]()
                                    i_ += 1
                                if j_ < len(LB):
                                    LB[j_][1]()
                                    j_ += 1
                            for _, th_ in LC:
                                th_()
                            if s_ >= 6 and (s_ - 6) % NTB == 0 and (s_ - 6) // NTB < NH - 1:
                                hfun[(s_ - 6) // NTB][3]()
                        hfun[NH - 1][3]()
                    S.barrier()
                    bp(l, "hgrn")
                    with ExitStack() as cs:
                        chf = sb("chf", [128, T + 2], F32, cs)
                        cacc = sb("cacc", [128, T], F32, cs)
                        Bs = sb("Bs", [128, T], F32, cs)
                        cC = [sb("cC%d" % i, [128, TB], F32, cs) for i in range(2)]
                        for j in range(4):
                            wB = load_w(w_in[(l * 28 + 16 + j) * 128:(l * 28 + 17 + j) * 128, :])
                            wC = load_w(w_in[(l * 28 + 20 + j) * 128:(l * 28 + 21 + j) * 128, :])
                            wH = load_w(w_in[(l * 28 + 24 + j) * 128:(l * 28 + 25 + j) * 128, :])
                            bk = pbank()
                            inproj_halo(wC, bk, 0)
                            bk2 = pbank()
                            inproj_halo(wH, bk2, 0)
                            act(cC[0][:, 0:2], bk[:, 0:2], AF.Copy, r=[bk.u()], w=[cC[0].u()])
                            tt(chf[:, 0:2], cC[0][:, 0:2], bk2[:, 0:2], ALU.mult, r=[cC[0].u(), bk2.u()], w=[chf.u("halo")])
                            for tb in range(NTB):
                                bC, bH, bB = pbank(), pbank(), pbank()
                                inproj_fm(wC, tb, bC)
                                inproj_fm(wH, tb, bH)
                                inproj_fm(wB, tb, bB)
                                c = cC[tb % 2]
                                act(c[:, :], bC[:, :], AF.Copy, r=[bC.u()], w=[c.u()])
                                tt(chf[:, 2 + tb * TB:2 + (tb + 1) * TB], c[:, :], bH[:, :], ALU.mult, r=[c.u(), bH.u()], w=[chf.u(tb)])
                                act(Bs[:, tb * TB:(tb + 1) * TB], bB[:, :], AF.Copy, r=[bB.u()], w=[Bs.u(tb)])
                            for tb in range(NTB):
                                rd = [chf.u(tb), par.u()] + ([chf.u(tb - 1)] if tb else [chf.u("halo")])
                                o = cacc[:, tb * TB:(tb + 1) * TB]
                                b0 = tb * TB
                                tsc(o, chf[:, b0 + 2:b0 + 2 + TB], pcol(P_CW + lp * 12 + 8 + j), None, ALU.mult, None, r=rd, w=[cacc.u(tb)])
                                stt(o, chf[:, b0 + 1:b0 + 1 + TB], pcol(P_CW + lp * 12 + 4 + j), o, ALU.mult, ALU.add, r=rd + [cacc.u(tb)], w=[cacc.u(tb)])
                                stt(o, chf[:, b0:b0 + TB], pcol(P_CW + lp * 12 + j), o, ALU.mult, ALU.add, r=rd + [cacc.u(tb)], w=[cacc.u(tb)])
                                mc = (4 + j) * T + tb * TB
                                tt(mixT[:, mc:mc + TB], o, Bs[:, tb * TB:(tb + 1) * TB], ALU.mult, r=[cacc.u(tb), Bs.u(tb)], w=[mixT.u((4 + j, tb))])
                    S.barrier()
                    bp(l, "conv")
                    hts.close()
                    with ExitStack() as os_:
                        mo2 = [sb("mo%d" % i, [128, 8 * 1024], F32, os_) for i in range(2)]
                        tres = sb("tres", [128, TB], F32, os_)

                        def op_mm(th):
                            mo = mo2[th]
                            for i in range(8):
                                wo = load_w(w_out[(l * 8 + i) * 128:(l * 8 + i + 1) * 128, :])
                                for t2 in range(2):
                                    tb = th * 2 + t2
                                    bk = pbank()
                                    for k in range(8):
                                        mm(bk[:, :], wo[:, k * 128:(k + 1) * 128], mixT[:, k * T + tb * TB:k * T + (tb + 1) * TB], k == 0, k == 7,
                                           r=[wo.u(), mixT.u((k, tb))], w=[bk.u()], signal=(k == 7))
                                    mo_ap = mo[:, i * 1024 + t2 * TB:i * 1024 + (t2 + 1) * TB]
                                    act(mo_ap, bk[:, :], AF.Copy, r=[bk.u()], w=[mo.u((i, t2))])
                                    sq = sqr[st["sq"]]
                                    st["sq"] = (st["sq"] + 1) % 2
                                    act(sq[:, :], bk[:, :], AF.Square, r=[bk.u()], w=[sq.u()])
                                    mm(pnorm[t2][:, :], ones_bf, sq[:, :], i == 0, i == 7, r=[sq.u(), cb.u()], w=[pnorm[t2].u()], signal=True)
                            return [finish_rstd(pnorm[t2], 1.0 / D) for t2 in range(2)]

                        def op_res(th, rss):
                            mo = mo2[th]
                            for t2 in range(2):
                                tb = th * 2 + t2
                                rs = rss[t2]
                                for i in range(8):
                                    mo_ap = mo[:, i * 1024 + t2 * TB:i * 1024 + (t2 + 1) * TB]
                                    stt(tres[:, :], mo_ap, pcol(P_GQM + 8 * lp + i), rs[:, :], ALU.mult, ALU.mult, r=[mo.u((i, t2)), rs.u(), par.u()], w=[tres.u()])
                                    xa = xT[:, i * T + tb * TB:i * T + (tb + 1) * TB]
                                    tt(xa, xa, tres[:, :], ALU.add, r=[xT.u((i, tb)), tres.u()], w=[xT.u((i, tb))])

                        rss1 = op_mm(1)
                        op_res(1, rss1)
                        rss0 = op_mm(0)
                        pn = pnorm[1]
                        for k in range(8):
                            sq = sqr[st["sq"]]
                            st["sq"] = (st["sq"] + 1) % 2
                            act(sq[:, 0:2], xT[:, k * T + T - 2:k * T + T], AF.Square, r=[xT.u((k, NTB - 1))], w=[sq.u()])
                            mm(pn[:, 0:2], ones_bf, sq[:, 0:2], k == 0, k == 7, r=[sq.u(), cb.u()], w=[pn.u()], signal=True)
                        act(lnv[0][:, 0:2], pn[:, 0:2], AF.Ln, r=[pn.u()], w=[lnv[0].u()], scale=1.0 / D, bias=EPS)
                        act(lnv[0][:, 0:2], lnv[0][:, 0:2], AF.Exp, r=[lnv[0].u()], w=[lnv[0].u()], scale=-0.5)
                        for k in range(8):
                            stt(xs_h[:, 2 * k:2 * k + 2], xT[:, k * T + T - 2:k * T + T], pcol(P_GPF + 8 * lp + k), lnv[0][:, 0:2], ALU.mult, ALU.mult,
                                r=[xT.u((k, NTB - 1)), lnv[0].u(), par.u()], w=[xs_h.u()])
                        ffn_ag = exchange(xs_h, 16, "hf")
                        op_res(0, rss0)
                    S.barrier()
            S.barrier()
            bp(l, "mix")
            HH = 1024 + 2
            with ExitStack() as fs:
                hF = sb("hF", [128, 8 * HH], BF16, fs)
                ug = sb("ug", [128, 1024 + 2], F32, fs)
                uv = sb("uv", [128, 1024 + 2], F32, fs)
                cg = [sb("cg%d" % i, [128, TB], F32, fs) for i in range(2)]
                cv = [sb("cv%d" % i, [128, TB], F32, fs) for i in range(2)]
                car = sb("car", [128, 4 * NFC], F32, fs)
                aT = sb("aT", [128, NFC * 1024], BF16, fs)
                yT = sb("yT", [128, 8 * 1024], F32, fs)
                tres = sb("tres2", [128, TB], F32, fs)
                ag, go = ffn_ag
                exchange_recv(ag, go, xg_h, 16)
                tsc(hal[:, :], xg_h[:, 0:16], pcol(P_SEL + 0), None, ALU.mult, None, r=[xg_h.u(), par.u()], w=[hal.u()])
                for i in range(1, 4):
                    stt(hal[:, :], xg_h[:, 16 * i:16 * i + 16], pcol(P_SEL + i), hal[:, :], ALU.mult, ALU.add,
                        r=[xg_h.u(), hal.u(), par.u()], w=[hal.u()])
                for th in range(2):
                    if th == 0:
                        for k in range(8):
                            cpy(hF[:, k * HH:k * HH + 2], hal[:, 2 * k:2 * k + 2], r=[hal.u()], w=[hF.u((k, "halo"))])
                    prenorm(hF, HH, P_GPF, [2 * th, 2 * th + 1], 2 * th)
                    for j in range(NFC):
                        wG = load_w(w_up[(l * 44 + j) * 128:(l * 44 + j + 1) * 128, :])
                        wV = load_w(w_up[(l * 44 + 22 + j) * 128:(l * 44 + 23 + j) * 128, :])
                        fo = P_FCW + lp * 132
                        halves = ((wG, ug, 0, j), (wV, uv, 2 * NFC, 22 + j))
                        for (wsl, uF, cofs, fofs) in halves:
                            if th == 0:
                                bh = pbank()
                                for k in range(8):
                                    mm(bh[:, 0:2], wsl[:, k * 128:(k + 1) * 128], hF[:, k * HH:k * HH + 2], k == 0, k == 7,
                                       r=[wsl.u(), hF.u((k, "halo"))], w=[bh.u()], signal=(k == 7))
                                act(uF[:, 0:2], bh[:, 0:2], AF.Copy, r=[bh.u()], w=[uF.u("halo")])
                            else:
                                cpy(uF[:, 0:2], car[:, cofs + 2 * j:cofs + 2 * j + 2], r=[car.u((cofs, j))], w=[uF.u("halo")])
                        for t2 in range(2):
                            cs = (cg[t2], cv[t2])
                            for hi, (wsl, uF, cofs, fofs) in enumerate(halves):
                                bk = pbank()
                                for k in range(8):
                                    c0 = k * HH + 2 + t2 * TB
                                    mm(bk[:, :], wsl[:, k * 128:(k + 1) * 128], hF[:, c0:c0 + TB], k == 0, k == 7,
                                       r=[wsl.u(), hF.u((k, t2))], w=[bk.u()], signal=(k == 7))
                                act(uF[:, 2 + t2 * TB:2 + (t2 + 1) * TB], bk[:, :], AF.Copy, r=[bk.u()], w=[uF.u(t2)])
                                act(cs[hi][:, :], bk[:, :], AF.Copy, r=[bk.u(), par.u()], w=[cs[hi].u()], scale=pcol(fo + 88 + fofs))
                            for tap, sh in ((1, 1), (0, 0)):
                                for hi, (wsl, uF, cofs, fofs) in enumerate(halves):
                                    c_ = cs[hi]
                                    rd = [uF.u(t2), uF.u(t2 - 1) if t2 else uF.u("halo"), par.u(), c_.u()]
                                    stt(c_[:, :], uF[:, sh + t2 * TB:sh + (t2 + 1) * TB], pcol(fo + tap * 44 + fofs), c_[:, :], ALU.mult, ALU.add, r=rd, w=[c_.u()])
                            act(cs[0][:, :], cs[0][:, :], AF.Silu, r=[cs[0].u()], w=[cs[0].u()])
                            ac = j * 1024 + t2 * TB
                            tt(aT[:, ac:ac + TB], cs[0][:, :], cs[1][:, :], ALU.mult, r=[cs[0].u(), cs[1].u()], w=[aT.u((j, t2))])
                        if th == 0:
                            for (wsl, uF, cofs, fofs) in halves:
                                cpy(car[:, cofs + 2 * j:cofs + 2 * j + 2], uF[:, 1024:1026], r=[uF.u(1)], w=[car.u((cofs, j))])
                    for i in range(8):
                        wd = [load_w(w_dn[(l * 8 + i) * 128:(l * 8 + i + 1) * 128, 0:1024]),
                              load_w(w_dn[(l * 8 + i) * 128:(l * 8 + i + 1) * 128, 1024:2048]),
                              load_w(w_dn[(l * 8 + i) * 128:(l * 8 + i + 1) * 128, 2048:2816])]
                        for t2 in range(2):
                            bk = pbank()
                            for k in range(NFC):
                                wsl = wd[k // 8]
                                kk = k % 8
                                mm(bk[:, :], wsl[:, kk * 128:(kk + 1) * 128], aT[:, k * 1024 + t2 * TB:k * 1024 + (t2 + 1) * TB], k == 0, k == NFC - 1,
                                   r=[wsl.u(), aT.u((k, t2))], w=[bk.u()], signal=(k == NFC - 1))
                            y_ap = yT[:, i * 1024 + t2 * TB:i * 1024 + (t2 + 1) * TB]
                            act(y_ap, bk[:, :], AF.Copy, r=[bk.u()], w=[yT.u((i, t2))])
                            sq = sqr[st["sq"]]
                            st["sq"] = (st["sq"] + 1) % 2
                            act(sq[:, :], bk[:, :], AF.Square, r=[bk.u()], w=[sq.u()])
                            mm(pnorm[t2][:, :], ones_bf, sq[:, :], i == 0, i == 7, r=[sq.u(), cb.u()], w=[pnorm[t2].u()], signal=True)
                    for t2 in range(2):
                        tb = th * 2 + t2
                        rs = finish_rstd(pnorm[t2], 1.0 / D)
                        for i in range(8):
                            y_ap = yT[:, i * 1024 + t2 * TB:i * 1024 + (t2 + 1) * TB]
                            stt(tres[:, :], y_ap, pcol(P_GQF + 8 * lp + i), rs[:, :], ALU.mult, ALU.mult, r=[yT.u((i, t2)), rs.u(), par.u()], w=[tres.u()])
                            xa = xT[:, i * T + tb * TB:i * T + (tb + 1) * TB]
                            tt(xa, xa, tres[:, :], ALU.add, r=[xT.u((i, tb)), tres.u()], w=[xT.u((i, tb))])
            S.barrier()
            bp(l, "ffn")

          except _Stop:
            break
        S.enabled = True
        for _i in range(int(os.environ.get("K_DUMMY_PE", "0"))):
            mm(pring[0][:, 0:2], ones_bf, cb[:, 0:2], True, True, r=[cb.u()], w=[pring[0].u()], signal=(_i % 64 == 63))
        for _i in range(int(os.environ.get("K_DUMMY_ACT", "0"))):
            act(lnv[0][:, 0:8], lnv[0][:, 0:8], AF.Copy, r=[lnv[0].u()], w=[lnv[0].u()])
        for _i in range(int(os.environ.get("K_DUMMY_DVE", "0"))):
            cpy(lnv[1][:, 0:8], lnv[1][:, 8:16], r=[lnv[1].u()], w=[lnv[1].u()])
        for _i in range(int(os.environ.get("K_DUMMY_SLOW", "0"))):
            act(xT[:, 0:2048], xT[:, 0:2048], AF.Copy, r=[xT.u((0, 0)), xT.u((0, 1)), xT.u((0, 2)), xT.u((0, 3))],
                w=[xT.u((0, 0)), xT.u((0, 1)), xT.u((0, 2)), xT.u((0, 3))])
        if os.environ.get("K_DUMMY_HI"):
            for _i in range(nlayers * 44 - 48, nlayers * 44):
                load_w(w_up[_i * 128:(_i + 1) * 128, :])
            for _i in range(nlayers * 8 - 8, nlayers * 8):
                load_w(w_dn[_i * 128:(_i + 1) * 128, 0:1024])
                load_w(w_dn[_i * 128:(_i + 1) * 128, 2048:2816])
        for _i in range(int(os.environ.get("K_DUMMY_DMA", "0"))):
            load_w(w_up[(_i % 40) * 128:(_i % 40 + 1) * 128, :])
        for _i in range(int(os.environ.get("K_DUMMY_CC", "0"))):
            ag_, go_ = exchange(xs_h, 16, "dummy")
            exchange_recv(ag_, go_, xg_h, 16)
        sigs = []
        for k in range(8):
            for tb in range(NTB):
                c0 = k * T + tb * TB
                sigs.append(S.dma("sp", lambda e, c0=c0: e.dma_start(out=out[:, c0:c0 + TB], in_=xT[:, c0:c0 + TB]), r=[xT.u((k, tb))]))
        for key, val in sigs:
            S._wait("sp", key, val)
        S.emit()
        print("kernel: recorded ops", S.nops, {e: len(S.q[e]) for e in S.ENG}, "sigcnt", S.cnt, "dma", [c for _, c in S.dsem][:4], "cc", S.cccnt)
    return nc


def _prep_inputs(x, lb_param, w_in, w_out, conv_w, ffn_w_up, ffn_conv_w, ffn_w_down,
                 hgrn_norm_g, pre_mix_g, post_mix_g, pre_ffn_g, post_ffn_g):
    f = np.float32

    def wl(w, nk):
        L, K, N = w.shape
        nj = N // 128
        a = np.asarray(w, f).reshape(L, nk, 128, nj, 128).transpose(0, 3, 2, 1, 4)
        return np.ascontiguousarray(a).reshape(L * nj * 128, nk * 128)

    w_in_r = wl(w_in, 8)
    w_out_r = wl(w_out, 8)
    w_up_r = wl(ffn_w_up, 8)
    w_dn_r = wl(ffn_w_down, NFC)

    def pk(a, n):
        a = np.asarray(a, f).reshape(DEPTH, n, 128).transpose(2, 0, 1)
        return np.ascontiguousarray(a).reshape(128, DEPTH * n)

    def pk3(a, n):
        a = np.asarray(a, f).reshape(DEPTH, 3, n, 128).transpose(3, 0, 1, 2)
        return np.ascontiguousarray(a).reshape(128, DEPTH * 3 * n)

    base = np.concatenate([pk(lb_param, 4), pk(hgrn_norm_g, 4), pk(pre_mix_g, 8), pk(post_mix_g, 8), pk(pre_ffn_g, 8),
                           pk(post_ffn_g, 8), pk3(conv_w, 4), pk3(ffn_conv_w, 44)], axis=1)
    assert base.shape[1] == P_SEL
    cbm = np.zeros((128, NCB), np.float32)
    cbm[:, C_ONES:C_ONES + 128] = 1.0
    cbm[:, C_ID:C_ID + 128] = np.eye(128, dtype=np.float32)
    s = np.arange(128)[:, None]
    t = np.arange(128)[None, :]
    m2 = ((s // 64) == (t // 64)) & (s <= t)
    cbm[:, C_MASK:C_MASK + 512] = np.tile(m2.astype(np.float32), (1, 4))
    rst = np.ones((128, 512), np.float32)
    rst[:, ::64] = 0.0
    cbm[:, C_RST:C_RST + 512] = rst
    cbm[:, C_ONE:C_ONE + 512] = 1.0
    cbm = cbm.astype(ml_dtypes.bfloat16)

    xs = np.asarray(x, f)
    in_maps = []
    for c in range(8):
        b, sgm = c // 4, c % 4
        xc = xs[b, sgm * T:(sgm + 1) * T, :]
        xc = np.ascontiguousarray(xc.T.reshape(8, 128, T).transpose(1, 0, 2)).reshape(128, 8 * T)
        extra = np.zeros((128, 14), np.float32)
        extra[:64, 12] = 1.0
        extra[64:, 13] = 1.0
        for i in range(4):
            extra[:, i] = 1.0 if i == sgm - 1 else 0.0
            extra[:, 4 + i] = 1.0 if i < sgm else 0.0
            extra[:, 8 + i] = 0.0 if i < sgm else 1.0
        par = np.ascontiguousarray(np.concatenate([base, extra], axis=1))
        in_maps.append({"xin": xc, "w_in": w_in_r, "w_out": w_out_r, "w_up": w_up_r, "w_dn": w_dn_r, "par": par, "cb": cbm})
    return in_maps


_NC_CACHE = {}
SPLIT = [(0, 4)]


def _to_fm(xc):
    return np.ascontiguousarray(xc.T.reshape(8, 128, T).transpose(1, 0, 2)).reshape(128, 8 * T)


def kernel(x, lb_param, w_in, w_out, conv_w, ffn_w_up, ffn_conv_w, ffn_w_down,
           hgrn_norm_g, pre_mix_g, post_mix_g, pre_ffn_g, post_ffn_g, _dbg=None, _nlayers=None):
    in_maps = _prep_inputs(x, lb_param, w_in, w_out, conv_w, ffn_w_up, ffn_conv_w, ffn_w_down,
                           hgrn_norm_g, pre_mix_g, post_mix_g, pre_ffn_g, post_ffn_g)
    split = SPLIT if _nlayers is None else [(0, _nlayers)]
    full = {k: in_maps[0][k] for k in ("w_in", "w_out", "w_up", "w_dn")}
    res = None
    for (l0, l1) in split:
        nl = l1 - l0
        key = (_dbg, nl, l0)
        if key not in _NC_CACHE:
            _NC_CACHE[key] = build_program(nl, _dbg, l0)
        nc = _NC_CACHE[key]
        for c, m in enumerate(in_maps):
            m["w_in"] = full["w_in"][l0 * 28 * 128:l1 * 28 * 128]
            m["w_out"] = full["w_out"][l0 * 8 * 128:l1 * 8 * 128]
            m["w_up"] = full["w_up"][l0 * 44 * 128:l1 * 44 * 128]
            m["w_dn"] = full["w_dn"][l0 * 8 * 128:l1 * 8 * 128]
            if res is not None:
                m["xin"] = np.asarray(res.results[c]["out"])
        res = run_bass_kernel_spmd(nc, in_maps, core_ids=list(range(8)))
    outp = np.empty((2, 4 * T, D), np.float32)
    for c in range(8):
        b, sgm = c // 4, c % 4
        o = np.asarray(res.results[c]["out"]).reshape(128, 8, T).transpose(2, 1, 0).reshape(T, D)
        outp[b, sgm * T:(sgm + 1) * T, :] = o
    return outp
```

```python
import os
from contextlib import ExitStack

import numpy as np
import ml_dtypes

import concourse.bass as bass
import concourse.mybir as mybir
from concourse.bass_utils import run_bass_kernel_spmd

F32 = mybir.dt.float32
BF16 = mybir.dt.bfloat16
AF = mybir.ActivationFunctionType
ALU = mybir.AluOpType

D = 1024
T = 2048
TB = 512
NTB = T // TB
DEPTH = 4
NH = 4
DFF = 2816
NFC = DFF // 128
EPS = 1e-6
QSCALE = 128 ** -0.5

P_LBP = 0
P_NG = P_LBP + 16
P_GPM = P_NG + 16
P_GQM = P_GPM + 32
P_GPF = P_GQM + 32
P_GQF = P_GPF + 32
P_CW = P_GQF + 32
P_FCW = P_CW + 48
P_SEL = P_FCW + 528
P_MLT = P_SEL + 4
P_OMM = P_MLT + 4
P_MA = P_OMM + 4
P_MB = P_MA + 1
NPAR = P_MB + 1
C_ONES = 0
C_ID = 128
C_MASK = 256
C_RST = 768
C_ONE = 1280
NCB = 1792

NSLOT = 6


class TT:
    __slots__ = ("w", "r")

    def __init__(self):
        self.w = None
        self.r = {}


class Tile:
    def __init__(self, t):
        self.t = t
        self.units = {}

    def u(self, key=0):
        x = self.units.get(key)
        if x is None:
            x = self.units[key] = TT()
        return x

    def __getitem__(self, idx):
        return self.t[idx]


class Sched:
    ENG = ("pe", "act", "dve", "pool", "sp")

    def __init__(self, nc, es):
        self.nc = nc
        self.q = {e: [] for e in self.ENG}
        self.cnt = {e: 0 for e in self.ENG}
        self.waited = {e: {} for e in self.ENG}
        self.sem = {e: es.enter_context(nc.semaphore("c_" + e)) for e in self.ENG}
        self.dsem = [[es.enter_context(nc.semaphore("d%d" % i)), 0] for i in range(24)]
        self.drr = 0
        self.es = es
        self.cccnt = 0
        self.semh = dict(self.sem)
        for i, (h, _) in enumerate(self.dsem):
            self.semh["d%d" % i] = h
        self.nops = 0
        self.enabled = True
        self.seen = {}
        self.cap = None

    def _wait(self, eng, key, val):
        if eng == "pe" and key == "pe":
            return
        if key in self.cnt:
            assert val <= self.cnt[key], ("wait on future signal", eng, key, val, self.cnt[key])
        if self.waited[eng].get(key, 0) >= val:
            return
        self.waited[eng][key] = val
        if self.seen.get(key, 0) < val:
            self.seen[key] = val
        self.q[eng].append(("w", key, val))

    def _deps(self, eng, reads, writes):
        for t in reads:
            if t.w is not None:
                self._wait(eng, *t.w)
        for t in writes:
            if t.w is not None:
                self._wait(eng, *t.w)
            for k, v in t.r.items():
                self._wait(eng, k, v)

    def _mark(self, sig, reads, writes):
        k, v = sig
        for t in reads:
            if t.r.get(k, 0) < v:
                t.r[k] = v
        for t in writes:
            t.w = sig
            t.r = {}

    def capture(self, f):
        assert self.cap is None
        self.cap = []
        try:
            f()
        finally:
            lst, self.cap = self.cap, None
        return lst

    def op(self, eng, fn, r=(), w=(), signal=True):
        if not self.enabled:
            return
        if self.cap is not None:
            self.cap.append((eng, lambda: self.op(eng, fn, r, w, signal)))
            return
        self._deps(eng, r, w)
        self.nops += 1
        if signal:
            self.cnt[eng] += 1
            sig = (eng, self.cnt[eng])
            self.q[eng].append(("i", fn, eng, 1))
        else:
            sig = (eng, self.cnt[eng] + 1)
            self.q[eng].append(("i", fn, None, 0))
        self._mark(sig, r, w)

    def dma(self, eng, fn, r=(), w=()):
        if not self.enabled:
            return None
        if self.cap is not None:
            self.cap.append((eng, lambda: self.dma(eng, fn, r, w)))
            return None
        self._deps(eng, r, w)
        i = self.drr
        self.drr = (self.drr + 1) % len(self.dsem)
        key = "d%d" % i
        prev = self.dsem[i][1]
        if prev and self.seen.get(key, 0) < prev:
            self._wait(eng, key, prev)
        self.dsem[i][1] = prev + 16
        sig = (key, prev + 16)
        self.q[eng].append(("i", fn, key, 16))
        self._mark(sig, r, w)
        return sig

    def collective(self, fn, r=(), w=()):
        if not self.enabled:
            return
        if self.cap is not None:
            self.cap.append(("pool", lambda: self.collective(fn, r, w)))
            return
        self._deps("pool", r, w)
        self.cccnt += 1
        key = "cc%d" % self.cccnt
        self.semh[key] = self.es.enter_context(self.nc.semaphore(key))
        sig = (key, 1)
        self.q["pool"].append(("i", fn, key, 1))
        self._mark(sig, r, w)

    def barrier(self):
        if not self.enabled:
            return
        cur = [(e, self.cnt[e]) for e in self.ENG if self.cnt[e]]
        cur += [("d%d" % i, c) for i, (_, c) in enumerate(self.dsem) if c]
        cur += [("cc%d" % i, 1) for i in range(1, self.cccnt + 1)]
        for e in self.ENG:
            for k, v in cur:
                if k == e:
                    continue
                self._wait(e, k, v)

    def emit(self):
        nc = self.nc
        engs = {"pe": "tensor", "act": "scalar", "dve": "vector", "pool": "gpsimd", "sp": "sync"}
        with nc.Block() as block:
            for e in self.ENG:
                items = self.q[e]

                def body(eng, items=items):
                    for it in items:
                        if it[0] == "w":
                            eng.wait_ge(self.semh[it[1]], it[2])
                        else:
                            ins = it[1](eng)
                            if it[2] is not None:
                                ins.then_inc(self.semh[it[2]], it[3])

                getattr(block, engs[e])(body)


def build_program(nlayers=DEPTH, dbg=None, lbase=0):
    nc = bass.Bass("TRN2", target_bir_lowering=False)
    xin = nc.dram_tensor("xin", [128, 8 * T], F32, kind="ExternalInput").ap()
    w_in = nc.dram_tensor("w_in", [nlayers * 28 * 128, 1024], F32, kind="ExternalInput").ap()
    w_out = nc.dram_tensor("w_out", [nlayers * 8 * 128, 1024], F32, kind="ExternalInput").ap()
    w_up = nc.dram_tensor("w_up", [nlayers * 44 * 128, 1024], F32, kind="ExternalInput").ap()
    w_dn = nc.dram_tensor("w_dn", [nlayers * 8 * 128, NFC * 128], F32, kind="ExternalInput").ap()
    par_d = nc.dram_tensor("par", [128, NPAR], F32, kind="ExternalInput").ap()
    cb_d = nc.dram_tensor("cb", [128, NCB], BF16, kind="ExternalInput").ap()
    out = nc.dram_tensor("out", [128, 8 * T], F32, kind="ExternalOutput").ap()

    es = ExitStack()
    with es:
        S = Sched(nc, es)

        uid = [0]

        def sb(name, shape, dt, stack=es):
            uid[0] += 1
            return Tile(stack.enter_context(nc.sbuf_tensor("%s_%d" % (name, uid[0]), shape, dt)))

        def ps(name, shape, dt, stack=es):
            uid[0] += 1
            return Tile(stack.enter_context(nc.psum_tensor("%s_%d" % (name, uid[0]), shape, dt)))

        xT = sb("xT", [128, 8 * T], F32)
        par = sb("par", [128, NPAR], F32)
        cb = sb("cb", [128, NCB], BF16)
        lbt = sb("lbt", [128, 64], F32)
        wring = [sb("wr%d" % i, [128, 1024], BF16) for i in range(NSLOT)]
        sqr = [sb("sq%d" % i, [128, TB], BF16) for i in range(2)]
        lnv = [sb("lnv%d" % i, [128, TB], F32) for i in range(1)]
        rstd = [sb("rstd%d" % i, [128, TB], F32) for i in range(2)]
        xs_h = sb("xs_h", [128, 16], F32)
        xg_h = sb("xg_h", [128, 64], F32)
        hal = sb("hal", [128, 16], F32)
        pring = [ps("pb%d" % i, [128, TB], F32) for i in range(5)]
        pnorm = [ps("pn%d" % i, [128, TB], F32) for i in range(2)]
        ptr = ps("ptr", [128, 1024], BF16)
        st = {"wr": 0, "pr": 0, "sq": 0, "ln": 0, "ex": 0}

        ones_bf = cb[:, C_ONES:C_ONES + 128]
        ident_bf = cb[:, C_ID:C_ID + 128]

        def pcol(off):
            return par[:, off:off + 1]

        def wslot():
            i = st["wr"]
            st["wr"] = (i + 1) % NSLOT
            return wring[i]

        def pbank():
            i = st["pr"]
            st["pr"] = (i + 1) % len(pring)
            return pring[i]

        def load_w(src_ap):
            sl = wslot()
            ncol = src_ap.shape[1]
            S.dma("pool", lambda e, sl=sl, src_ap=src_ap, ncol=ncol: e.dma_start(out=sl[:, 0:ncol], in_=src_ap), w=[sl.u()])
            return sl

        def mm(out_ap, lhsT, rhs, start, stop, r, w, signal):
            S.op("pe", lambda e: e.matmul(out_ap, lhsT=lhsT, rhs=rhs, start=start, stop=stop), r=r, w=w, signal=signal)

        def act(out_ap, in_ap, func, r, w, scale=None, bias=None):
            kw = {}
            if scale is not None:
                kw["scale"] = scale
            if bias is not None:
                kw["bias"] = bias
            S.op("act", lambda e: e.activation(out=out_ap, in_=in_ap, func=func, **kw), r=r, w=w)

        def tsc(out_ap, in0, s1, s2, op0, op1, r, w, eng="dve"):
            if s2 is None:
                S.op(eng, lambda e: e.tensor_scalar(out=out_ap, in0=in0, scalar1=s1, scalar2=None, op0=op0), r=r, w=w)
            else:
                S.op(eng, lambda e: e.tensor_scalar(out=out_ap, in0=in0, scalar1=s1, scalar2=s2, op0=op0, op1=op1), r=r, w=w)

        def stt(out_ap, in0, sc, in1, op0, op1, r, w):
            S.op("dve", lambda e: e.scalar_tensor_tensor(out=out_ap, in0=in0, scalar=sc, in1=in1, op0=op0, op1=op1), r=r, w=w)

        def tt(out_ap, in0, in1, op, r, w, eng="dve"):
            S.op(eng, lambda e: e.tensor_tensor(out=out_ap, in0=in0, in1=in1, op=op), r=r, w=w)

        def cpy(out_ap, in_ap, r, w, eng="dve"):
            S.op(eng, lambda e: e.tensor_copy(out_ap, in_ap), r=r, w=w)

        def exchange(src, ncols, tag, runits=None):
            i = st["ex"]
            st["ex"] += 1
            agin = nc.dram_tensor("agin%d" % i, [128, ncols], F32)
            agout = nc.dram_tensor("agout%d" % i, [4 * 128, ncols], F32)
            gi, go = TT(), TT()
            S.dma("sp", lambda e: e.dma_start(out=agin.ap()[:, :], in_=src[:, 0:ncols]), r=(runits or [src.u()]), w=[gi])
            S.collective(lambda e: e.collective_compute("AllGather", ALU.bypass, replica_groups=[[0, 1, 2, 3], [4, 5, 6, 7]],
                                                        ins=[agin.ap().opt()], outs=[agout.ap().opt()]), r=[gi], w=[go])
            return agout, go

        def exchange_recv(agout, go, dst, ncols, nr=4):
            S.dma("sp", lambda e: e.dma_start(out=dst[:, 0:nr * ncols].rearrange("p (r c) -> p r c", r=nr),
                                              in_=agout.ap()[0:nr * 128, :].rearrange("(r p) c -> p r c", p=128)), r=[go], w=[dst.u()])

        def norm_stats(src_fn, nk, tb_key, scale, pn):
            for k in range(nk):
                ap, un = src_fn(k)
                sq = sqr[st["sq"]]
                st["sq"] = (st["sq"] + 1) % 2
                if nk == 8 and k % 2 == 1:
                    tt(sq[:, :], ap, ap, ALU.mult, r=un, w=[sq.u()])
                else:
                    act(sq[:, :], ap, AF.Square, r=un, w=[sq.u()])
                mm(pn[:, :], ones_bf, sq[:, :], k == 0, k == nk - 1, r=[sq.u(), cb.u()], w=[pn.u()], signal=True)
            return finish_rstd(pn, scale)

        def finish_rstd(pn, scale):
            i = st["ln"]
            st["ln"] = (i + 1) % 2
            lv, rs = lnv[0], rstd[i]
            act(lv[:, :], pn[:, :], AF.Ln, r=[pn.u()], w=[lv.u()], scale=scale, bias=EPS)
            act(rs[:, :], lv[:, :], AF.Exp, r=[lv.u()], w=[rs.u()], scale=-0.5)
            return rs

        for k in range(8):
            for tb in range(NTB):
                c0 = k * T + tb * TB
                S.dma("sp", lambda e, c0=c0: e.dma_start(out=xT[:, c0:c0 + TB], in_=xin[:, c0:c0 + TB]), w=[xT.u((k, tb))])
        S.dma("sp", lambda e: e.dma_start(out=par[:, :], in_=par_d[:, :]), w=[par.u()])
        S.dma("sp", lambda e: e.dma_start(out=cb[:, :], in_=cb_d[:, :]), w=[cb.u()])
        L = lambda l: lbt[:, 48 + 0:48 + 4]
        mx = lbt[:, 48:52]
        tt(mx, par[:, P_LBP:P_LBP + 4], par[:, P_LBP + 4:P_LBP + 8], ALU.max, r=[par.u()], w=[lbt.u()])
        tt(mx, mx, par[:, P_LBP + 8:P_LBP + 12], ALU.max, r=[lbt.u()], w=[lbt.u()])
        tt(mx, mx, par[:, P_LBP + 12:P_LBP + 16], ALU.max, r=[lbt.u()], w=[lbt.u()])
        for l in range(4):
            tt(lbt[:, 4 * l:4 * l + 4], par[:, P_LBP + 4 * l:P_LBP + 4 * l + 4], mx, ALU.subtract, r=[par.u(), lbt.u()], w=[lbt.u()])
        act(lbt[:, 0:16], lbt[:, 0:16], AF.Exp, r=[lbt.u()], w=[lbt.u()])
        sm = lbt[:, 52:56]
        tt(sm, lbt[:, 0:4], lbt[:, 4:8], ALU.add, r=[lbt.u()], w=[lbt.u()])
        tt(sm, sm, lbt[:, 8:12], ALU.add, r=[lbt.u()], w=[lbt.u()])
        tt(sm, sm, lbt[:, 12:16], ALU.add, r=[lbt.u()], w=[lbt.u()])
        S.op("dve", lambda e: e.reciprocal(sm, sm), r=[lbt.u()], w=[lbt.u()])
        for l in range(4):
            tt(lbt[:, 4 * l:4 * l + 4], lbt[:, 4 * l:4 * l + 4], sm, ALU.mult, r=[lbt.u()], w=[lbt.u()])
        tt(lbt[:, 8:12], lbt[:, 8:12], lbt[:, 4:8], ALU.add, r=[lbt.u()], w=[lbt.u()])
        tt(lbt[:, 12:16], lbt[:, 12:16], lbt[:, 8:12], ALU.add, r=[lbt.u()], w=[lbt.u()])
        S.op("dve", lambda e: e.memset(lbt[:, 0:4], 0.0), w=[lbt.u()])
        tsc(lbt[:, 16:32], lbt[:, 0:16], -1.0, 1.0, ALU.mult, ALU.add, r=[lbt.u()], w=[lbt.u()])
        tsc(lbt[:, 32:48], lbt[:, 16:32], -1.0, None, ALU.mult, None, r=[lbt.u()], w=[lbt.u()])

        HW = T + 2

        class _Stop(Exception):
            pass

        def bp(l, name):
            if dbg == (l, name):
                S.barrier()
                S.enabled = False

        for l in range(nlayers):
          try:
            lp = l + lbase
            bp(l, "pre")
            with ExitStack() as mixs:
                mixT = sb("mixT", [128, 8 * T], BF16, mixs)
                hts = ExitStack()
                hT = sb("hT", [128, 8 * HW], BF16, hts)

                def prenorm(hT, hw, gofs, tbs, toff):
                    for tb in tbs:
                        rs = norm_stats(lambda k: (xT[:, k * T + tb * TB:k * T + (tb + 1) * TB], [xT.u((k, tb))]), 8, tb, 1.0 / D, pnorm[tb % 2])
                        for k in range(8):
                            c0 = k * hw + 2 + (tb - toff) * TB
                            stt(hT[:, c0:c0 + TB], xT[:, k * T + tb * TB:k * T + (tb + 1) * TB], pcol(gofs + 8 * lp + k), rs[:, :],
                                ALU.mult, ALU.mult, r=[xT.u((k, tb)), rs.u(), par.u()], w=[hT.u((k, tb - toff))])

                def halo_exchange(hT, hw, last_c0, last_key):
                    for k in range(8):
                        cpy(xs_h[:, 2 * k:2 * k + 2], hT[:, k * hw + last_c0:k * hw + last_c0 + 2], r=[hT.u((k, last_key))], w=[xs_h.u()])
                    ag, go = exchange(xs_h, 16, "h")
                    exchange_recv(ag, go, xg_h, 16)
                    tsc(hal[:, :], xg_h[:, 0:16], pcol(P_SEL + 0), None, ALU.mult, None, r=[xg_h.u(), par.u()], w=[hal.u()])
                    for i in range(1, 4):
                        stt(hal[:, :], xg_h[:, 16 * i:16 * i + 16], pcol(P_SEL + i), hal[:, :], ALU.mult, ALU.add,
                            r=[xg_h.u(), hal.u(), par.u()], w=[hal.u()])
                    for k in range(8):
                        cpy(hT[:, k * hw:k * hw + 2], hal[:, 2 * k:2 * k + 2], r=[hal.u()], w=[hT.u((k, "halo"))])

                prenorm(hT, HW, P_GPM, range(NTB), 0)
                bp(l, "norm")
                halo_exchange(hT, HW, 2 + T - 2, NTB - 1)
                bp(l, "halo")

                def hrhs(k, tb):
                    c0 = k * HW + 2 + tb * TB
                    return hT[:, c0:c0 + TB]

                def inproj_fm(wsl, tb, bank):
                    for k in range(8):
                        mm(bank[:, :], wsl[:, k * 128:(k + 1) * 128], hrhs(k, tb), k == 0, k == 7,
                           r=[wsl.u(), hT.u((k, tb))], w=[bank.u()], signal=(k == 7))

                def inproj_halo(wsl, bank, col):
                    for k in range(8):
                        mm(bank[:, col:col + 2], wsl[:, k * 128:(k + 1) * 128], hT[:, k * HW:k * HW + 2], k == 0, k == 7,
                           r=[wsl.u(), hT.u((k, "halo"))], w=[bank.u()], signal=(k == 7))

                with ExitStack() as mxs:
                    with ExitStack() as hs:
                        tq = sb("tq", [128, TB], F32, hs)
                        tsig = sb("tsig", [128, TB], F32, hs)
                        tk = sb("tk", [128, TB], F32, hs)
                        tb_ = sb("tb_", [128, TB], F32, hs)
                        tB = sb("tB", [128, TB], F32, hs)
                        tbm = sb("tbm", [128, TB], F32, hs)
                        tEp = sb("tEp", [128, TB], F32, hs)
                        qt = [sb("qt%d" % i, [128, TB], BF16, hs) for i in range(3)]
                        kt = [sb("kt%d" % i, [128, TB], BF16, hs) for i in range(2)]
                        kh = [sb("kh%d" % i, [128, TB], BF16, hs) for i in range(2)]
                        vtok = [sb("vtok%d" % i, [128, TB], BF16, hs) for i in range(3)]
                        ktokA = sb("ktokA", [128, TB], BF16, hs)
                        ktokB = sb("ktokB", [128, TB], BF16, hs)
                        PT = [sb("PT%d" % i, [128, TB], BF16, hs) for i in range(2)]
                        Stl = [sb("Stl%d" % i, [128, 8 * 128], BF16, hs) for i in range(2)]
                        scal = [sb("scal%d" % i, [128, 24], F32, hs) for i in range(2)]
                        oloc2 = [sb("oloc%d" % i, [128, T], F32, hs) for i in range(2)]
                        Sst2 = [sb("Sst%d" % i, [128, 128], F32, hs) for i in range(2)]
                        Bprev = sb("Bprev", [128, 1], F32, hs)
                        xs_s = sb("xs_s", [128, 132], F32, hs)
                        xg_s = sb("xg_s", [128, 3 * 132], F32, hs)
                        acc = sb("acc", [128, 128], F32, hs)
                        avec = sb("avec", [128, 4], F32, hs)
                        Sin = sb("Sin", [128, 128], BF16, hs)

                        stages = {}
                        def head_setup(h):
                            wq = load_w(w_in[(l * 28 + 0 + h) * 128:(l * 28 + 1 + h) * 128, :])
                            wf = load_w(w_in[(l * 28 + 4 + h) * 128:(l * 28 + 5 + h) * 128, :])
                            wi = load_w(w_in[(l * 28 + 8 + h) * 128:(l * 28 + 9 + h) * 128, :])
                            wg = load_w(w_in[(l * 28 + 12 + h) * 128:(l * 28 + 13 + h) * 128, :])
                            lb_c = lbt[:, 4 * lp + h:4 * lp + h + 1]
                            oml_c = lbt[:, 16 + 4 * lp + h:16 + 4 * lp + h + 1]
                            noml_c = lbt[:, 32 + 4 * lp + h:32 + 4 * lp + h + 1]

                            def stageA(tb, h=h, wq=wq, wf=wf, wi=wi, wg=wg, lb_c=lb_c, oml_c=oml_c, noml_c=noml_c):
                                g_ = h * NTB + tb
                                q_, k_, v_, sc_ = qt[g_ % 3], kt[g_ % 2], vtok[g_ % 3], scal[g_ % 2]
                                if tb == 0:
                                    S.op("dve", lambda e: e.memset(Bprev[:, :], 0.0), w=[Bprev.u()])
                                bf, bq, bg, bv = pbank(), pbank(), pbank(), pbank()
                                inproj_fm(wf, tb, bf)
                                inproj_fm(wq, tb, bq)
                                inproj_fm(wg, tb, bg)
                                for tt_ in range(4):
                                    for k in range(8):
                                        c0 = k * HW + 2 + tb * TB + tt_ * 128
                                        mm(bv[:, tt_ * 128:(tt_ + 1) * 128], hT[:, c0:c0 + 128], wi[:, k * 128:(k + 1) * 128], k == 0, k == 7,
                                           r=[wi.u(), hT.u((k, tb))], w=[bv.u()], signal=(k == 7))
                                act(tsig[:, :], bf[:, :], AF.Sigmoid, r=[bf.u()], w=[tsig.u()])
                                act(tEp[:, :], tsig[:, :], AF.Ln, r=[tsig.u(), lbt.u()], w=[tEp.u()], scale=oml_c, bias=lb_c)
                                act(tq[:, :], bq[:, :], AF.Silu, r=[bq.u()], w=[tq.u()])
                                act(mixT[:, h * T + tb * TB:h * T + (tb + 1) * TB], bg[:, :], AF.Silu, r=[bg.u()], w=[mixT.u((h, tb))])
                                S.op("dve", lambda e: e.tensor_tensor_scan(out=tb_[:, :], data0=cb[:, C_RST:C_RST + TB], data1=tEp[:, :], initial=0.0,
                                                                           op0=ALU.mult, op1=ALU.add), r=[cb.u(), tEp.u()], w=[tb_.u()])
                                b3 = tb_[:, :].rearrange("p (c t) -> p c t", t=64)
                                bm3 = tbm[:, :].rearrange("p (c t) -> p c t", t=64)
                                tt(bm3, b3, b3[:, :, 31:32].to_broadcast([128, 8, 64]), ALU.subtract, r=[tb_.u()], w=[tbm.u()])
                                S.op("dve", lambda e: e.tensor_tensor_scan(out=tB[:, :], data0=cb[:, C_ONE:C_ONE + TB], data1=tEp[:, :], initial=Bprev[:, 0:1],
                                                                           op0=ALU.mult, op1=ALU.add), r=[cb.u(), tEp.u(), Bprev.u()], w=[tB.u()])
                                cpy(Bprev[:, :], tB[:, TB - 1:TB], r=[tB.u()], w=[Bprev.u()])
                                tsc(tk[:, :], tsig[:, :], noml_c, oml_c, ALU.mult, ALU.add, r=[tsig.u(), lbt.u()], w=[tk.u()])
                                act(sc_[:, 0:8].rearrange("p (c o) -> p c o", o=1), b3[:, :, 63:64], AF.Exp, r=[tb_.u()], w=[sc_.u()])
                                act(sc_[:, 8:16].rearrange("p (c o) -> p c o", o=1), bm3[:, :, 63:64], AF.Exp, r=[tbm.u()], w=[sc_.u()])
                                act(sc_[:, 16:24].rearrange("p (c o) -> p c o", o=1), b3[:, :, 31:32], AF.Exp, r=[tb_.u()], w=[sc_.u()])
                                act(tEp[:, :], tbm[:, :], AF.Exp, r=[tbm.u()], w=[tEp.u()])
                                act(tbm[:, :], tbm[:, :], AF.Exp, r=[tbm.u()], w=[tbm.u()], scale=-1.0)
                                act(tB[:, :], tB[:, :], AF.Exp, r=[tB.u()], w=[tB.u()])
                                stt(q_[:, :], tq[:, :], QSCALE, tEp[:, :], ALU.mult, ALU.mult, r=[tq.u(), tEp.u()], w=[q_.u()])
                                tt(tk[:, :], tk[:, :], tbm[:, :], ALU.mult, r=[tk.u(), tbm.u()], w=[tk.u()])
                                cpy(k_[:, :], tk[:, :], r=[tk.u()], w=[k_.u()])
                                kh_ = kh[g_ % 2]
                                tt(kh_[:, :].rearrange("p (c t) -> p c t", t=64), tk[:, :].rearrange("p (c t) -> p c t", t=64),
                                   sc_[:, 8:16].rearrange("p (c o) -> p c o", o=1).to_broadcast([128, 8, 64]), ALU.mult,
                                   r=[tk.u(), sc_.u()], w=[kh_.u()])
                                stt(mixT[:, (4 + h) * T + tb * TB:(4 + h) * T + (tb + 1) * TB], tq[:, :], QSCALE, tB[:, :], ALU.mult, ALU.mult, r=[tq.u(), tB.u()], w=[mixT.u((4 + h, tb))])
                                if tb == NTB - 1:
                                    cpy(xs_s[:, 128:129], tB[:, TB - 1:TB], r=[tB.u()], w=[xs_s.u("d")])
                                cpy(v_[:, :], bv[:, :], r=[bv.u()], w=[v_.u()])

                            def stageB(tb, h=h):
                                g_ = h * NTB + tb
                                q_, k_, v_, sc_ = qt[g_ % 3], kt[g_ % 2], vtok[g_ % 3], scal[g_ % 2]
                                PT_, Stl_ = PT[g_ % 2], Stl[g_ % 2]
                                if tb == 0:
                                    S.op("dve", lambda e: e.memset(Sst2[0][:, :], 0.0), w=[Sst2[0].u()])
                                kh_ = kh[g_ % 2]
                                for j in range(4):
                                    S.op("pe", lambda e, j=j, kh_=kh_: e.transpose(ptr[:, j * 128:(j + 1) * 128], kh_[:, j * 128:(j + 1) * 128], ident_bf),
                                         r=[kh_.u(), cb.u()], w=[ptr.u()], signal=(j == 3))
                                tsc(ktokA[:, :], ptr[:, 0:TB], pcol(P_MA), None, ALU.mult, None, r=[ptr.u(), par.u()], w=[ktokA.u()])
                                tsc(ktokB[:, :], ptr[:, 0:TB], pcol(P_MB), None, ALU.mult, None, r=[ptr.u(), par.u()], w=[ktokB.u()])
                                bs = pbank()
                                for j in range(4):
                                    mm(bs[:, j * 128:(j + 1) * 128], k_[:, j * 128:(j + 1) * 128], q_[:, j * 128:(j + 1) * 128], True, True,
                                       r=[k_.u(), q_.u()], w=[bs.u()], signal=(j == 3))
                                tt(PT_[:, :], bs[:, :], cb[:, C_MASK:C_MASK + TB], ALU.mult, r=[bs.u(), cb.u()], w=[PT_.u()])
                                bu = [pbank(), pbank()]
                                for c in range(8):
                                    jj = c // 2
                                    b_ = bu[c // 4]
                                    kx = ktokB if c % 2 else ktokA
                                    mm(b_[:, (c % 4) * 128:(c % 4 + 1) * 128], kx[:, jj * 128:(jj + 1) * 128], v_[:, jj * 128:(jj + 1) * 128],
                                       True, True, r=[kx.u(), v_.u()], w=[b_.u()], signal=(c % 4 == 3))
                                for c in range(8):
                                    b_ = bu[c // 4]
                                    Sa, Sb = Sst2[c % 2], Sst2[(c + 1) % 2]
                                    stt(Sb[:, :], Sa[:, :], sc_[:, c:c + 1], b_[:, (c % 4) * 128:(c % 4 + 1) * 128], ALU.mult, ALU.add,
                                        r=[Sa.u(), sc_.u(), b_.u()], w=[Sb.u()])
                                    act(Stl_[:, c * 128:(c + 1) * 128], Sa[:, :], AF.Copy, r=[Sa.u(), sc_.u()], w=[Stl_.u(c)], scale=sc_[:, 16 + c:17 + c])
                                if tb == NTB - 1:
                                    cpy(xs_s[:, 0:128], Sst2[0][:, :], r=[Sst2[0].u()], w=[xs_s.u("s")])
                                    stages[("ag", h)] = exchange(xs_s, 132, "s", [xs_s.u("d"), xs_s.u("s")])

                            def stageC(tb, h=h):
                                g_ = h * NTB + tb
                                oloc = oloc2[h % 2]
                                q_, v_ = qt[g_ % 3], vtok[g_ % 3]
                                PT_, Stl_ = PT[g_ % 2], Stl[g_ % 2]
                                bo = pbank()
                                for c in range(8):
                                    jj = c // 2
                                    mm(bo[:, c * 64:(c + 1) * 64], v_[:, jj * 128:(jj + 1) * 128], PT_[:, c * 64:(c + 1) * 64], True, False,
                                       r=[v_.u(), PT_.u()], w=[bo.u()], signal=False)
                                    mm(bo[:, c * 64:(c + 1) * 64], Stl_[:, c * 128:(c + 1) * 128], q_[:, c * 64:(c + 1) * 64], False, True,
                                       r=[Stl_.u(c), q_.u()], w=[bo.u()], signal=(c == 7))
                                cpy(oloc[:, tb * TB:(tb + 1) * TB], bo[:, :], r=[bo.u()], w=[oloc.u(tb)])

                            def epilogue(h=h):
                                oloc = oloc2[h % 2]
                                ag, go = stages[("ag", h)]
                                exchange_recv(ag, go, xg_s, 132, 3)
                                G = lambda i, a, b: xg_s[:, 132 * i + a:132 * i + b]
                                ru = [xg_s.u(), par.u()]
                                tsc(acc[:, :], G(0, 0, 128), pcol(P_MLT + 0), None, ALU.mult, None, r=ru, w=[acc.u()])
                                for i in (1, 2):
                                    tsc(avec[:, i:i + 1], G(i, 128, 129), pcol(P_MLT + i), pcol(P_OMM + i), ALU.mult, ALU.add, r=ru, w=[avec.u()])
                                    tsc(acc[:, :], acc[:, :], avec[:, i:i + 1], None, ALU.mult, None, r=[acc.u(), avec.u()], w=[acc.u()])
                                    stt(acc[:, :], G(i, 0, 128), pcol(P_MLT + i), acc[:, :], ALU.mult, ALU.add, r=ru + [acc.u()], w=[acc.u()])
                                cpy(Sin[:, :], acc[:, :], r=[acc.u()], w=[Sin.u()])
                                for tb in range(NTB):
                                    bc = pbank()
                                    mm(bc[:, :], Sin[:, :], mixT[:, (4 + h) * T + tb * TB:(4 + h) * T + (tb + 1) * TB], True, True, r=[Sin.u(), mixT.u((4 + h, tb))], w=[bc.u()], signal=True)
                                    o = oloc[:, tb * TB:(tb + 1) * TB]
                                    tt(o, o, bc[:, :], ALU.add, r=[oloc.u(tb), bc.u()], w=[oloc.u(tb)])
                                    rs = norm_stats(lambda k: (o, [oloc.u(tb)]), 1, tb, 1.0 / 128, pnorm[tb % 2])
                                    stt(o, o, pcol(P_NG + 4 * lp + h), rs[:, :], ALU.mult, ALU.mult, r=[oloc.u(tb), rs.u(), par.u()], w=[oloc.u(tb)])
                                    mc = h * T + tb * TB
                                    tt(mixT[:, mc:mc + TB], o, mixT[:, mc:mc + TB], ALU.mult, r=[oloc.u(tb), mixT.u((h, tb))], w=[mixT.u((h, tb))])
                            return stageA, stageB, stageC, epilogue

                        NG = NH * NTB
                        hfun = {}
                        for s_ in range(NG + 2):
                            LA, LB, LC = [], [], []
                            if s_ < NG:
                                if s_ % NTB == 0:
                                    hfun[s_ // NTB] = head_setup(s_ // NTB)
                                LA = S.capture(lambda: hfun[s_ // NTB][0](s_ % NTB))
                            if 0 <= s_ - 1 < NG:
                                LB = S.capture(lambda: hfun[(s_ - 1) // NTB][1]((s_ - 1) % NTB))
                            if 0 <= s_ - 2 < NG:
                                LC = S.capture(lambda: hfun[(s_ - 2) // NTB][2]((s_ - 2) % NTB))
                            for e_, th_ in LA:
                                if e_ == "pe":
                                    th_()
                            ep = [x for x in LA if x[0] != "pe"]
                            i_, j_ = 0, 0
                            while i_ < len(ep) or j_ < len(LB):
                                if i_ < len(ep):
                                    ep[i_][1]()
                                    i_ += 1
                                if j_ < len(LB):
                                    LB[j_][1]()
                                    j_ += 1
                            for _, th_ in LC:
                                th_()
                            if s_ >= 6 and (s_ - 6) % NTB == 0 and (s_ - 6) // NTB < NH - 1:
                                hfun[(s_ - 6) // NTB][3]()
                        hfun[NH - 1][3]()
                    S.barrier()
                    bp(l, "hgrn")
                    with ExitStack() as cs:
                        chf = sb("chf", [128, T + 2], F32, cs)
                        cacc = sb("cacc", [128, T], F32, cs)
                        Bs = sb("Bs", [128, T], F32, cs)
                        cC = [sb("cC%d" % i, [128, TB], F32, cs) for i in range(2)]
                        for j in range(4):
                            wB = load_w(w_in[(l * 28 + 16 + j) * 128:(l * 28 + 17 + j) * 128, :])
                            wC = load_w(w_in[(l * 28 + 20 + j) * 128:(l * 28 + 21 + j) * 128, :])
                            wH = load_w(w_in[(l * 28 + 24 + j) * 128:(l * 28 + 25 + j) * 128, :])
                            bk = pbank()
                            inproj_halo(wC, bk, 0)
                            bk2 = pbank()
                            inproj_halo(wH, bk2, 0)
                            act(cC[0][:, 0:2], bk[:, 0:2], AF.Copy, r=[bk.u()], w=[cC[0].u()])
                            tt(chf[:, 0:2], cC[0][:, 0:2], bk2[:, 0:2], ALU.mult, r=[cC[0].u(), bk2.u()], w=[chf.u("halo")])
                            for tb in range(NTB):
                                bC, bH, bB = pbank(), pbank(), pbank()
                                inproj_fm(wC, tb, bC)
                                inproj_fm(wH, tb, bH)
                                inproj_fm(wB, tb, bB)
                                c = cC[tb % 2]
                                act(c[:, :], bC[:, :], AF.Copy, r=[bC.u()], w=[c.u()])
                                tt(chf[:, 2 + tb * TB:2 + (tb + 1) * TB], c[:, :], bH[:, :], ALU.mult, r=[c.u(), bH.u()], w=[chf.u(tb)])
                                act(Bs[:, tb * TB:(tb + 1) * TB], bB[:, :], AF.Copy, r=[bB.u()], w=[Bs.u(tb)])
                            for tb in range(NTB):
                                rd = [chf.u(tb), par.u()] + ([chf.u(tb - 1)] if tb else [chf.u("halo")])
                                o = cacc[:, tb * TB:(tb + 1) * TB]
                                b0 = tb * TB
                                tsc(o, chf[:, b0 + 2:b0 + 2 + TB], pcol(P_CW + lp * 12 + 8 + j), None, ALU.mult, None, r=rd, w=[cacc.u(tb)])
                                stt(o, chf[:, b0 + 1:b0 + 1 + TB], pcol(P_CW + lp * 12 + 4 + j), o, ALU.mult, ALU.add, r=rd + [cacc.u(tb)], w=[cacc.u(tb)])
                                stt(o, chf[:, b0:b0 + TB], pcol(P_CW + lp * 12 + j), o, ALU.mult, ALU.add, r=rd + [cacc.u(tb)], w=[cacc.u(tb)])
                                mc = (4 + j) * T + tb * TB
                                tt(mixT[:, mc:mc + TB], o, Bs[:, tb * TB:(tb + 1) * TB], ALU.mult, r=[cacc.u(tb), Bs.u(tb)], w=[mixT.u((4 + j, tb))])
                    S.barrier()
                    bp(l, "conv")
                    hts.close()
                    with ExitStack() as os_:
                        mo2 = [sb("mo%d" % i, [128, 8 * 1024], F32, os_) for i in range(2)]
                        tres = sb("tres", [128, TB], F32, os_)

                        def op_mm(th):
                            mo = mo2[th]
                            for i in range(8):
                                wo = load_w(w_out[(l * 8 + i) * 128:(l * 8 + i + 1) * 128, :])
                                for t2 in range(2):
                                    tb = th * 2 + t2
                                    bk = pbank()
                                    for k in range(8):
                                        mm(bk[:, :], wo[:, k * 128:(k + 1) * 128], mixT[:, k * T + tb * TB:k * T + (tb + 1) * TB], k == 0, k == 7,
                                           r=[wo.u(), mixT.u((k, tb))], w=[bk.u()], signal=(k == 7))
                                    mo_ap = mo[:, i * 1024 + t2 * TB:i * 1024 + (t2 + 1) * TB]
                                    act(mo_ap, bk[:, :], AF.Copy, r=[bk.u()], w=[mo.u((i, t2))])
                                    sq = sqr[st["sq"]]
                                    st["sq"] = (st["sq"] + 1) % 2
                                    act(sq[:, :], bk[:, :], AF.Square, r=[bk.u()], w=[sq.u()])
                                    mm(pnorm[t2][:, :], ones_bf, sq[:, :], i == 0, i == 7, r=[sq.u(), cb.u()], w=[pnorm[t2].u()], signal=True)
                            return [finish_rstd(pnorm[t2], 1.0 / D) for t2 in range(2)]

                        def op_res(th, rss):
                            mo = mo2[th]
                            for t2 in range(2):
                                tb = th * 2 + t2
                                rs = rss[t2]
                                for i in range(8):
                                    mo_ap = mo[:, i * 1024 + t2 * TB:i * 1024 + (t2 + 1) * TB]
                                    stt(tres[:, :], mo_ap, pcol(P_GQM + 8 * lp + i), rs[:, :], ALU.mult, ALU.mult, r=[mo.u((i, t2)), rs.u(), par.u()], w=[tres.u()])
                                    xa = xT[:, i * T + tb * TB:i * T + (tb + 1) * TB]
                                    tt(xa, xa, tres[:, :], ALU.add, r=[xT.u((i, tb)), tres.u()], w=[xT.u((i, tb))])

                        rss1 = op_mm(1)
                        op_res(1, rss1)
                        rss0 = op_mm(0)
                        pn = pnorm[1]
                        for k in range(8):
                            sq = sqr[st["sq"]]
                            st["sq"] = (st["sq"] + 1) % 2
                            act(sq[:, 0:2], xT[:, k * T + T - 2:k * T + T], AF.Square, r=[xT.u((k, NTB - 1))], w=[sq.u()])
                            mm(pn[:, 0:2], ones_bf, sq[:, 0:2], k == 0, k == 7, r=[sq.u(), cb.u()], w=[pn.u()], signal=True)
                        act(lnv[0][:, 0:2], pn[:, 0:2], AF.Ln, r=[pn.u()], w=[lnv[0].u()], scale=1.0 / D, bias=EPS)
                        act(lnv[0][:, 0:2], lnv[0][:, 0:2], AF.Exp, r=[lnv[0].u()], w=[lnv[0].u()], scale=-0.5)
                        for k in range(8):
                            stt(xs_h[:, 2 * k:2 * k + 2], xT[:, k * T + T - 2:k * T + T], pcol(P_GPF + 8 * lp + k), lnv[0][:, 0:2], ALU.mult, ALU.mult,
                                r=[xT.u((k, NTB - 1)), lnv[0].u(), par.u()], w=[xs_h.u()])
                        ffn_ag = exchange(xs_h, 16, "hf")
                        op_res(0, rss0)
                    S.barrier()
            S.barrier()
            bp(l, "mix")
            HH = 1024 + 2
            with ExitStack() as fs:
                hF = sb("hF", [128, 8 * HH], BF16, fs)
                ug = sb("ug", [128, 1024 + 2], F32, fs)
                uv = sb("uv", [128, 1024 + 2], F32, fs)
                cg = [sb("cg%d" % i, [128, TB], F32, fs) for i in range(2)]
                cv = [sb("cv%d" % i, [128, TB], F32, fs) for i in range(2)]
                car = sb("car", [128, 4 * NFC], F32, fs)
                aT = sb("aT", [128, NFC * 1024], BF16, fs)
                yT = sb("yT", [128, 8 * 1024], F32, fs)
                tres = sb("tres2", [128, TB], F32, fs)
                ag, go = ffn_ag
                exchange_recv(ag, go, xg_h, 16)
                tsc(hal[:, :], xg_h[:, 0:16], pcol(P_SEL + 0), None, ALU.mult, None, r=[xg_h.u(), par.u()], w=[hal.u()])
                for i in range(1, 4):
                    stt(hal[:, :], xg_h[:, 16 * i:16 * i + 16], pcol(P_SEL + i), hal[:, :], ALU.mult, ALU.add,
                        r=[xg_h.u(), hal.u(), par.u()], w=[hal.u()])
                for th in range(2):
                    if th == 0:
                        for k in range(8):
                            cpy(hF[:, k * HH:k * HH + 2], hal[:, 2 * k:2 * k + 2], r=[hal.u()], w=[hF.u((k, "halo"))])
                    prenorm(hF, HH, P_GPF, [2 * th, 2 * th + 1], 2 * th)
                    for j in range(NFC):
                        wG = load_w(w_up[(l * 44 + j) * 128:(l * 44 + j + 1) * 128, :])
                        wV = load_w(w_up[(l * 44 + 22 + j) * 128:(l * 44 + 23 + j) * 128, :])
                        fo = P_FCW + lp * 132
                        halves = ((wG, ug, 0, j), (wV, uv, 2 * NFC, 22 + j))
                        for (wsl, uF, cofs, fofs) in halves:
                            if th == 0:
                                bh = pbank()
                                for k in range(8):
                                    mm(bh[:, 0:2], wsl[:, k * 128:(k + 1) * 128], hF[:, k * HH:k * HH + 2], k == 0, k == 7,
                                       r=[wsl.u(), hF.u((k, "halo"))], w=[bh.u()], signal=(k == 7))
                                act(uF[:, 0:2], bh[:, 0:2], AF.Copy, r=[bh.u()], w=[uF.u("halo")])
                            else:
                                cpy(uF[:, 0:2], car[:, cofs + 2 * j:cofs + 2 * j + 2], r=[car.u((cofs, j))], w=[uF.u("halo")])
                        for t2 in range(2):
                            cs = (cg[t2], cv[t2])
                            for hi, (wsl, uF, cofs, fofs) in enumerate(halves):
                                bk = pbank()
                                for k in range(8):
                                    c0 = k * HH + 2 + t2 * TB
                                    mm(bk[:, :], wsl[:, k * 128:(k + 1) * 128], hF[:, c0:c0 + TB], k == 0, k == 7,
                                       r=[wsl.u(), hF.u((k, t2))], w=[bk.u()], signal=(k == 7))
                                act(uF[:, 2 + t2 * TB:2 + (t2 + 1) * TB], bk[:, :], AF.Copy, r=[bk.u()], w=[uF.u(t2)])
                                act(cs[hi][:, :], bk[:, :], AF.Copy, r=[bk.u(), par.u()], w=[cs[hi].u()], scale=pcol(fo + 88 + fofs))
                            for tap, sh in ((1, 1), (0, 0)):
                                for hi, (wsl, uF, cofs, fofs) in enumerate(halves):
                                    c_ = cs[hi]
                                    rd = [uF.u(t2), uF.u(t2 - 1) if t2 else uF.u("halo"), par.u(), c_.u()]
                                    stt(c_[:, :], uF[:, sh + t2 * TB:sh + (t2 + 1) * TB], pcol(fo + tap * 44 + fofs), c_[:, :], ALU.mult, ALU.add, r=rd, w=[c_.u()])
                            act(cs[0][:, :], cs[0][:, :], AF.Silu, r=[cs[0].u()], w=[cs[0].u()])
                            ac = j * 1024 + t2 * TB
                            tt(aT[:, ac:ac + TB], cs[0][:, :], cs[1][:, :], ALU.mult, r=[cs[0].u(), cs[1].u()], w=[aT.u((j, t2))])
                        if th == 0:
                            for (wsl, uF, cofs, fofs) in halves:
                                cpy(car[:, cofs + 2 * j:cofs + 2 * j + 2], uF[:, 1024:1026], r=[uF.u(1)], w=[car.u((cofs, j))])
                    for i in range(8):
                        wd = [load_w(w_dn[(l * 8 + i) * 128:(l * 8 + i + 1) * 128, 0:1024]),
                              load_w(w_dn[(l * 8 + i) * 128:(l * 8 + i + 1) * 128, 1024:2048]),
                              load_w(w_dn[(l * 8 + i) * 128:(l * 8 + i + 1) * 128, 2048:2816])]
                        for t2 in range(2):
                            bk = pbank()
                            for k in range(NFC):
                                wsl = wd[k // 8]
                                kk = k % 8
                                mm(bk[:, :], wsl[:, kk * 128:(kk + 1) * 128], aT[:, k * 1024 + t2 * TB:k * 1024 + (t2 + 1) * TB], k == 0, k == NFC - 1,
                                   r=[wsl.u(), aT.u((k, t2))], w=[bk.u()], signal=(k == NFC - 1))
                            y_ap = yT[:, i * 1024 + t2 * TB:i * 1024 + (t2 + 1) * TB]
                            act(y_ap, bk[:, :], AF.Copy, r=[bk.u()], w=[yT.u((i, t2))])
                            sq = sqr[st["sq"]]
                            st["sq"] = (st["sq"] + 1) % 2
                            act(sq[:, :], bk[:, :], AF.Square, r=[bk.u()], w=[sq.u()])
                            mm(pnorm[t2][:, :], ones_bf, sq[:, :], i == 0, i == 7, r=[sq.u(), cb.u()], w=[pnorm[t2].u()], signal=True)
                    for t2 in range(2):
                        tb = th * 2 + t2
                        rs = finish_rstd(pnorm[t2], 1.0 / D)
                        for i in range(8):
                            y_ap = yT[:, i * 1024 + t2 * TB:i * 1024 + (t2 + 1) * TB]
                            stt(tres[:, :], y_ap, pcol(P_GQF + 8 * lp + i), rs[:, :], ALU.mult, ALU.mult, r=[yT.u((i, t2)), rs.u(), par.u()], w=[tres.u()])
                            xa = xT[:, i * T + tb * TB:i * T + (tb + 1) * TB]
                            tt(xa, xa, tres[:, :], ALU.add, r=[xT.u((i, tb)), tres.u()], w=[xT.u((i, tb))])
            S.barrier()
            bp(l, "ffn")

          except _Stop:
            break
        S.enabled = True
        for _i in range(int(os.environ.get("K_DUMMY_PE", "0"))):
            mm(pring[0][:, 0:2], ones_bf, cb[:, 0:2], True, True, r=[cb.u()], w=[pring[0].u()], signal=(_i % 64 == 63))
        for _i in range(int(os.environ.get("K_DUMMY_ACT", "0"))):
            act(lnv[0][:, 0:8], lnv[0][:, 0:8], AF.Copy, r=[lnv[0].u()], w=[lnv[0].u()])
        for _i in range(int(os.environ.get("K_DUMMY_DVE", "0"))):
            cpy(lnv[1][:, 0:8], lnv[1][:, 8:16], r=[lnv[1].u()], w=[lnv[1].u()])
        for _i in range(int(os.environ.get("K_DUMMY_SLOW", "0"))):
            act(xT[:, 0:2048], xT[:, 0:2048], AF.Copy, r=[xT.u((0, 0)), xT.u((0, 1)), xT.u((0, 2)), xT.u((0, 3))],
                w=[xT.u((0, 0)), xT.u((0, 1)), xT.u((0, 2)), xT.u((0, 3))])
        if os.environ.get("K_DUMMY_HI"):
            for _i in range(nlayers * 44 - 48, nlayers * 44):
                load_w(w_up[_i * 128:(_i + 1) * 128, :])
            for _i in range(nlayers * 8 - 8, nlayers * 8):
                load_w(w_dn[_i * 128:(_i + 1) * 128, 0:1024])
                load_w(w_dn[_i * 128:(_i + 1) * 128, 2048:2816])
        for _i in range(int(os.environ.get("K_DUMMY_DMA", "0"))):
            load_w(w_up[(_i % 40) * 128:(_i % 40 + 1) * 128, :])
        for _i in range(int(os.environ.get("K_DUMMY_CC", "0"))):
            ag_, go_ = exchange(xs_h, 16, "dummy")
            exchange_recv(ag_, go_, xg_h, 16)
        sigs = []
        for k in range(8):
            for tb in range(NTB):
                c0 = k * T + tb * TB
                sigs.append(S.dma("sp", lambda e, c0=c0: e.dma_start(out=out[:, c0:c0 + TB], in_=xT[:, c0:c0 + TB]), r=[xT.u((k, tb))]))
        for key, val in sigs:
            S._wait("sp", key, val)
        S.emit()
        print("kernel: recorded ops", S.nops, {e: len(S.q[e]) for e in S.ENG}, "sigcnt", S.cnt, "dma", [c for _, c in S.dsem][:4], "cc", S.cccnt)
    return nc


def _prep_inputs(x, lb_param, w_in, w_out, conv_w, ffn_w_up, ffn_conv_w, ffn_w_down,
                 hgrn_norm_g, pre_mix_g, post_mix_g, pre_ffn_g, post_ffn_g):
    f = np.float32

    def wl(w, nk):
        L, K, N = w.shape
        nj = N // 128
        a = np.asarray(w, f).reshape(L, nk, 128, nj, 128).transpose(0, 3, 2, 1, 4)
        return np.ascontiguousarray(a).reshape(L * nj * 128, nk * 128)

    w_in_r = wl(w_in, 8)
    w_out_r = wl(w_out, 8)
    w_up_r = wl(ffn_w_up, 8)
    w_dn_r = wl(ffn_w_down, NFC)

    def pk(a, n):
        a = np.asarray(a, f).reshape(DEPTH, n, 128).transpose(2, 0, 1)
        return np.ascontiguousarray(a).reshape(128, DEPTH * n)

    def pk3(a, n):
        a = np.asarray(a, f).reshape(DEPTH, 3, n, 128).transpose(3, 0, 1, 2)
        return np.ascontiguousarray(a).reshape(128, DEPTH * 3 * n)

    base = np.concatenate([pk(lb_param, 4), pk(hgrn_norm_g, 4), pk(pre_mix_g, 8), pk(post_mix_g, 8), pk(pre_ffn_g, 8),
                           pk(post_ffn_g, 8), pk3(conv_w, 4), pk3(ffn_conv_w, 44)], axis=1)
    assert base.shape[1] == P_SEL
    cbm = np.zeros((128, NCB), np.float32)
    cbm[:, C_ONES:C_ONES + 128] = 1.0
    cbm[:, C_ID:C_ID + 128] = np.eye(128, dtype=np.float32)
    s = np.arange(128)[:, None]
    t = np.arange(128)[None, :]
    m2 = ((s // 64) == (t // 64)) & (s <= t)
    cbm[:, C_MASK:C_MASK + 512] = np.tile(m2.astype(np.float32), (1, 4))
    rst = np.ones((128, 512), np.float32)
    rst[:, ::64] = 0.0
    cbm[:, C_RST:C_RST + 512] = rst
    cbm[:, C_ONE:C_ONE + 512] = 1.0
    cbm = cbm.astype(ml_dtypes.bfloat16)

    xs = np.asarray(x, f)
    in_maps = []
    for c in range(8):
        b, sgm = c // 4, c % 4
        xc = xs[b, sgm * T:(sgm + 1) * T, :]
        xc = np.ascontiguousarray(xc.T.reshape(8, 128, T).transpose(1, 0, 2)).reshape(128, 8 * T)
        extra = np.zeros((128, 14), np.float32)
        extra[:64, 12] = 1.0
        extra[64:, 13] = 1.0
        for i in range(4):
            extra[:, i] = 1.0 if i == sgm - 1 else 0.0
            extra[:, 4 + i] = 1.0 if i < sgm else 0.0
            extra[:, 8 + i] = 0.0 if i < sgm else 1.0
        par = np.ascontiguousarray(np.concatenate([base, extra], axis=1))
        in_maps.append({"xin": xc, "w_in": w_in_r, "w_out": w_out_r, "w_up": w_up_r, "w_dn": w_dn_r, "par": par, "cb": cbm})
    return in_maps


_NC_CACHE = {}
SPLIT = [(0, 4)]


def _to_fm(xc):
    return np.ascontiguousarray(xc.T.reshape(8, 128, T).transpose(1, 0, 2)).reshape(128, 8 * T)


def kernel(x, lb_param, w_in, w_out, conv_w, ffn_w_up, ffn_conv_w, ffn_w_down,
           hgrn_norm_g, pre_mix_g, post_mix_g, pre_ffn_g, post_ffn_g, _dbg=None, _nlayers=None):
    in_maps = _prep_inputs(x, lb_param, w_in, w_out, conv_w, ffn_w_up, ffn_conv_w, ffn_w_down,
                           hgrn_norm_g, pre_mix_g, post_mix_g, pre_ffn_g, post_ffn_g)
    split = SPLIT if _nlayers is None else [(0, _nlayers)]
    full = {k: in_maps[0][k] for k in ("w_in", "w_out", "w_up", "w_dn")}
    res = None
    for (l0, l1) in split:
        nl = l1 - l0
        key = (_dbg, nl, l0)
        if key not in _NC_CACHE:
            _NC_CACHE[key] = build_program(nl, _dbg, l0)
        nc = _NC_CACHE[key]
        for c, m in enumerate(in_maps):
            m["w_in"] = full["w_in"][l0 * 28 * 128:l1 * 28 * 128]
            m["w_out"] = full["w_out"][l0 * 8 * 128:l1 * 8 * 128]
            m["w_up"] = full["w_up"][l0 * 44 * 128:l1 * 44 * 128]
            m["w_dn"] = full["w_dn"][l0 * 8 * 128:l1 * 8 * 128]
            if res is not None:
                m["xin"] = np.asarray(res.results[c]["out"])
        res = run_bass_kernel_spmd(nc, in_maps, core_ids=list(range(8)))
    outp = np.empty((2, 4 * T, D), np.float32)
    for c in range(8):
        b, sgm = c // 4, c % 4
        o = np.asarray(res.results[c]["out"]).reshape(128, 8, T).transpose(2, 1, 0).reshape(T, D)
        outp[b, sgm * T:(sgm + 1) * T, :] = o
    return outp
```

```python
import os
from contextlib import ExitStack

import numpy as np
import ml_dtypes

import concourse.bass as bass
import concourse.mybir as mybir
from concourse.bass_utils import run_bass_kernel_spmd

F32 = mybir.dt.float32
BF16 = mybir.dt.bfloat16
AF = mybir.ActivationFunctionType
ALU = mybir.AluOpType

D = 1024
T = 2048
TB = 512
NTB = T // TB
DEPTH = 4
NH = 4
DFF = 2816
NFC = DFF // 128
EPS = 1e-6
QSCALE = 128 ** -0.5

P_LBP = 0
P_NG = P_LBP + 16
P_GPM = P_NG + 16
P_GQM = P_GPM + 32
P_GPF = P_GQM + 32
P_GQF = P_GPF + 32
P_CW = P_GQF + 32
P_FCW = P_CW + 48
P_SEL = P_FCW + 528
P_MLT = P_SEL + 4
P_OMM = P_MLT + 4
P_MA = P_OMM + 4
P_MB = P_MA + 1
NPAR = P_MB + 1
C_ONES = 0
C_ID = 128
C_MASK = 256
C_RST = 768
C_ONE = 1280
NCB = 1792

NSLOT = 6


class TT:
    __slots__ = ("w", "r")

    def __init__(self):
        self.w = None
        self.r = {}


class Tile:
    def __init__(self, t):
        self.t = t
        self.units = {}

    def u(self, key=0):
        x = self.units.get(key)
        if x is None:
            x = self.units[key] = TT()
        return x

    def __getitem__(self, idx):
        return self.t[idx]


class Sched:
    ENG = ("pe", "act", "dve", "pool", "sp")

    def __init__(self, nc, es):
        self.nc = nc
        self.q = {e: [] for e in self.ENG}
        self.cnt = {e: 0 for e in self.ENG}
        self.waited = {e: {} for e in self.ENG}
        self.sem = {e: es.enter_context(nc.semaphore("c_" + e)) for e in self.ENG}
        self.dsem = [[es.enter_context(nc.semaphore("d%d" % i)), 0] for i in range(24)]
        self.drr = 0
        self.es = es
        self.cccnt = 0
        self.semh = dict(self.sem)
        for i, (h, _) in enumerate(self.dsem):
            self.semh["d%d" % i] = h
        self.nops = 0
        self.enabled = True
        self.seen = {}
        self.cap = None

    def _wait(self, eng, key, val):
        if eng == "pe" and key == "pe":
            return
        if key in self.cnt:
            assert val <= self.cnt[key], ("wait on future signal", eng, key, val, self.cnt[key])
        if self.waited[eng].get(key, 0) >= val:
            return
        self.waited[eng][key] = val
        if self.seen.get(key, 0) < val:
            self.seen[key] = val
        self.q[eng].append(("w", key, val))

    def _deps(self, eng, reads, writes):
        for t in reads:
            if t.w is not None:
                self._wait(eng, *t.w)
        for t in writes:
            if t.w is not None:
                self._wait(eng, *t.w)
            for k, v in t.r.items():
                self._wait(eng, k, v)

    def _mark(self, sig, reads, writes):
        k, v = sig
        for t in reads:
            if t.r.get(k, 0) < v:
                t.r[k] = v
        for t in writes:
            t.w = sig
            t.r = {}

    def capture(self, f):
        assert self.cap is None
        self.cap = []
        try:
            f()
        finally:
            lst, self.cap = self.cap, None
        return lst

    def op(self, eng, fn, r=(), w=(), signal=True):
        if not self.enabled:
            return
        if self.cap is not None:
            self.cap.append((eng, lambda: self.op(eng, fn, r, w, signal)))
            return
        self._deps(eng, r, w)
        self.nops += 1
        if signal:
            self.cnt[eng] += 1
            sig = (eng, self.cnt[eng])
            self.q[eng].append(("i", fn, eng, 1))
        else:
            sig = (eng, self.cnt[eng] + 1)
            self.q[eng].append(("i", fn, None, 0))
        self._mark(sig, r, w)

    def dma(self, eng, fn, r=(), w=()):
        if not self.enabled:
            return None
        if self.cap is not None:
            self.cap.append((eng, lambda: self.dma(eng, fn, r, w)))
            return None
        self._deps(eng, r, w)
        i = self.drr
        self.drr = (self.drr + 1) % len(self.dsem)
        key = "d%d" % i
        prev = self.dsem[i][1]
        if prev and self.seen.get(key, 0) < prev:
            self._wait(eng, key, prev)
        self.dsem[i][1] = prev + 16
        sig = (key, prev + 16)
        self.q[eng].append(("i", fn, key, 16))
        self._mark(sig, r, w)
        return sig

    def collective(self, fn, r=(), w=()):
        if not self.enabled:
            return
        if self.cap is not None:
            self.cap.append(("pool", lambda: self.collective(fn, r, w)))
            return
        self._deps("pool", r, w)
        self.cccnt += 1
        key = "cc%d" % self.cccnt
        self.semh[key] = self.es.enter_context(self.nc.semaphore(key))
        sig = (key, 1)
        self.q["pool"].append(("i", fn, key, 1))
        self._mark(sig, r, w)

    def barrier(self):
        if not self.enabled:
            return
        cur = [(e, self.cnt[e]) for e in self.ENG if self.cnt[e]]
        cur += [("d%d" % i, c) for i, (_, c) in enumerate(self.dsem) if c]
        cur += [("cc%d" % i, 1) for i in range(1, self.cccnt + 1)]
        for e in self.ENG:
            for k, v in cur:
                if k == e:
                    continue
                self._wait(e, k, v)

    def emit(self):
        nc = self.nc
        engs = {"pe": "tensor", "act": "scalar", "dve": "vector", "pool": "gpsimd", "sp": "sync"}
        with nc.Block() as block:
            for e in self.ENG:
                items = self.q[e]

                def body(eng, items=items):
                    for it in items:
                        if it[0] == "w":
                            eng.wait_ge(self.semh[it[1]], it[2])
                        else:
                            ins = it[1](eng)
                            if it[2] is not None:
                                ins.then_inc(self.semh[it[2]], it[3])

                getattr(block, engs[e])(body)


def build_program(nlayers=DEPTH, dbg=None, lbase=0):
    nc = bass.Bass("TRN2", target_bir_lowering=False)
    xin = nc.dram_tensor("xin", [128, 8 * T], F32, kind="ExternalInput").ap()
    w_in = nc.dram_tensor("w_in", [nlayers * 28 * 128, 1024], F32, kind="ExternalInput").ap()
    w_out = nc.dram_tensor("w_out", [nlayers * 8 * 128, 1024], F32, kind="ExternalInput").ap()
    w_up = nc.dram_tensor("w_up", [nlayers * 44 * 128, 1024], F32, kind="ExternalInput").ap()
    w_dn = nc.dram_tensor("w_dn", [nlayers * 8 * 128, NFC * 128], F32, kind="ExternalInput").ap()
    par_d = nc.dram_tensor("par", [128, NPAR], F32, kind="ExternalInput").ap()
    cb_d = nc.dram_tensor("cb", [128, NCB], BF16, kind="ExternalInput").ap()
    out = nc.dram_tensor("out", [128, 8 * T], F32, kind="ExternalOutput").ap()

    es = ExitStack()
    with es:
        S = Sched(nc, es)

        uid = [0]

        def sb(name, shape, dt, stack=es):
            uid[0] += 1
            return Tile(stack.enter_context(nc.sbuf_tensor("%s_%d" % (name, uid[0]), shape, dt)))

        def ps(name, shape, dt, stack=es):
            uid[0] += 1
            return Tile(stack.enter_context(nc.psum_tensor("%s_%d" % (name, uid[0]), shape, dt)))

        xT = sb("xT", [128, 8 * T], F32)
        par = sb("par", [128, NPAR], F32)
        cb = sb("cb", [128, NCB], BF16)
        lbt = sb("lbt", [128, 64], F32)
        wring = [sb("wr%d" % i, [128, 1024], BF16) for i in range(NSLOT)]
        sqr = [sb("sq%d" % i, [128, TB], BF16) for i in range(2)]
        lnv = [sb("lnv%d" % i, [128, TB], F32) for i in range(1)]
        rstd = [sb("rstd%d" % i, [128, TB], F32) for i in range(2)]
        xs_h = sb("xs_h", [128, 16], F32)
        xg_h = sb("xg_h", [128, 64], F32)
        hal = sb("hal", [128, 16], F32)
        pring = [ps("pb%d" % i, [128, TB], F32) for i in range(5)]
        pnorm = [ps("pn%d" % i, [128, TB], F32) for i in range(2)]
        ptr = ps("ptr", [128, 1024], BF16)
        st = {"wr": 0, "pr": 0, "sq": 0, "ln": 0, "ex": 0}

        ones_bf = cb[:, C_ONES:C_ONES + 128]
        ident_bf = cb[:, C_ID:C_ID + 128]

        def pcol(off):
            return par[:, off:off + 1]

        def wslot():
            i = st["wr"]
            st["wr"] = (i + 1) % NSLOT
            return wring[i]

        def pbank():
            i = st["pr"]
            st["pr"] = (i + 1) % len(pring)
            return pring[i]

        def load_w(src_ap):
            sl = wslot()
            ncol = src_ap.shape[1]
            S.dma("pool", lambda e, sl=sl, src_ap=src_ap, ncol=ncol: e.dma_start(out=sl[:, 0:ncol], in_=src_ap), w=[sl.u()])
            return sl

        def mm(out_ap, lhsT, rhs, start, stop, r, w, signal):
            S.op("pe", lambda e: e.matmul(out_ap, lhsT=lhsT, rhs=rhs, start=start, stop=stop), r=r, w=w, signal=signal)

        def act(out_ap, in_ap, func, r, w, scale=None, bias=None):
            kw = {}
            if scale is not None:
                kw["scale"] = scale
            if bias is not None:
                kw["bias"] = bias
            S.op("act", lambda e: e.activation(out=out_ap, in_=in_ap, func=func, **kw), r=r, w=w)

        def tsc(out_ap, in0, s1, s2, op0, op1, r, w, eng="dve"):
            if s2 is None:
                S.op(eng, lambda e: e.tensor_scalar(out=out_ap, in0=in0, scalar1=s1, scalar2=None, op0=op0), r=r, w=w)
            else:
                S.op(eng, lambda e: e.tensor_scalar(out=out_ap, in0=in0, scalar1=s1, scalar2=s2, op0=op0, op1=op1), r=r, w=w)

        def stt(out_ap, in0, sc, in1, op0, op1, r, w):
            S.op("dve", lambda e: e.scalar_tensor_tensor(out=out_ap, in0=in0, scalar=sc, in1=in1, op0=op0, op1=op1), r=r, w=w)

        def tt(out_ap, in0, in1, op, r, w, eng="dve"):
            S.op(eng, lambda e: e.tensor_tensor(out=out_ap, in0=in0, in1=in1, op=op), r=r, w=w)

        def cpy(out_ap, in_ap, r, w, eng="dve"):
            S.op(eng, lambda e: e.tensor_copy(out_ap, in_ap), r=r, w=w)

        def exchange(src, ncols, tag, runits=None):
            i = st["ex"]
            st["ex"] += 1
            agin = nc.dram_tensor("agin%d" % i, [128, ncols], F32)
            agout = nc.dram_tensor("agout%d" % i, [4 * 128, ncols], F32)
            gi, go = TT(), TT()
            S.dma("sp", lambda e: e.dma_start(out=agin.ap()[:, :], in_=src[:, 0:ncols]), r=(runits or [src.u()]), w=[gi])
            S.collective(lambda e: e.collective_compute("AllGather", ALU.bypass, replica_groups=[[0, 1, 2, 3], [4, 5, 6, 7]],
                                                        ins=[agin.ap().opt()], outs=[agout.ap().opt()]), r=[gi], w=[go])
            return agout, go

        def exchange_recv(agout, go, dst, ncols, nr=4):
            S.dma("sp", lambda e: e.dma_start(out=dst[:, 0:nr * ncols].rearrange("p (r c) -> p r c", r=nr),
                                              in_=agout.ap()[0:nr * 128, :].rearrange("(r p) c -> p r c", p=128)), r=[go], w=[dst.u()])

        def norm_stats(src_fn, nk, tb_key, scale, pn):
            for k in range(nk):
                ap, un = src_fn(k)
                sq = sqr[st["sq"]]
                st["sq"] = (st["sq"] + 1) % 2
                act(sq[:, :], ap, AF.Square, r=un, w=[sq.u()])
                mm(pn[:, :], ones_bf, sq[:, :], k == 0, k == nk - 1, r=[sq.u(), cb.u()], w=[pn.u()], signal=True)
            return finish_rstd(pn, scale)

        def finish_rstd(pn, scale):
            i = st["ln"]
            st["ln"] = (i + 1) % 2
            lv, rs = lnv[0], rstd[i]
            act(lv[:, :], pn[:, :], AF.Ln, r=[pn.u()], w=[lv.u()], scale=scale, bias=EPS)
            act(rs[:, :], lv[:, :], AF.Exp, r=[lv.u()], w=[rs.u()], scale=-0.5)
            return rs

        for k in range(8):
            for tb in range(NTB):
                c0 = k * T + tb * TB
                S.dma("sp", lambda e, c0=c0: e.dma_start(out=xT[:, c0:c0 + TB], in_=xin[:, c0:c0 + TB]), w=[xT.u((k, tb))])
        S.dma("sp", lambda e: e.dma_start(out=par[:, :], in_=par_d[:, :]), w=[par.u()])
        S.dma("sp", lambda e: e.dma_start(out=cb[:, :], in_=cb_d[:, :]), w=[cb.u()])
        L = lambda l: lbt[:, 48 + 0:48 + 4]
        mx = lbt[:, 48:52]
        tt(mx, par[:, P_LBP:P_LBP + 4], par[:, P_LBP + 4:P_LBP + 8], ALU.max, r=[par.u()], w=[lbt.u()])
        tt(mx, mx, par[:, P_LBP + 8:P_LBP + 12], ALU.max, r=[lbt.u()], w=[lbt.u()])
        tt(mx, mx, par[:, P_LBP + 12:P_LBP + 16], ALU.max, r=[lbt.u()], w=[lbt.u()])
        for l in range(4):
            tt(lbt[:, 4 * l:4 * l + 4], par[:, P_LBP + 4 * l:P_LBP + 4 * l + 4], mx, ALU.subtract, r=[par.u(), lbt.u()], w=[lbt.u()])
        act(lbt[:, 0:16], lbt[:, 0:16], AF.Exp, r=[lbt.u()], w=[lbt.u()])
        sm = lbt[:, 52:56]
        tt(sm, lbt[:, 0:4], lbt[:, 4:8], ALU.add, r=[lbt.u()], w=[lbt.u()])
        tt(sm, sm, lbt[:, 8:12], ALU.add, r=[lbt.u()], w=[lbt.u()])
        tt(sm, sm, lbt[:, 12:16], ALU.add, r=[lbt.u()], w=[lbt.u()])
        S.op("dve", lambda e: e.reciprocal(sm, sm), r=[lbt.u()], w=[lbt.u()])
        for l in range(4):
            tt(lbt[:, 4 * l:4 * l + 4], lbt[:, 4 * l:4 * l + 4], sm, ALU.mult, r=[lbt.u()], w=[lbt.u()])
        tt(lbt[:, 8:12], lbt[:, 8:12], lbt[:, 4:8], ALU.add, r=[lbt.u()], w=[lbt.u()])
        tt(lbt[:, 12:16], lbt[:, 12:16], lbt[:, 8:12], ALU.add, r=[lbt.u()], w=[lbt.u()])
        S.op("dve", lambda e: e.memset(lbt[:, 0:4], 0.0), w=[lbt.u()])
        tsc(lbt[:, 16:32], lbt[:, 0:16], -1.0, 1.0, ALU.mult, ALU.add, r=[lbt.u()], w=[lbt.u()])
        tsc(lbt[:, 32:48], lbt[:, 16:32], -1.0, None, ALU.mult, None, r=[lbt.u()], w=[lbt.u()])

        HW = T + 2

        class _Stop(Exception):
            pass

        def bp(l, name):
            if dbg == (l, name):
                S.barrier()
                S.enabled = False

        for l in range(nlayers):
          try:
            lp = l + lbase
            bp(l, "pre")
            with ExitStack() as mixs:
                mixT = sb("mixT", [128, 8 * T], BF16, mixs)
                hts = ExitStack()
                hT = sb("hT", [128, 8 * HW], BF16, hts)

                def prenorm(hT, hw, gofs, tbs, toff):
                    for tb in tbs:
                        rs = norm_stats(lambda k: (xT[:, k * T + tb * TB:k * T + (tb + 1) * TB], [xT.u((k, tb))]), 8, tb, 1.0 / D, pnorm[tb % 2])
                        for k in range(8):
                            c0 = k * hw + 2 + (tb - toff) * TB
                            stt(hT[:, c0:c0 + TB], xT[:, k * T + tb * TB:k * T + (tb + 1) * TB], pcol(gofs + 8 * lp + k), rs[:, :],
                                ALU.mult, ALU.mult, r=[xT.u((k, tb)), rs.u(), par.u()], w=[hT.u((k, tb - toff))])

                def halo_exchange(hT, hw, last_c0, last_key):
                    for k in range(8):
                        cpy(xs_h[:, 2 * k:2 * k + 2], hT[:, k * hw + last_c0:k * hw + last_c0 + 2], r=[hT.u((k, last_key))], w=[xs_h.u()])
                    ag, go = exchange(xs_h, 16, "h")
                    exchange_recv(ag, go, xg_h, 16)
                    tsc(hal[:, :], xg_h[:, 0:16], pcol(P_SEL + 0), None, ALU.mult, None, r=[xg_h.u(), par.u()], w=[hal.u()])
                    for i in range(1, 4):
                        stt(hal[:, :], xg_h[:, 16 * i:16 * i + 16], pcol(P_SEL + i), hal[:, :], ALU.mult, ALU.add,
                            r=[xg_h.u(), hal.u(), par.u()], w=[hal.u()])
                    for k in range(8):
                        cpy(hT[:, k * hw:k * hw + 2], hal[:, 2 * k:2 * k + 2], r=[hal.u()], w=[hT.u((k, "halo"))])

                prenorm(hT, HW, P_GPM, range(NTB), 0)
                bp(l, "norm")
                halo_exchange(hT, HW, 2 + T - 2, NTB - 1)
                bp(l, "halo")

                def hrhs(k, tb):
                    c0 = k * HW + 2 + tb * TB
                    return hT[:, c0:c0 + TB]

                def inproj_fm(wsl, tb, bank):
                    for k in range(8):
                        mm(bank[:, :], wsl[:, k * 128:(k + 1) * 128], hrhs(k, tb), k == 0, k == 7,
                           r=[wsl.u(), hT.u((k, tb))], w=[bank.u()], signal=(k == 7))

                def inproj_halo(wsl, bank, col):
                    for k in range(8):
                        mm(bank[:, col:col + 2], wsl[:, k * 128:(k + 1) * 128], hT[:, k * HW:k * HW + 2], k == 0, k == 7,
                           r=[wsl.u(), hT.u((k, "halo"))], w=[bank.u()], signal=(k == 7))

                with ExitStack() as mxs:
                    with ExitStack() as hs:
                        tq = sb("tq", [128, TB], F32, hs)
                        tsig = sb("tsig", [128, TB], F32, hs)
                        tk = sb("tk", [128, TB], F32, hs)
                        tb_ = sb("tb_", [128, TB], F32, hs)
                        tB = sb("tB", [128, TB], F32, hs)
                        tbm = sb("tbm", [128, TB], F32, hs)
                        tEp = sb("tEp", [128, TB], F32, hs)
                        qt = [sb("qt%d" % i, [128, TB], BF16, hs) for i in range(3)]
                        kt = [sb("kt%d" % i, [128, TB], BF16, hs) for i in range(2)]
                        kh = [sb("kh%d" % i, [128, TB], BF16, hs) for i in range(2)]
                        vtok = [sb("vtok%d" % i, [128, TB], BF16, hs) for i in range(3)]
                        ktokA = sb("ktokA", [128, TB], BF16, hs)
                        ktokB = sb("ktokB", [128, TB], BF16, hs)
                        PT = [sb("PT%d" % i, [128, TB], BF16, hs) for i in range(2)]
                        Stl = [sb("Stl%d" % i, [128, 8 * 128], BF16, hs) for i in range(2)]
                        scal = [sb("scal%d" % i, [128, 24], F32, hs) for i in range(2)]
                        oloc2 = [sb("oloc%d" % i, [128, T], F32, hs) for i in range(2)]
                        Sst2 = [sb("Sst%d" % i, [128, 128], F32, hs) for i in range(2)]
                        Bprev = sb("Bprev", [128, 1], F32, hs)
                        xs_s = sb("xs_s", [128, 132], F32, hs)
                        xg_s = sb("xg_s", [128, 3 * 132], F32, hs)
                        acc = sb("acc", [128, 128], F32, hs)
                        avec = sb("avec", [128, 4], F32, hs)
                        Sin = sb("Sin", [128, 128], BF16, hs)

                        stages = {}
                        def head_setup(h):
                            wq = load_w(w_in[(l * 28 + 0 + h) * 128:(l * 28 + 1 + h) * 128, :])
                            wf = load_w(w_in[(l * 28 + 4 + h) * 128:(l * 28 + 5 + h) * 128, :])
                            wi = load_w(w_in[(l * 28 + 8 + h) * 128:(l * 28 + 9 + h) * 128, :])
                            wg = load_w(w_in[(l * 28 + 12 + h) * 128:(l * 28 + 13 + h) * 128, :])
                            lb_c = lbt[:, 4 * lp + h:4 * lp + h + 1]
                            oml_c = lbt[:, 16 + 4 * lp + h:16 + 4 * lp + h + 1]
                            noml_c = lbt[:, 32 + 4 * lp + h:32 + 4 * lp + h + 1]

                            def stageA(tb, h=h, wq=wq, wf=wf, wi=wi, wg=wg, lb_c=lb_c, oml_c=oml_c, noml_c=noml_c):
                                g_ = h * NTB + tb
                                q_, k_, v_, sc_ = qt[g_ % 3], kt[g_ % 2], vtok[g_ % 3], scal[g_ % 2]
                                if tb == 0:
                                    S.op("dve", lambda e: e.memset(Bprev[:, :], 0.0), w=[Bprev.u()])
                                bf, bq, bg, bv = pbank(), pbank(), pbank(), pbank()
                                inproj_fm(wf, tb, bf)
                                inproj_fm(wq, tb, bq)
                                inproj_fm(wg, tb, bg)
                                for tt_ in range(4):
                                    for k in range(8):
                                        c0 = k * HW + 2 + tb * TB + tt_ * 128
                                        mm(bv[:, tt_ * 128:(tt_ + 1) * 128], hT[:, c0:c0 + 128], wi[:, k * 128:(k + 1) * 128], k == 0, k == 7,
                                           r=[wi.u(), hT.u((k, tb))], w=[bv.u()], signal=(k == 7))
                                act(tsig[:, :], bf[:, :], AF.Sigmoid, r=[bf.u()], w=[tsig.u()])
                                act(tEp[:, :], tsig[:, :], AF.Ln, r=[tsig.u(), lbt.u()], w=[tEp.u()], scale=oml_c, bias=lb_c)
                                act(tq[:, :], bq[:, :], AF.Silu, r=[bq.u()], w=[tq.u()])
                                act(mixT[:, h * T + tb * TB:h * T + (tb + 1) * TB], bg[:, :], AF.Silu, r=[bg.u()], w=[mixT.u((h, tb))])
                                S.op("dve", lambda e: e.tensor_tensor_scan(out=tb_[:, :], data0=cb[:, C_RST:C_RST + TB], data1=tEp[:, :], initial=0.0,
                                                                           op0=ALU.mult, op1=ALU.add), r=[cb.u(), tEp.u()], w=[tb_.u()])
                                b3 = tb_[:, :].rearrange("p (c t) -> p c t", t=64)
                                bm3 = tbm[:, :].rearrange("p (c t) -> p c t", t=64)
                                tt(bm3, b3, b3[:, :, 31:32].to_broadcast([128, 8, 64]), ALU.subtract, r=[tb_.u()], w=[tbm.u()])
                                S.op("dve", lambda e: e.tensor_tensor_scan(out=tB[:, :], data0=cb[:, C_ONE:C_ONE + TB], data1=tEp[:, :], initial=Bprev[:, 0:1],
                                                                           op0=ALU.mult, op1=ALU.add), r=[cb.u(), tEp.u(), Bprev.u()], w=[tB.u()])
                                cpy(Bprev[:, :], tB[:, TB - 1:TB], r=[tB.u()], w=[Bprev.u()])
                                tsc(tk[:, :], tsig[:, :], noml_c, oml_c, ALU.mult, ALU.add, r=[tsig.u(), lbt.u()], w=[tk.u()])
                                act(sc_[:, 0:8].rearrange("p (c o) -> p c o", o=1), b3[:, :, 63:64], AF.Exp, r=[tb_.u()], w=[sc_.u()])
                                act(sc_[:, 8:16].rearrange("p (c o) -> p c o", o=1), bm3[:, :, 63:64], AF.Exp, r=[tbm.u()], w=[sc_.u()])
                                act(sc_[:, 16:24].rearrange("p (c o) -> p c o", o=1), b3[:, :, 31:32], AF.Exp, r=[tb_.u()], w=[sc_.u()])
                                act(tEp[:, :], tbm[:, :], AF.Exp, r=[tbm.u()], w=[tEp.u()])
                                act(tbm[:, :], tbm[:, :], AF.Exp, r=[tbm.u()], w=[tbm.u()], scale=-1.0)
                                act(tB[:, :], tB[:, :], AF.Exp, r=[tB.u()], w=[tB.u()])
                                stt(q_[:, :], tq[:, :], QSCALE, tEp[:, :], ALU.mult, ALU.mult, r=[tq.u(), tEp.u()], w=[q_.u()])
                                tt(tk[:, :], tk[:, :], tbm[:, :], ALU.mult, r=[tk.u(), tbm.u()], w=[tk.u()])
                                cpy(k_[:, :], tk[:, :], r=[tk.u()], w=[k_.u()])
                                kh_ = kh[g_ % 2]
                                tt(kh_[:, :].rearrange("p (c t) -> p c t", t=64), tk[:, :].rearrange("p (c t) -> p c t", t=64),
                                   sc_[:, 8:16].rearrange("p (c o) -> p c o", o=1).to_broadcast([128, 8, 64]), ALU.mult,
                                   r=[tk.u(), sc_.u()], w=[kh_.u()])
                                stt(mixT[:, (4 + h) * T + tb * TB:(4 + h) * T + (tb + 1) * TB], tq[:, :], QSCALE, tB[:, :], ALU.mult, ALU.mult, r=[tq.u(), tB.u()], w=[mixT.u((4 + h, tb))])
                                if tb == NTB - 1:
                                    cpy(xs_s[:, 128:129], tB[:, TB - 1:TB], r=[tB.u()], w=[xs_s.u("d")])
                                cpy(v_[:, :], bv[:, :], r=[bv.u()], w=[v_.u()])

                            def stageB(tb, h=h):
                                g_ = h * NTB + tb
                                q_, k_, v_, sc_ = qt[g_ % 3], kt[g_ % 2], vtok[g_ % 3], scal[g_ % 2]
                                PT_, Stl_ = PT[g_ % 2], Stl[g_ % 2]
                                if tb == 0:
                                    S.op("dve", lambda e: e.memset(Sst2[0][:, :], 0.0), w=[Sst2[0].u()])
                                kh_ = kh[g_ % 2]
                                for j in range(4):
                                    S.op("pe", lambda e, j=j, kh_=kh_: e.transpose(ptr[:, j * 128:(j + 1) * 128], kh_[:, j * 128:(j + 1) * 128], ident_bf),
                                         r=[kh_.u(), cb.u()], w=[ptr.u()], signal=(j == 3))
                                tsc(ktokA[:, :], ptr[:, 0:TB], pcol(P_MA), None, ALU.mult, None, r=[ptr.u(), par.u()], w=[ktokA.u()])
                                tsc(ktokB[:, :], ptr[:, 0:TB], pcol(P_MB), None, ALU.mult, None, r=[ptr.u(), par.u()], w=[ktokB.u()])
                                bs = pbank()
                                for j in range(4):
                                    mm(bs[:, j * 128:(j + 1) * 128], k_[:, j * 128:(j + 1) * 128], q_[:, j * 128:(j + 1) * 128], True, True,
                                       r=[k_.u(), q_.u()], w=[bs.u()], signal=(j == 3))
                                tt(PT_[:, :], bs[:, :], cb[:, C_MASK:C_MASK + TB], ALU.mult, r=[bs.u(), cb.u()], w=[PT_.u()])
                                bu = [pbank(), pbank()]
                                for c in range(8):
                                    jj = c // 2
                                    b_ = bu[c // 4]
                                    kx = ktokB if c % 2 else ktokA
                                    mm(b_[:, (c % 4) * 128:(c % 4 + 1) * 128], kx[:, jj * 128:(jj + 1) * 128], v_[:, jj * 128:(jj + 1) * 128],
                                       True, True, r=[kx.u(), v_.u()], w=[b_.u()], signal=(c % 4 == 3))
                                for c in range(8):
                                    b_ = bu[c // 4]
                                    Sa, Sb = Sst2[c % 2], Sst2[(c + 1) % 2]
                                    stt(Sb[:, :], Sa[:, :], sc_[:, c:c + 1], b_[:, (c % 4) * 128:(c % 4 + 1) * 128], ALU.mult, ALU.add,
                                        r=[Sa.u(), sc_.u(), b_.u()], w=[Sb.u()])
                                    act(Stl_[:, c * 128:(c + 1) * 128], Sa[:, :], AF.Copy, r=[Sa.u(), sc_.u()], w=[Stl_.u(c)], scale=sc_[:, 16 + c:17 + c])
                                if tb == NTB - 1:
                                    cpy(xs_s[:, 0:128], Sst2[0][:, :], r=[Sst2[0].u()], w=[xs_s.u("s")])
                                    stages[("ag", h)] = exchange(xs_s, 132, "s", [xs_s.u("d"), xs_s.u("s")])

                            def stageC(tb, h=h):
                                g_ = h * NTB + tb
                                oloc = oloc2[h % 2]
                                q_, v_ = qt[g_ % 3], vtok[g_ % 3]
                                PT_, Stl_ = PT[g_ % 2], Stl[g_ % 2]
                                bo = pbank()
                                for c in range(8):
                                    jj = c // 2
                                    mm(bo[:, c * 64:(c + 1) * 64], v_[:, jj * 128:(jj + 1) * 128], PT_[:, c * 64:(c + 1) * 64], True, False,
                                       r=[v_.u(), PT_.u()], w=[bo.u()], signal=False)
                                    mm(bo[:, c * 64:(c + 1) * 64], Stl_[:, c * 128:(c + 1) * 128], q_[:, c * 64:(c + 1) * 64], False, True,
                                       r=[Stl_.u(c), q_.u()], w=[bo.u()], signal=(c == 7))
                                cpy(oloc[:, tb * TB:(tb + 1) * TB], bo[:, :], r=[bo.u()], w=[oloc.u(tb)])

                            def epilogue(h=h):
                                oloc = oloc2[h % 2]
                                ag, go = stages[("ag", h)]
                                exchange_recv(ag, go, xg_s, 132, 3)
                                G = lambda i, a, b: xg_s[:, 132 * i + a:132 * i + b]
                                ru = [xg_s.u(), par.u()]
                                tsc(acc[:, :], G(0, 0, 128), pcol(P_MLT + 0), None, ALU.mult, None, r=ru, w=[acc.u()])
                                for i in (1, 2):
                                    tsc(avec[:, i:i + 1], G(i, 128, 129), pcol(P_MLT + i), pcol(P_OMM + i), ALU.mult, ALU.add, r=ru, w=[avec.u()])
                                    tsc(acc[:, :], acc[:, :], avec[:, i:i + 1], None, ALU.mult, None, r=[acc.u(), avec.u()], w=[acc.u()])
                                    stt(acc[:, :], G(i, 0, 128), pcol(P_MLT + i), acc[:, :], ALU.mult, ALU.add, r=ru + [acc.u()], w=[acc.u()])
                                cpy(Sin[:, :], acc[:, :], r=[acc.u()], w=[Sin.u()])
                                for tb0 in (0, 2):
                                    tbs = (tb0, tb0 + 1)
                                    bcs, os_ap, sqs, rss = {}, {}, {}, {}
                                    for tb in tbs:
                                        bcs[tb] = pbank()
                                        mm(bcs[tb][:, :], Sin[:, :], mixT[:, (4 + h) * T + tb * TB:(4 + h) * T + (tb + 1) * TB], True, True,
                                           r=[Sin.u(), mixT.u((4 + h, tb))], w=[bcs[tb].u()], signal=True)
                                    for tb in tbs:
                                        os_ap[tb] = oloc[:, tb * TB:(tb + 1) * TB]
                                        tt(os_ap[tb], os_ap[tb], bcs[tb][:, :], ALU.add, r=[oloc.u(tb), bcs[tb].u()], w=[oloc.u(tb)])
                                    for tb in tbs:
                                        sq = sqr[st["sq"]]
                                        st["sq"] = (st["sq"] + 1) % 2
                                        sqs[tb] = sq
                                        act(sq[:, :], os_ap[tb], AF.Square, r=[oloc.u(tb)], w=[sq.u()])
                                    for tb in tbs:
                                        pn = pnorm[tb % 2]
                                        mm(pn[:, :], ones_bf, sqs[tb][:, :], True, True, r=[sqs[tb].u(), cb.u()], w=[pn.u()], signal=True)
                                    for tb in tbs:
                                        rss[tb] = finish_rstd(pnorm[tb % 2], 1.0 / 128)
                                    for tb in tbs:
                                        stt(os_ap[tb], os_ap[tb], pcol(P_NG + 4 * lp + h), rss[tb][:, :], ALU.mult, ALU.mult,
                                            r=[oloc.u(tb), rss[tb].u(), par.u()], w=[oloc.u(tb)])
                                    for tb in tbs:
                                        mc = h * T + tb * TB
                                        tt(mixT[:, mc:mc + TB], os_ap[tb], mixT[:, mc:mc + TB], ALU.mult, r=[oloc.u(tb), mixT.u((h, tb))], w=[mixT.u((h, tb))])
                            return stageA, stageB, stageC, epilogue

                        NG = NH * NTB
                        hfun = {}
                        for s_ in range(NG + 2):
                            LA, LB, LC = [], [], []
                            if s_ < NG:
                                if s_ % NTB == 0:
                                    hfun[s_ // NTB] = head_setup(s_ // NTB)
                                LA = S.capture(lambda: hfun[s_ // NTB][0](s_ % NTB))
                            if 0 <= s_ - 1 < NG:
                                LB = S.capture(lambda: hfun[(s_ - 1) // NTB][1]((s_ - 1) % NTB))
                            if 0 <= s_ - 2 < NG:
                                LC = S.capture(lambda: hfun[(s_ - 2) // NTB][2]((s_ - 2) % NTB))
                            for e_, th_ in LA:
                                if e_ == "pe":
                                    th_()
                            ep = [x for x in LA if x[0] != "pe"]
                            i_, j_ = 0, 0
                            while i_ < len(ep) or j_ < len(LB):
                                if i_ < len(ep):
                                    ep[i_][1]()
                                    i_ += 1
                                if j_ < len(LB):
                                    LB[j_][1]()
                                    j_ += 1
                            for _, th_ in LC:
                                th_()
                            if s_ >= 6 and (s_ - 6) % NTB == 0 and (s_ - 6) // NTB < NH - 1:
                                hfun[(s_ - 6) // NTB][3]()
                        hfun[NH - 1][3]()
                    S.barrier()
                    bp(l, "hgrn")
                    with ExitStack() as cs:
                        chf = sb("chf", [128, T + 2], F32, cs)
                        cacc = sb("cacc", [128, T], F32, cs)
                        Bs = sb("Bs", [128, T], F32, cs)
                        cC = [sb("cC%d" % i, [128, TB], F32, cs) for i in range(2)]
                        for j in range(4):
                            wB = load_w(w_in[(l * 28 + 16 + j) * 128:(l * 28 + 17 + j) * 128, :])
                            wC = load_w(w_in[(l * 28 + 20 + j) * 128:(l * 28 + 21 + j) * 128, :])
                            wH = load_w(w_in[(l * 28 + 24 + j) * 128:(l * 28 + 25 + j) * 128, :])
                            bk = pbank()
                            inproj_halo(wC, bk, 0)
                            bk2 = pbank()
                            inproj_halo(wH, bk2, 0)
                            act(cC[0][:, 0:2], bk[:, 0:2], AF.Copy, r=[bk.u()], w=[cC[0].u()])
                            tt(chf[:, 0:2], cC[0][:, 0:2], bk2[:, 0:2], ALU.mult, r=[cC[0].u(), bk2.u()], w=[chf.u("halo")])
                            for tb in range(NTB):
                                bC, bH, bB = pbank(), pbank(), pbank()
                                inproj_fm(wC, tb, bC)
                                inproj_fm(wH, tb, bH)
                                inproj_fm(wB, tb, bB)
                                c = cC[tb % 2]
                                act(c[:, :], bC[:, :], AF.Copy, r=[bC.u()], w=[c.u()])
                                tt(chf[:, 2 + tb * TB:2 + (tb + 1) * TB], c[:, :], bH[:, :], ALU.mult, r=[c.u(), bH.u()], w=[chf.u(tb)])
                                act(Bs[:, tb * TB:(tb + 1) * TB], bB[:, :], AF.Copy, r=[bB.u()], w=[Bs.u(tb)])
                            for tb in range(NTB):
                                rd = [chf.u(tb), par.u()] + ([chf.u(tb - 1)] if tb else [chf.u("halo")])
                                o = cacc[:, tb * TB:(tb + 1) * TB]
                                b0 = tb * TB
                                tsc(o, chf[:, b0 + 2:b0 + 2 + TB], pcol(P_CW + lp * 12 + 8 + j), None, ALU.mult, None, r=rd, w=[cacc.u(tb)])
                                stt(o, chf[:, b0 + 1:b0 + 1 + TB], pcol(P_CW + lp * 12 + 4 + j), o, ALU.mult, ALU.add, r=rd + [cacc.u(tb)], w=[cacc.u(tb)])
                                stt(o, chf[:, b0:b0 + TB], pcol(P_CW + lp * 12 + j), o, ALU.mult, ALU.add, r=rd + [cacc.u(tb)], w=[cacc.u(tb)])
                                mc = (4 + j) * T + tb * TB
                                tt(mixT[:, mc:mc + TB], o, Bs[:, tb * TB:(tb + 1) * TB], ALU.mult, r=[cacc.u(tb), Bs.u(tb)], w=[mixT.u((4 + j, tb))])
                    S.barrier()
                    bp(l, "conv")
                    hts.close()
                    with ExitStack() as os_:
                        mo2 = [sb("mo%d" % i, [128, 8 * 1024], F32, os_) for i in range(2)]
                        tres = sb("tres", [128, TB], F32, os_)

                        def op_mm(th):
                            mo = mo2[th]
                            for i in range(8):
                                wo = load_w(w_out[(l * 8 + i) * 128:(l * 8 + i + 1) * 128, :])
                                for t2 in range(2):
                                    tb = th * 2 + t2
                                    bk = pbank()
                                    for k in range(8):
                                        mm(bk[:, :], wo[:, k * 128:(k + 1) * 128], mixT[:, k * T + tb * TB:k * T + (tb + 1) * TB], k == 0, k == 7,
                                           r=[wo.u(), mixT.u((k, tb))], w=[bk.u()], signal=(k == 7))
                                    mo_ap = mo[:, i * 1024 + t2 * TB:i * 1024 + (t2 + 1) * TB]
                                    act(mo_ap, bk[:, :], AF.Copy, r=[bk.u()], w=[mo.u((i, t2))])
                                    sq = sqr[st["sq"]]
                                    st["sq"] = (st["sq"] + 1) % 2
                                    act(sq[:, :], bk[:, :], AF.Square, r=[bk.u()], w=[sq.u()])
                                    mm(pnorm[t2][:, :], ones_bf, sq[:, :], i == 0, i == 7, r=[sq.u(), cb.u()], w=[pnorm[t2].u()], signal=True)
                            return [finish_rstd(pnorm[t2], 1.0 / D) for t2 in range(2)]

                        def op_res(th, rss):
                            mo = mo2[th]
                            for t2 in range(2):
                                tb = th * 2 + t2
                                rs = rss[t2]
                                for i in range(8):
                                    mo_ap = mo[:, i * 1024 + t2 * TB:i * 1024 + (t2 + 1) * TB]
                                    stt(tres[:, :], mo_ap, pcol(P_GQM + 8 * lp + i), rs[:, :], ALU.mult, ALU.mult, r=[mo.u((i, t2)), rs.u(), par.u()], w=[tres.u()])
                                    xa = xT[:, i * T + tb * TB:i * T + (tb + 1) * TB]
                                    tt(xa, xa, tres[:, :], ALU.add, r=[xT.u((i, tb)), tres.u()], w=[xT.u((i, tb))])

                        rss1 = op_mm(1)
                        op_res(1, rss1)
                        rss0 = op_mm(0)
                        pn = pnorm[1]
                        for k in range(8):
                            sq = sqr[st["sq"]]
                            st["sq"] = (st["sq"] + 1) % 2
                            act(sq[:, 0:2], xT[:, k * T + T - 2:k * T + T], AF.Square, r=[xT.u((k, NTB - 1))], w=[sq.u()])
                            mm(pn[:, 0:2], ones_bf, sq[:, 0:2], k == 0, k == 7, r=[sq.u(), cb.u()], w=[pn.u()], signal=True)
                        act(lnv[0][:, 0:2], pn[:, 0:2], AF.Ln, r=[pn.u()], w=[lnv[0].u()], scale=1.0 / D, bias=EPS)
                        act(lnv[0][:, 0:2], lnv[0][:, 0:2], AF.Exp, r=[lnv[0].u()], w=[lnv[0].u()], scale=-0.5)
                        for k in range(8):
                            stt(xs_h[:, 2 * k:2 * k + 2], xT[:, k * T + T - 2:k * T + T], pcol(P_GPF + 8 * lp + k), lnv[0][:, 0:2], ALU.mult, ALU.mult,
                                r=[xT.u((k, NTB - 1)), lnv[0].u(), par.u()], w=[xs_h.u()])
                        ffn_ag = exchange(xs_h, 16, "hf")
                        op_res(0, rss0)
                    S.barrier()
            S.barrier()
            bp(l, "mix")
            HH = 1024 + 2
            with ExitStack() as fs:
                hF = sb("hF", [128, 8 * HH], BF16, fs)
                ug = sb("ug", [128, 1024 + 2], F32, fs)
                uv = sb("uv", [128, 1024 + 2], F32, fs)
                cg = [sb("cg%d" % i, [128, TB], F32, fs) for i in range(2)]
                cv = [sb("cv%d" % i, [128, TB], F32, fs) for i in range(2)]
                car = sb("car", [128, 4 * NFC], F32, fs)
                aT = sb("aT", [128, NFC * 1024], BF16, fs)
                yT = sb("yT", [128, 8 * 1024], F32, fs)
                tres = sb("tres2", [128, TB], F32, fs)
                ag, go = ffn_ag
                exchange_recv(ag, go, xg_h, 16)
                tsc(hal[:, :], xg_h[:, 0:16], pcol(P_SEL + 0), None, ALU.mult, None, r=[xg_h.u(), par.u()], w=[hal.u()])
                for i in range(1, 4):
                    stt(hal[:, :], xg_h[:, 16 * i:16 * i + 16], pcol(P_SEL + i), hal[:, :], ALU.mult, ALU.add,
                        r=[xg_h.u(), hal.u(), par.u()], w=[hal.u()])
                for th in range(2):
                    if th == 0:
                        for k in range(8):
                            cpy(hF[:, k * HH:k * HH + 2], hal[:, 2 * k:2 * k + 2], r=[hal.u()], w=[hF.u((k, "halo"))])
                    prenorm(hF, HH, P_GPF, [2 * th, 2 * th + 1], 2 * th)
                    for j in range(NFC):
                        wG = load_w(w_up[(l * 44 + j) * 128:(l * 44 + j + 1) * 128, :])
                        wV = load_w(w_up[(l * 44 + 22 + j) * 128:(l * 44 + 23 + j) * 128, :])
                        fo = P_FCW + lp * 132
                        halves = ((wG, ug, 0, j), (wV, uv, 2 * NFC, 22 + j))
                        for (wsl, uF, cofs, fofs) in halves:
                            if th == 0:
                                bh = pbank()
                                for k in range(8):
                                    mm(bh[:, 0:2], wsl[:, k * 128:(k + 1) * 128], hF[:, k * HH:k * HH + 2], k == 0, k == 7,
                                       r=[wsl.u(), hF.u((k, "halo"))], w=[bh.u()], signal=(k == 7))
                                act(uF[:, 0:2], bh[:, 0:2], AF.Copy, r=[bh.u()], w=[uF.u("halo")])
                            else:
                                cpy(uF[:, 0:2], car[:, cofs + 2 * j:cofs + 2 * j + 2], r=[car.u((cofs, j))], w=[uF.u("halo")])
                        for t2 in range(2):
                            cs = (cg[t2], cv[t2])
                            for hi, (wsl, uF, cofs, fofs) in enumerate(halves):
                                bk = pbank()
                                for k in range(8):
                                    c0 = k * HH + 2 + t2 * TB
                                    mm(bk[:, :], wsl[:, k * 128:(k + 1) * 128], hF[:, c0:c0 + TB], k == 0, k == 7,
                                       r=[wsl.u(), hF.u((k, t2))], w=[bk.u()], signal=(k == 7))
                                act(uF[:, 2 + t2 * TB:2 + (t2 + 1) * TB], bk[:, :], AF.Copy, r=[bk.u()], w=[uF.u(t2)])
                                act(cs[hi][:, :], bk[:, :], AF.Copy, r=[bk.u(), par.u()], w=[cs[hi].u()], scale=pcol(fo + 88 + fofs))
                            for tap, sh in ((1, 1), (0, 0)):
                                for hi, (wsl, uF, cofs, fofs) in enumerate(halves):
                                    c_ = cs[hi]
                                    rd = [uF.u(t2), uF.u(t2 - 1) if t2 else uF.u("halo"), par.u(), c_.u()]
                                    stt(c_[:, :], uF[:, sh + t2 * TB:sh + (t2 + 1) * TB], pcol(fo + tap * 44 + fofs), c_[:, :], ALU.mult, ALU.add, r=rd, w=[c_.u()])
                            act(cs[0][:, :], cs[0][:, :], AF.Silu, r=[cs[0].u()], w=[cs[0].u()])
                            ac = j * 1024 + t2 * TB
                            tt(aT[:, ac:ac + TB], cs[0][:, :], cs[1][:, :], ALU.mult, r=[cs[0].u(), cs[1].u()], w=[aT.u((j, t2))])
                        if th == 0:
                            for (wsl, uF, cofs, fofs) in halves:
                                cpy(car[:, cofs + 2 * j:cofs + 2 * j + 2], uF[:, 1024:1026], r=[uF.u(1)], w=[car.u((cofs, j))])
                    for i in range(8):
                        wd = [load_w(w_dn[(l * 8 + i) * 128:(l * 8 + i + 1) * 128, 0:1024]),
                              load_w(w_dn[(l * 8 + i) * 128:(l * 8 + i + 1) * 128, 1024:2048]),
                              load_w(w_dn[(l * 8 + i) * 128:(l * 8 + i + 1) * 128, 2048:2816])]
                        for t2 in range(2):
                            bk = pbank()
                            for k in range(NFC):
                                wsl = wd[k // 8]
                                kk = k % 8
                                mm(bk[:, :], wsl[:, kk * 128:(kk + 1) * 128], aT[:, k * 1024 + t2 * TB:k * 1024 + (t2 + 1) * TB], k == 0, k == NFC - 1,
                                   r=[wsl.u(), aT.u((k, t2))], w=[bk.u()], signal=(k == NFC - 1))
                            y_ap = yT[:, i * 1024 + t2 * TB:i * 1024 + (t2 + 1) * TB]
                            act(y_ap, bk[:, :], AF.Copy, r=[bk.u()], w=[yT.u((i, t2))])
                            sq = sqr[st["sq"]]
                            st["sq"] = (st["sq"] + 1) % 2
                            act(sq[:, :], bk[:, :], AF.Square, r=[bk.u()], w=[sq.u()])
                            mm(pnorm[t2][:, :], ones_bf, sq[:, :], i == 0, i == 7, r=[sq.u(), cb.u()], w=[pnorm[t2].u()], signal=True)
                    for t2 in range(2):
                        tb = th * 2 + t2
                        rs = finish_rstd(pnorm[t2], 1.0 / D)
                        for i in range(8):
                            y_ap = yT[:, i * 1024 + t2 * TB:i * 1024 + (t2 + 1) * TB]
                            stt(tres[:, :], y_ap, pcol(P_GQF + 8 * lp + i), rs[:, :], ALU.mult, ALU.mult, r=[yT.u((i, t2)), rs.u(), par.u()], w=[tres.u()])
                            xa = xT[:, i * T + tb * TB:i * T + (tb + 1) * TB]
                            tt(xa, xa, tres[:, :], ALU.add, r=[xT.u((i, tb)), tres.u()], w=[xT.u((i, tb))])
            S.barrier()
            bp(l, "ffn")

          except _Stop:
            break
        S.enabled = True
        for _i in range(int(os.environ.get("K_DUMMY_PE", "0"))):
            mm(pring[0][:, 0:2], ones_bf, cb[:, 0:2], True, True, r=[cb.u()], w=[pring[0].u()], signal=(_i % 64 == 63))
        for _i in range(int(os.environ.get("K_DUMMY_ACT", "0"))):
            act(lnv[0][:, 0:8], lnv[0][:, 0:8], AF.Copy, r=[lnv[0].u()], w=[lnv[0].u()])
        for _i in range(int(os.environ.get("K_DUMMY_DVE", "0"))):
            cpy(lnv[1][:, 0:8], lnv[1][:, 8:16], r=[lnv[1].u()], w=[lnv[1].u()])
        for _i in range(int(os.environ.get("K_DUMMY_SLOW", "0"))):
            act(xT[:, 0:2048], xT[:, 0:2048], AF.Copy, r=[xT.u((0, 0)), xT.u((0, 1)), xT.u((0, 2)), xT.u((0, 3))],
                w=[xT.u((0, 0)), xT.u((0, 1)), xT.u((0, 2)), xT.u((0, 3))])
        if os.environ.get("K_DUMMY_HI"):
            for _i in range(nlayers * 44 - 48, nlayers * 44):
                load_w(w_up[_i * 128:(_i + 1) * 128, :])
            for _i in range(nlayers * 8 - 8, nlayers * 8):
                load_w(w_dn[_i * 128:(_i + 1) * 128, 0:1024])
                load_w(w_dn[_i * 128:(_i + 1) * 128, 2048:2816])
        for _i in range(int(os.environ.get("K_DUMMY_DMA", "0"))):
            load_w(w_up[(_i % 40) * 128:(_i % 40 + 1) * 128, :])
        for _i in range(int(os.environ.get("K_DUMMY_CC", "0"))):
            ag_, go_ = exchange(xs_h, 16, "dummy")
            exchange_recv(ag_, go_, xg_h, 16)
        sigs = []
        for k in range(8):
            for tb in range(NTB):
                c0 = k * T + tb * TB
                sigs.append(S.dma("sp", lambda e, c0=c0: e.dma_start(out=out[:, c0:c0 + TB], in_=xT[:, c0:c0 + TB]), r=[xT.u((k, tb))]))
        for key, val in sigs:
            S._wait("sp", key, val)
        S.emit()
        print("kernel: recorded ops", S.nops, {e: len(S.q[e]) for e in S.ENG}, "sigcnt", S.cnt, "dma", [c for _, c in S.dsem][:4], "cc", S.cccnt)
    return nc


def _prep_inputs(x, lb_param, w_in, w_out, conv_w, ffn_w_up, ffn_conv_w, ffn_w_down,
                 hgrn_norm_g, pre_mix_g, post_mix_g, pre_ffn_g, post_ffn_g):
    f = np.float32

    def wl(w, nk):
        L, K, N = w.shape
        nj = N // 128
        a = np.asarray(w, f).reshape(L, nk, 128, nj, 128).transpose(0, 3, 2, 1, 4)
        return np.ascontiguousarray(a).reshape(L * nj * 128, nk * 128)

    w_in_r = wl(w_in, 8)
    w_out_r = wl(w_out, 8)
    w_up_r = wl(ffn_w_up, 8)
    w_dn_r = wl(ffn_w_down, NFC)

    def pk(a, n):
        a = np.asarray(a, f).reshape(DEPTH, n, 128).transpose(2, 0, 1)
        return np.ascontiguousarray(a).reshape(128, DEPTH * n)

    def pk3(a, n):
        a = np.asarray(a, f).reshape(DEPTH, 3, n, 128).transpose(3, 0, 1, 2)
        return np.ascontiguousarray(a).reshape(128, DEPTH * 3 * n)

    base = np.concatenate([pk(lb_param, 4), pk(hgrn_norm_g, 4), pk(pre_mix_g, 8), pk(post_mix_g, 8), pk(pre_ffn_g, 8),
                           pk(post_ffn_g, 8), pk3(conv_w, 4), pk3(ffn_conv_w, 44)], axis=1)
    assert base.shape[1] == P_SEL
    cbm = np.zeros((128, NCB), np.float32)
    cbm[:, C_ONES:C_ONES + 128] = 1.0
    cbm[:, C_ID:C_ID + 128] = np.eye(128, dtype=np.float32)
    s = np.arange(128)[:, None]
    t = np.arange(128)[None, :]
    m2 = ((s // 64) == (t // 64)) & (s <= t)
    cbm[:, C_MASK:C_MASK + 512] = np.tile(m2.astype(np.float32), (1, 4))
    rst = np.ones((128, 512), np.float32)
    rst[:, ::64] = 0.0
    cbm[:, C_RST:C_RST + 512] = rst
    cbm[:, C_ONE:C_ONE + 512] = 1.0
    cbm = cbm.astype(ml_dtypes.bfloat16)

    xs = np.asarray(x, f)
    in_maps = []
    for c in range(8):
        b, sgm = c // 4, c % 4
        xc = xs[b, sgm * T:(sgm + 1) * T, :]
        xc = np.ascontiguousarray(xc.T.reshape(8, 128, T).transpose(1, 0, 2)).reshape(128, 8 * T)
        extra = np.zeros((128, 14), np.float32)
        extra[:64, 12] = 1.0
        extra[64:, 13] = 1.0
        for i in range(4):
            extra[:, i] = 1.0 if i == sgm - 1 else 0.0
            extra[:, 4 + i] = 1.0 if i < sgm else 0.0
            extra[:, 8 + i] = 0.0 if i < sgm else 1.0
        par = np.ascontiguousarray(np.concatenate([base, extra], axis=1))
        in_maps.append({"xin": xc, "w_in": w_in_r, "w_out": w_out_r, "w_up": w_up_r, "w_dn": w_dn_r, "par": par, "cb": cbm})
    return in_maps


_NC_CACHE = {}
SPLIT = [(0, 4)]


def _to_fm(xc):
    return np.ascontiguousarray(xc.T.reshape(8, 128, T).transpose(1, 0, 2)).reshape(128, 8 * T)


def kernel(x, lb_param, w_in, w_out, conv_w, ffn_w_up, ffn_conv_w, ffn_w_down,
           hgrn_norm_g, pre_mix_g, post_mix_g, pre_ffn_g, post_ffn_g, _dbg=None, _nlayers=None):
    in_maps = _prep_inputs(x, lb_param, w_in, w_out, conv_w, ffn_w_up, ffn_conv_w, ffn_w_down,
                           hgrn_norm_g, pre_mix_g, post_mix_g, pre_ffn_g, post_ffn_g)
    split = SPLIT if _nlayers is None else [(0, _nlayers)]
    full = {k: in_maps[0][k] for k in ("w_in", "w_out", "w_up", "w_dn")}
    res = None
    for (l0, l1) in split:
        nl = l1 - l0
        key = (_dbg, nl, l0)
        if key not in _NC_CACHE:
            _NC_CACHE[key] = build_program(nl, _dbg, l0)
        nc = _NC_CACHE[key]
        for c, m in enumerate(in_maps):
            m["w_in"] = full["w_in"][l0 * 28 * 128:l1 * 28 * 128]
            m["w_out"] = full["w_out"][l0 * 8 * 128:l1 * 8 * 128]
            m["w_up"] = full["w_up"][l0 * 44 * 128:l1 * 44 * 128]
            m["w_dn"] = full["w_dn"][l0 * 8 * 128:l1 * 8 * 128]
            if res is not None:
                m["xin"] = np.asarray(res.results[c]["out"])
        res = run_bass_kernel_spmd(nc, in_maps, core_ids=list(range(8)))
    outp = np.empty((2, 4 * T, D), np.float32)
    for c in range(8):
        b, sgm = c // 4, c % 4
        o = np.asarray(res.results[c]["out"]).reshape(128, 8, T).transpose(2, 1, 0).reshape(T, D)
        outp[b, sgm * T:(sgm + 1) * T, :] = o
    return outp
```

```python
import os
from contextlib import ExitStack

import numpy as np
import ml_dtypes

import concourse.bass as bass
import concourse.mybir as mybir
from concourse.bass_utils import run_bass_kernel_spmd

F32 = mybir.dt.float32
BF16 = mybir.dt.bfloat16
AF = mybir.ActivationFunctionType
ALU = mybir.AluOpType

D = 1024
T = 2048
TB = 512
NTB = T // TB
DEPTH = 4
NH = 4
DFF = 2816
NFC = DFF // 128
EPS = 1e-6
QSCALE = 128 ** -0.5

P_LBP = 0
P_NG = P_LBP + 16
P_GPM = P_NG + 16
P_GQM = P_GPM + 32
P_GPF = P_GQM + 32
P_GQF = P_GPF + 32
P_CW = P_GQF + 32
P_FCW = P_CW + 48
P_SEL = P_FCW + 528
P_MLT = P_SEL + 4
P_OMM = P_MLT + 4
P_MA = P_OMM + 4
P_MB = P_MA + 1
NPAR = P_MB + 1
C_ONES = 0
C_ID = 128
C_MASK = 256
C_RST = 768
C_ONE = 1280
NCB = 1792

NSLOT = 6


class TT:
    __slots__ = ("w", "r")

    def __init__(self):
        self.w = None
        self.r = {}


class Tile:
    def __init__(self, t):
        self.t = t
        self.units = {}

    def u(self, key=0):
        x = self.units.get(key)
        if x is None:
            x = self.units[key] = TT()
        return x

    def __getitem__(self, idx):
        return self.t[idx]


class Sched:
    ENG = ("pe", "act", "dve", "pool", "sp")

    def __init__(self, nc, es):
        self.nc = nc
        self.q = {e: [] for e in self.ENG}
        self.cnt = {e: 0 for e in self.ENG}
        self.waited = {e: {} for e in self.ENG}
        self.sem = {e: es.enter_context(nc.semaphore("c_" + e)) for e in self.ENG}
        self.dsem = [[es.enter_context(nc.semaphore("d%d" % i)), 0] for i in range(24)]
        self.drr = 0
        self.es = es
        self.cccnt = 0
        self.semh = dict(self.sem)
        for i, (h, _) in enumerate(self.dsem):
            self.semh["d%d" % i] = h
        self.nops = 0
        self.enabled = True
        self.seen = {}
        self.cap = None

    def _wait(self, eng, key, val):
        if eng == "pe" and key == "pe":
            return
        if key in self.cnt:
            assert val <= self.cnt[key], ("wait on future signal", eng, key, val, self.cnt[key])
        if self.waited[eng].get(key, 0) >= val:
            return
        self.waited[eng][key] = val
        if self.seen.get(key, 0) < val:
            self.seen[key] = val
        self.q[eng].append(("w", key, val))

    def _deps(self, eng, reads, writes):
        for t in reads:
            if t.w is not None:
                self._wait(eng, *t.w)
        for t in writes:
            if t.w is not None:
                self._wait(eng, *t.w)
            for k, v in t.r.items():
                self._wait(eng, k, v)

    def _mark(self, sig, reads, writes):
        k, v = sig
        for t in reads:
            if t.r.get(k, 0) < v:
                t.r[k] = v
        for t in writes:
            t.w = sig
            t.r = {}

    def capture(self, f):
        assert self.cap is None
        self.cap = []
        try:
            f()
        finally:
            lst, self.cap = self.cap, None
        return lst

    def op(self, eng, fn, r=(), w=(), signal=True):
        if not self.enabled:
            return
        if self.cap is not None:
            self.cap.append((eng, lambda: self.op(eng, fn, r, w, signal)))
            return
        self._deps(eng, r, w)
        self.nops += 1
        if signal:
            self.cnt[eng] += 1
            sig = (eng, self.cnt[eng])
            self.q[eng].append(("i", fn, eng, 1))
        else:
            sig = (eng, self.cnt[eng] + 1)
            self.q[eng].append(("i", fn, None, 0))
        self._mark(sig, r, w)

    def dma(self, eng, fn, r=(), w=()):
        if not self.enabled:
            return None
        if self.cap is not None:
            self.cap.append((eng, lambda: self.dma(eng, fn, r, w)))
            return None
        self._deps(eng, r, w)
        i = self.drr
        self.drr = (self.drr + 1) % len(self.dsem)
        key = "d%d" % i
        prev = self.dsem[i][1]
        if prev and self.seen.get(key, 0) < prev:
            self._wait(eng, key, prev)
        self.dsem[i][1] = prev + 16
        sig = (key, prev + 16)
        self.q[eng].append(("i", fn, key, 16))
        self._mark(sig, r, w)
        return sig

    def collective(self, fn, r=(), w=()):
        if not self.enabled:
            return
        if self.cap is not None:
            self.cap.append(("pool", lambda: self.collective(fn, r, w)))
            return
        self._deps("pool", r, w)
        self.cccnt += 1
        key = "cc%d" % self.cccnt
        self.semh[key] = self.es.enter_context(self.nc.semaphore(key))
        sig = (key, 1)
        self.q["pool"].append(("i", fn, key, 1))
        self._mark(sig, r, w)

    def barrier(self):
        if not self.enabled:
            return
        cur = [(e, self.cnt[e]) for e in self.ENG if self.cnt[e]]
        cur += [("d%d" % i, c) for i, (_, c) in enumerate(self.dsem) if c]
        cur += [("cc%d" % i, 1) for i in range(1, self.cccnt + 1)]
        for e in self.ENG:
            for k, v in cur:
                if k == e:
                    continue
                self._wait(e, k, v)

    def emit(self):
        nc = self.nc
        engs = {"pe": "tensor", "act": "scalar", "dve": "vector", "pool": "gpsimd", "sp": "sync"}
        with nc.Block() as block:
            for e in self.ENG:
                items = self.q[e]

                def body(eng, items=items):
                    for it in items:
                        if it[0] == "w":
                            eng.wait_ge(self.semh[it[1]], it[2])
                        else:
                            ins = it[1](eng)
                            if it[2] is not None:
                                ins.then_inc(self.semh[it[2]], it[3])

                getattr(block, engs[e])(body)


def build_program(nlayers=DEPTH, dbg=None, lbase=0):
    nc = bass.Bass("TRN2", target_bir_lowering=False)
    xin = nc.dram_tensor("xin", [128, 8 * T], F32, kind="ExternalInput").ap()
    w_in = nc.dram_tensor("w_in", [nlayers * 28 * 128, 1024], F32, kind="ExternalInput").ap()
    w_out = nc.dram_tensor("w_out", [nlayers * 8 * 128, 1024], F32, kind="ExternalInput").ap()
    w_up = nc.dram_tensor("w_up", [nlayers * 44 * 128, 1024], F32, kind="ExternalInput").ap()
    w_dn = nc.dram_tensor("w_dn", [nlayers * 8 * 128, NFC * 128], F32, kind="ExternalInput").ap()
    par_d = nc.dram_tensor("par", [128, NPAR], F32, kind="ExternalInput").ap()
    cb_d = nc.dram_tensor("cb", [128, NCB], BF16, kind="ExternalInput").ap()
    out = nc.dram_tensor("out", [128, 8 * T], F32, kind="ExternalOutput").ap()

    es = ExitStack()
    with es:
        S = Sched(nc, es)

        uid = [0]

        def sb(name, shape, dt, stack=es):
            uid[0] += 1
            return Tile(stack.enter_context(nc.sbuf_tensor("%s_%d" % (name, uid[0]), shape, dt)))

        def ps(name, shape, dt, stack=es):
            uid[0] += 1
            return Tile(stack.enter_context(nc.psum_tensor("%s_%d" % (name, uid[0]), shape, dt)))

        xT = sb("xT", [128, 8 * T], F32)
        par = sb("par", [128, NPAR], F32)
        cb = sb("cb", [128, NCB], BF16)
        lbt = sb("lbt", [128, 64], F32)
        wring = [sb("wr%d" % i, [128, 1024], BF16) for i in range(NSLOT)]
        sqr = [sb("sq%d" % i, [128, TB], BF16) for i in range(2)]
        lnv = [sb("lnv%d" % i, [128, TB], F32) for i in range(1)]
        rstd = [sb("rstd%d" % i, [128, TB], F32) for i in range(2)]
        xs_h = sb("xs_h", [128, 16], F32)
        xg_h = sb("xg_h", [128, 64], F32)
        hal = sb("hal", [128, 16], F32)
        pring = [ps("pb%d" % i, [128, TB], F32) for i in range(5)]
        pnorm = [ps("pn%d" % i, [128, TB], F32) for i in range(2)]
        ptr = ps("ptr", [128, 1024], BF16)
        st = {"wr": 0, "pr": 0, "sq": 0, "ln": 0, "ex": 0}

        ones_bf = cb[:, C_ONES:C_ONES + 128]
        ident_bf = cb[:, C_ID:C_ID + 128]

        def pcol(off):
            return par[:, off:off + 1]

        def wslot():
            i = st["wr"]
            st["wr"] = (i + 1) % NSLOT
            return wring[i]

        def pbank():
            i = st["pr"]
            st["pr"] = (i + 1) % len(pring)
            return pring[i]

        def load_w(src_ap):
            sl = wslot()
            ncol = src_ap.shape[1]
            S.dma("pool", lambda e, sl=sl, src_ap=src_ap, ncol=ncol: e.dma_start(out=sl[:, 0:ncol], in_=src_ap), w=[sl.u()])
            return sl

        def mm(out_ap, lhsT, rhs, start, stop, r, w, signal):
            S.op("pe", lambda e: e.matmul(out_ap, lhsT=lhsT, rhs=rhs, start=start, stop=stop), r=r, w=w, signal=signal)

        def act(out_ap, in_ap, func, r, w, scale=None, bias=None):
            kw = {}
            if scale is not None:
                kw["scale"] = scale
            if bias is not None:
                kw["bias"] = bias
            S.op("act", lambda e: e.activation(out=out_ap, in_=in_ap, func=func, **kw), r=r, w=w)

        def tsc(out_ap, in0, s1, s2, op0, op1, r, w, eng="dve"):
            if s2 is None:
                S.op(eng, lambda e: e.tensor_scalar(out=out_ap, in0=in0, scalar1=s1, scalar2=None, op0=op0), r=r, w=w)
            else:
                S.op(eng, lambda e: e.tensor_scalar(out=out_ap, in0=in0, scalar1=s1, scalar2=s2, op0=op0, op1=op1), r=r, w=w)

        def stt(out_ap, in0, sc, in1, op0, op1, r, w):
            S.op("dve", lambda e: e.scalar_tensor_tensor(out=out_ap, in0=in0, scalar=sc, in1=in1, op0=op0, op1=op1), r=r, w=w)

        def tt(out_ap, in0, in1, op, r, w, eng="dve"):
            S.op(eng, lambda e: e.tensor_tensor(out=out_ap, in0=in0, in1=in1, op=op), r=r, w=w)

        def cpy(out_ap, in_ap, r, w, eng="dve"):
            S.op(eng, lambda e: e.tensor_copy(out_ap, in_ap), r=r, w=w)

        def exchange(src, ncols, tag, runits=None):
            i = st["ex"]
            st["ex"] += 1
            agin = nc.dram_tensor("agin%d" % i, [128, ncols], F32)
            agout = nc.dram_tensor("agout%d" % i, [4 * 128, ncols], F32)
            gi, go = TT(), TT()
            S.dma("sp", lambda e: e.dma_start(out=agin.ap()[:, :], in_=src[:, 0:ncols]), r=(runits or [src.u()]), w=[gi])
            S.collective(lambda e: e.collective_compute("AllGather", ALU.bypass, replica_groups=[[0, 1, 2, 3], [4, 5, 6, 7]],
                                                        ins=[agin.ap().opt()], outs=[agout.ap().opt()]), r=[gi], w=[go])
            return agout, go

        def exchange_recv(agout, go, dst, ncols, nr=4):
            S.dma("sp", lambda e: e.dma_start(out=dst[:, 0:nr * ncols].rearrange("p (r c) -> p r c", r=nr),
                                              in_=agout.ap()[0:nr * 128, :].rearrange("(r p) c -> p r c", p=128)), r=[go], w=[dst.u()])

        def norm_stats(src_fn, nk, tb_key, scale, pn):
            for k in range(nk):
                ap, un = src_fn(k)
                sq = sqr[st["sq"]]
                st["sq"] = (st["sq"] + 1) % 2
                act(sq[:, :], ap, AF.Square, r=un, w=[sq.u()])
                mm(pn[:, :], ones_bf, sq[:, :], k == 0, k == nk - 1, r=[sq.u(), cb.u()], w=[pn.u()], signal=True)
            return finish_rstd(pn, scale)

        def finish_rstd(pn, scale):
            i = st["ln"]
            st["ln"] = (i + 1) % 2
            lv, rs = lnv[0], rstd[i]
            act(lv[:, :], pn[:, :], AF.Ln, r=[pn.u()], w=[lv.u()], scale=scale, bias=EPS)
            act(rs[:, :], lv[:, :], AF.Exp, r=[lv.u()], w=[rs.u()], scale=-0.5)
            return rs

        for k in range(8):
            for tb in range(NTB):
                c0 = k * T + tb * TB
                S.dma("sp", lambda e, c0=c0: e.dma_start(out=xT[:, c0:c0 + TB], in_=xin[:, c0:c0 + TB]), w=[xT.u((k, tb))])
        S.dma("sp", lambda e: e.dma_start(out=par[:, :], in_=par_d[:, :]), w=[par.u()])
        S.dma("sp", lambda e: e.dma_start(out=cb[:, :], in_=cb_d[:, :]), w=[cb.u()])
        L = lambda l: lbt[:, 48 + 0:48 + 4]
        mx = lbt[:, 48:52]
        tt(mx, par[:, P_LBP:P_LBP + 4], par[:, P_LBP + 4:P_LBP + 8], ALU.max, r=[par.u()], w=[lbt.u()])
        tt(mx, mx, par[:, P_LBP + 8:P_LBP + 12], ALU.max, r=[lbt.u()], w=[lbt.u()])
        tt(mx, mx, par[:, P_LBP + 12:P_LBP + 16], ALU.max, r=[lbt.u()], w=[lbt.u()])
        for l in range(4):
            tt(lbt[:, 4 * l:4 * l + 4], par[:, P_LBP + 4 * l:P_LBP + 4 * l + 4], mx, ALU.subtract, r=[par.u(), lbt.u()], w=[lbt.u()])
        act(lbt[:, 0:16], lbt[:, 0:16], AF.Exp, r=[lbt.u()], w=[lbt.u()])
        sm = lbt[:, 52:56]
        tt(sm, lbt[:, 0:4], lbt[:, 4:8], ALU.add, r=[lbt.u()], w=[lbt.u()])
        tt(sm, sm, lbt[:, 8:12], ALU.add, r=[lbt.u()], w=[lbt.u()])
        tt(sm, sm, lbt[:, 12:16], ALU.add, r=[lbt.u()], w=[lbt.u()])
        S.op("dve", lambda e: e.reciprocal(sm, sm), r=[lbt.u()], w=[lbt.u()])
        for l in range(4):
            tt(lbt[:, 4 * l:4 * l + 4], lbt[:, 4 * l:4 * l + 4], sm, ALU.mult, r=[lbt.u()], w=[lbt.u()])
        tt(lbt[:, 8:12], lbt[:, 8:12], lbt[:, 4:8], ALU.add, r=[lbt.u()], w=[lbt.u()])
        tt(lbt[:, 12:16], lbt[:, 12:16], lbt[:, 8:12], ALU.add, r=[lbt.u()], w=[lbt.u()])
        S.op("dve", lambda e: e.memset(lbt[:, 0:4], 0.0), w=[lbt.u()])
        tsc(lbt[:, 16:32], lbt[:, 0:16], -1.0, 1.0, ALU.mult, ALU.add, r=[lbt.u()], w=[lbt.u()])
        tsc(lbt[:, 32:48], lbt[:, 16:32], -1.0, None, ALU.mult, None, r=[lbt.u()], w=[lbt.u()])

        HW = T + 2

        class _Stop(Exception):
            pass

        def bp(l, name):
            if dbg == (l, name):
                S.barrier()
                S.enabled = False

        for l in range(nlayers):
          try:
            lp = l + lbase
            bp(l, "pre")
            with ExitStack() as mixs:
                mixT = sb("mixT", [128, 8 * T], BF16, mixs)
                hts = ExitStack()
                hT = sb("hT", [128, 8 * HW], BF16, hts)

                def prenorm(hT, hw, gofs, tbs, toff):
                    for tb in tbs:
                        rs = norm_stats(lambda k: (xT[:, k * T + tb * TB:k * T + (tb + 1) * TB], [xT.u((k, tb))]), 8, tb, 1.0 / D, pnorm[tb % 2])
                        for k in range(8):
                            c0 = k * hw + 2 + (tb - toff) * TB
                            stt(hT[:, c0:c0 + TB], xT[:, k * T + tb * TB:k * T + (tb + 1) * TB], pcol(gofs + 8 * lp + k), rs[:, :],
                                ALU.mult, ALU.mult, r=[xT.u((k, tb)), rs.u(), par.u()], w=[hT.u((k, tb - toff))])

                def halo_exchange(hT, hw, last_c0, last_key):
                    for k in range(8):
                        cpy(xs_h[:, 2 * k:2 * k + 2], hT[:, k * hw + last_c0:k * hw + last_c0 + 2], r=[hT.u((k, last_key))], w=[xs_h.u()])
                    ag, go = exchange(xs_h, 16, "h")
                    exchange_recv(ag, go, xg_h, 16)
                    tsc(hal[:, :], xg_h[:, 0:16], pcol(P_SEL + 0), None, ALU.mult, None, r=[xg_h.u(), par.u()], w=[hal.u()])
                    for i in range(1, 4):
                        stt(hal[:, :], xg_h[:, 16 * i:16 * i + 16], pcol(P_SEL + i), hal[:, :], ALU.mult, ALU.add,
                            r=[xg_h.u(), hal.u(), par.u()], w=[hal.u()])
                    for k in range(8):
                        cpy(hT[:, k * hw:k * hw + 2], hal[:, 2 * k:2 * k + 2], r=[hal.u()], w=[hT.u((k, "halo"))])

                prenorm(hT, HW, P_GPM, range(NTB), 0)
                bp(l, "norm")
                halo_exchange(hT, HW, 2 + T - 2, NTB - 1)
                bp(l, "halo")

                def hrhs(k, tb):
                    c0 = k * HW + 2 + tb * TB
                    return hT[:, c0:c0 + TB]

                def inproj_fm(wsl, tb, bank):
                    for k in range(8):
                        mm(bank[:, :], wsl[:, k * 128:(k + 1) * 128], hrhs(k, tb), k == 0, k == 7,
                           r=[wsl.u(), hT.u((k, tb))], w=[bank.u()], signal=(k == 7))

                def inproj_halo(wsl, bank, col):
                    for k in range(8):
                        mm(bank[:, col:col + 2], wsl[:, k * 128:(k + 1) * 128], hT[:, k * HW:k * HW + 2], k == 0, k == 7,
                           r=[wsl.u(), hT.u((k, "halo"))], w=[bank.u()], signal=(k == 7))

                with ExitStack() as mxs:
                    with ExitStack() as hs:
                        tq = sb("tq", [128, TB], F32, hs)
                        tsig = sb("tsig", [128, TB], F32, hs)
                        tk = sb("tk", [128, TB], F32, hs)
                        tb_ = sb("tb_", [128, TB], F32, hs)
                        tB = sb("tB", [128, TB], F32, hs)
                        tbm = sb("tbm", [128, TB], F32, hs)
                        tEp = sb("tEp", [128, TB], F32, hs)
                        qt = [sb("qt%d" % i, [128, TB], BF16, hs) for i in range(3)]
                        kt = [sb("kt%d" % i, [128, TB], BF16, hs) for i in range(2)]
                        kh = [sb("kh%d" % i, [128, TB], BF16, hs) for i in range(2)]
                        vtok = [sb("vtok%d" % i, [128, TB], BF16, hs) for i in range(3)]
                        ktokA = sb("ktokA", [128, TB], BF16, hs)
                        ktokB = sb("ktokB", [128, TB], BF16, hs)
                        PT = [sb("PT%d" % i, [128, TB], BF16, hs) for i in range(2)]
                        Stl = [sb("Stl%d" % i, [128, 8 * 128], BF16, hs) for i in range(2)]
                        scal = [sb("scal%d" % i, [128, 24], F32, hs) for i in range(2)]
                        oloc2 = [sb("oloc%d" % i, [128, T], F32, hs) for i in range(2)]
                        Sst2 = [sb("Sst%d" % i, [128, 128], F32, hs) for i in range(2)]
                        Bprev = sb("Bprev", [128, 1], F32, hs)
                        xs_s = sb("xs_s", [128, 132], F32, hs)
                        xg_s = sb("xg_s", [128, 3 * 132], F32, hs)
                        acc = sb("acc", [128, 128], F32, hs)
                        avec = sb("avec", [128, 4], F32, hs)
                        Sin = sb("Sin", [128, 128], BF16, hs)

                        stages = {}
                        def head_setup(h):
                            wq = load_w(w_in[(l * 28 + 0 + h) * 128:(l * 28 + 1 + h) * 128, :])
                            wf = load_w(w_in[(l * 28 + 4 + h) * 128:(l * 28 + 5 + h) * 128, :])
                            wi = load_w(w_in[(l * 28 + 8 + h) * 128:(l * 28 + 9 + h) * 128, :])
                            wg = load_w(w_in[(l * 28 + 12 + h) * 128:(l * 28 + 13 + h) * 128, :])
                            lb_c = lbt[:, 4 * lp + h:4 * lp + h + 1]
                            oml_c = lbt[:, 16 + 4 * lp + h:16 + 4 * lp + h + 1]
                            noml_c = lbt[:, 32 + 4 * lp + h:32 + 4 * lp + h + 1]

                            def stageA(tb, h=h, wq=wq, wf=wf, wi=wi, wg=wg, lb_c=lb_c, oml_c=oml_c, noml_c=noml_c):
                                g_ = h * NTB + tb
                                q_, k_, v_, sc_ = qt[g_ % 3], kt[g_ % 2], vtok[g_ % 3], scal[g_ % 2]
                                if tb == 0:
                                    S.op("dve", lambda e: e.memset(Bprev[:, :], 0.0), w=[Bprev.u()])
                                bf, bq, bg, bv = pbank(), pbank(), pbank(), pbank()
                                inproj_fm(wf, tb, bf)
                                inproj_fm(wq, tb, bq)
                                inproj_fm(wg, tb, bg)
                                for tt_ in range(4):
                                    for k in range(8):
                                        c0 = k * HW + 2 + tb * TB + tt_ * 128
                                        mm(bv[:, tt_ * 128:(tt_ + 1) * 128], hT[:, c0:c0 + 128], wi[:, k * 128:(k + 1) * 128], k == 0, k == 7,
                                           r=[wi.u(), hT.u((k, tb))], w=[bv.u()], signal=(k == 7))
                                act(tsig[:, :], bf[:, :], AF.Sigmoid, r=[bf.u()], w=[tsig.u()])
                                act(tEp[:, :], tsig[:, :], AF.Ln, r=[tsig.u(), lbt.u()], w=[tEp.u()], scale=oml_c, bias=lb_c)
                                act(tq[:, :], bq[:, :], AF.Silu, r=[bq.u()], w=[tq.u()])
                                act(mixT[:, h * T + tb * TB:h * T + (tb + 1) * TB], bg[:, :], AF.Silu, r=[bg.u()], w=[mixT.u((h, tb))])
                                S.op("dve", lambda e: e.tensor_tensor_scan(out=tb_[:, :], data0=cb[:, C_RST:C_RST + TB], data1=tEp[:, :], initial=0.0,
                                                                           op0=ALU.mult, op1=ALU.add), r=[cb.u(), tEp.u()], w=[tb_.u()])
                                b3 = tb_[:, :].rearrange("p (c t) -> p c t", t=64)
                                bm3 = tbm[:, :].rearrange("p (c t) -> p c t", t=64)
                                tt(bm3, b3, b3[:, :, 31:32].to_broadcast([128, 8, 64]), ALU.subtract, r=[tb_.u()], w=[tbm.u()])
                                S.op("dve", lambda e: e.tensor_tensor_scan(out=tB[:, :], data0=cb[:, C_ONE:C_ONE + TB], data1=tEp[:, :], initial=Bprev[:, 0:1],
                                                                           op0=ALU.mult, op1=ALU.add), r=[cb.u(), tEp.u(), Bprev.u()], w=[tB.u()])
                                cpy(Bprev[:, :], tB[:, TB - 1:TB], r=[tB.u()], w=[Bprev.u()])
                                tsc(tk[:, :], tsig[:, :], noml_c, oml_c, ALU.mult, ALU.add, r=[tsig.u(), lbt.u()], w=[tk.u()])
                                act(sc_[:, 0:8].rearrange("p (c o) -> p c o", o=1), b3[:, :, 63:64], AF.Exp, r=[tb_.u()], w=[sc_.u()])
                                act(sc_[:, 8:16].rearrange("p (c o) -> p c o", o=1), bm3[:, :, 63:64], AF.Exp, r=[tbm.u()], w=[sc_.u()])
                                act(sc_[:, 16:24].rearrange("p (c o) -> p c o", o=1), b3[:, :, 31:32], AF.Exp, r=[tb_.u()], w=[sc_.u()])
                                act(tEp[:, :], tbm[:, :], AF.Exp, r=[tbm.u()], w=[tEp.u()])
                                act(tbm[:, :], tbm[:, :], AF.Exp, r=[tbm.u()], w=[tbm.u()], scale=-1.0)
                                act(tB[:, :], tB[:, :], AF.Exp, r=[tB.u()], w=[tB.u()])
                                stt(q_[:, :], tq[:, :], QSCALE, tEp[:, :], ALU.mult, ALU.mult, r=[tq.u(), tEp.u()], w=[q_.u()])
                                tt(tk[:, :], tk[:, :], tbm[:, :], ALU.mult, r=[tk.u(), tbm.u()], w=[tk.u()])
                                cpy(k_[:, :], tk[:, :], r=[tk.u()], w=[k_.u()])
                                kh_ = kh[g_ % 2]
                                tt(kh_[:, :].rearrange("p (c t) -> p c t", t=64), tk[:, :].rearrange("p (c t) -> p c t", t=64),
                                   sc_[:, 8:16].rearrange("p (c o) -> p c o", o=1).to_broadcast([128, 8, 64]), ALU.mult,
                                   r=[tk.u(), sc_.u()], w=[kh_.u()])
                                stt(mixT[:, (4 + h) * T + tb * TB:(4 + h) * T + (tb + 1) * TB], tq[:, :], QSCALE, tB[:, :], ALU.mult, ALU.mult, r=[tq.u(), tB.u()], w=[mixT.u((4 + h, tb))])
                                if tb == NTB - 1:
                                    cpy(xs_s[:, 128:129], tB[:, TB - 1:TB], r=[tB.u()], w=[xs_s.u("d")])
                                cpy(v_[:, :], bv[:, :], r=[bv.u()], w=[v_.u()])

                            def stageB(tb, h=h):
                                g_ = h * NTB + tb
                                q_, k_, v_, sc_ = qt[g_ % 3], kt[g_ % 2], vtok[g_ % 3], scal[g_ % 2]
                                PT_, Stl_ = PT[g_ % 2], Stl[g_ % 2]
                                if tb == 0:
                                    S.op("dve", lambda e: e.memset(Sst2[0][:, :], 0.0), w=[Sst2[0].u()])
                                kh_ = kh[g_ % 2]
                                for j in range(4):
                                    S.op("pe", lambda e, j=j, kh_=kh_: e.transpose(ptr[:, j * 128:(j + 1) * 128], kh_[:, j * 128:(j + 1) * 128], ident_bf),
                                         r=[kh_.u(), cb.u()], w=[ptr.u()], signal=(j == 3))
                                tsc(ktokA[:, :], ptr[:, 0:TB], pcol(P_MA), None, ALU.mult, None, r=[ptr.u(), par.u()], w=[ktokA.u()])
                                tsc(ktokB[:, :], ptr[:, 0:TB], pcol(P_MB), None, ALU.mult, None, r=[ptr.u(), par.u()], w=[ktokB.u()])
                                bs = pbank()
                                for j in range(4):
                                    mm(bs[:, j * 128:(j + 1) * 128], k_[:, j * 128:(j + 1) * 128], q_[:, j * 128:(j + 1) * 128], True, True,
                                       r=[k_.u(), q_.u()], w=[bs.u()], signal=(j == 3))
                                tt(PT_[:, :], bs[:, :], cb[:, C_MASK:C_MASK + TB], ALU.mult, r=[bs.u(), cb.u()], w=[PT_.u()])
                                bu = [pbank(), pbank()]
                                for c in range(8):
                                    jj = c // 2
                                    b_ = bu[c // 4]
                                    kx = ktokB if c % 2 else ktokA
                                    mm(b_[:, (c % 4) * 128:(c % 4 + 1) * 128], kx[:, jj * 128:(jj + 1) * 128], v_[:, jj * 128:(jj + 1) * 128],
                                       True, True, r=[kx.u(), v_.u()], w=[b_.u()], signal=(c % 4 == 3))
                                for c in range(8):
                                    b_ = bu[c // 4]
                                    Sa, Sb = Sst2[c % 2], Sst2[(c + 1) % 2]
                                    stt(Sb[:, :], Sa[:, :], sc_[:, c:c + 1], b_[:, (c % 4) * 128:(c % 4 + 1) * 128], ALU.mult, ALU.add,
                                        r=[Sa.u(), sc_.u(), b_.u()], w=[Sb.u()])
                                    act(Stl_[:, c * 128:(c + 1) * 128], Sa[:, :], AF.Copy, r=[Sa.u(), sc_.u()], w=[Stl_.u(c)], scale=sc_[:, 16 + c:17 + c])
                                if tb == NTB - 1:
                                    cpy(xs_s[:, 0:128], Sst2[0][:, :], r=[Sst2[0].u()], w=[xs_s.u("s")])
                                    stages[("ag", h)] = exchange(xs_s, 132, "s", [xs_s.u("d"), xs_s.u("s")])

                            def stageC(tb, h=h):
                                g_ = h * NTB + tb
                                oloc = oloc2[h % 2]
                                q_, v_ = qt[g_ % 3], vtok[g_ % 3]
                                PT_, Stl_ = PT[g_ % 2], Stl[g_ % 2]
                                bo = pbank()
                                for c in range(8):
                                    jj = c // 2
                                    mm(bo[:, c * 64:(c + 1) * 64], v_[:, jj * 128:(jj + 1) * 128], PT_[:, c * 64:(c + 1) * 64], True, False,
                                       r=[v_.u(), PT_.u()], w=[bo.u()], signal=False)
                                    mm(bo[:, c * 64:(c + 1) * 64], Stl_[:, c * 128:(c + 1) * 128], q_[:, c * 64:(c + 1) * 64], False, True,
                                       r=[Stl_.u(c), q_.u()], w=[bo.u()], signal=(c == 7))
                                cpy(oloc[:, tb * TB:(tb + 1) * TB], bo[:, :], r=[bo.u()], w=[oloc.u(tb)])

                            def epilogue(h=h):
                                oloc = oloc2[h % 2]
                                ag, go = stages[("ag", h)]
                                exchange_recv(ag, go, xg_s, 132, 3)
                                G = lambda i, a, b: xg_s[:, 132 * i + a:132 * i + b]
                                ru = [xg_s.u(), par.u()]
                                tsc(acc[:, :], G(0, 0, 128), pcol(P_MLT + 0), None, ALU.mult, None, r=ru, w=[acc.u()])
                                for i in (1, 2):
                                    tsc(avec[:, i:i + 1], G(i, 128, 129), pcol(P_MLT + i), pcol(P_OMM + i), ALU.mult, ALU.add, r=ru, w=[avec.u()])
                                    tsc(acc[:, :], acc[:, :], avec[:, i:i + 1], None, ALU.mult, None, r=[acc.u(), avec.u()], w=[acc.u()])
                                    stt(acc[:, :], G(i, 0, 128), pcol(P_MLT + i), acc[:, :], ALU.mult, ALU.add, r=ru + [acc.u()], w=[acc.u()])
                                cpy(Sin[:, :], acc[:, :], r=[acc.u()], w=[Sin.u()])
                                for tb0 in (0, 2):
                                    tbs = (tb0, tb0 + 1)
                                    bcs, os_ap, sqs, rss = {}, {}, {}, {}
                                    for tb in tbs:
                                        bcs[tb] = pbank()
                                        mm(bcs[tb][:, :], Sin[:, :], mixT[:, (4 + h) * T + tb * TB:(4 + h) * T + (tb + 1) * TB], True, True,
                                           r=[Sin.u(), mixT.u((4 + h, tb))], w=[bcs[tb].u()], signal=True)
                                    for tb in tbs:
                                        os_ap[tb] = oloc[:, tb * TB:(tb + 1) * TB]
                                        tt(os_ap[tb], os_ap[tb], bcs[tb][:, :], ALU.add, r=[oloc.u(tb), bcs[tb].u()], w=[oloc.u(tb)])
                                    for tb in tbs:
                                        sq = sqr[st["sq"]]
                                        st["sq"] = (st["sq"] + 1) % 2
                                        sqs[tb] = sq
                                        act(sq[:, :], os_ap[tb], AF.Square, r=[oloc.u(tb)], w=[sq.u()])
                                    for tb in tbs:
                                        pn = pnorm[tb % 2]
                                        mm(pn[:, :], ones_bf, sqs[tb][:, :], True, True, r=[sqs[tb].u(), cb.u()], w=[pn.u()], signal=True)
                                    for tb in tbs:
                                        rss[tb] = finish_rstd(pnorm[tb % 2], 1.0 / 128)
                                    for tb in tbs:
                                        stt(os_ap[tb], os_ap[tb], pcol(P_NG + 4 * lp + h), rss[tb][:, :], ALU.mult, ALU.mult,
                                            r=[oloc.u(tb), rss[tb].u(), par.u()], w=[oloc.u(tb)])
                                    for tb in tbs:
                                        mc = h * T + tb * TB
                                        tt(mixT[:, mc:mc + TB], os_ap[tb], mixT[:, mc:mc + TB], ALU.mult, r=[oloc.u(tb), mixT.u((h, tb))], w=[mixT.u((h, tb))])
                            return stageA, stageB, stageC, epilogue

                        NG = NH * NTB
                        hfun = {}
                        for s_ in range(NG + 2):
                            LA, LB, LC = [], [], []
                            if s_ < NG:
                                if s_ % NTB == 0:
                                    hfun[s_ // NTB] = head_setup(s_ // NTB)
                                LA = S.capture(lambda: hfun[s_ // NTB][0](s_ % NTB))
                            if 0 <= s_ - 1 < NG:
                                LB = S.capture(lambda: hfun[(s_ - 1) // NTB][1]((s_ - 1) % NTB))
                            if 0 <= s_ - 2 < NG:
                                LC = S.capture(lambda: hfun[(s_ - 2) // NTB][2]((s_ - 2) % NTB))
                            for e_, th_ in LA:
                                if e_ == "pe":
                                    th_()
                            ep = [x for x in LA if x[0] != "pe"]
                            i_, j_ = 0, 0
                            while i_ < len(ep) or j_ < len(LB):
                                if i_ < len(ep):
                                    ep[i_][1]()
                                    i_ += 1
                                if j_ < len(LB):
                                    LB[j_][1]()
                                    j_ += 1
                            for _, th_ in LC:
                                th_()
                            if s_ >= 6 and (s_ - 6) % NTB == 0 and (s_ - 6) // NTB < NH - 1:
                                hfun[(s_ - 6) // NTB][3]()
                        hfun[NH - 1][3]()
                    S.barrier()
                    bp(l, "hgrn")
                    with ExitStack() as cs:
                        chf = sb("chf", [128, T + 2], F32, cs)
                        cacc = sb("cacc", [128, T], F32, cs)
                        Bs = sb("Bs", [128, T], F32, cs)
                        cC = [sb("cC%d" % i, [128, TB], F32, cs) for i in range(2)]
                        for j in range(4):
                            wB = load_w(w_in[(l * 28 + 16 + j) * 128:(l * 28 + 17 + j) * 128, :])
                            wC = load_w(w_in[(l * 28 + 20 + j) * 128:(l * 28 + 21 + j) * 128, :])
                            wH = load_w(w_in[(l * 28 + 24 + j) * 128:(l * 28 + 25 + j) * 128, :])
                            bk = pbank()
                            inproj_halo(wC, bk, 0)
                            bk2 = pbank()
                            inproj_halo(wH, bk2, 0)
                            act(cC[0][:, 0:2], bk[:, 0:2], AF.Copy, r=[bk.u()], w=[cC[0].u()])
                            tt(chf[:, 0:2], cC[0][:, 0:2], bk2[:, 0:2], ALU.mult, r=[cC[0].u(), bk2.u()], w=[chf.u("halo")])
                            for tb in range(NTB):
                                bC, bH, bB = pbank(), pbank(), pbank()
                                inproj_fm(wC, tb, bC)
                                inproj_fm(wH, tb, bH)
                                inproj_fm(wB, tb, bB)
                                c = cC[tb % 2]
                                act(c[:, :], bC[:, :], AF.Copy, r=[bC.u()], w=[c.u()])
                                tt(chf[:, 2 + tb * TB:2 + (tb + 1) * TB], c[:, :], bH[:, :], ALU.mult, r=[c.u(), bH.u()], w=[chf.u(tb)])
                                act(Bs[:, tb * TB:(tb + 1) * TB], bB[:, :], AF.Copy, r=[bB.u()], w=[Bs.u(tb)])
                            for tb in range(NTB):
                                rd = [chf.u(tb), par.u()] + ([chf.u(tb - 1)] if tb else [chf.u("halo")])
                                o = cacc[:, tb * TB:(tb + 1) * TB]
                                b0 = tb * TB
                                tsc(o, chf[:, b0 + 2:b0 + 2 + TB], pcol(P_CW + lp * 12 + 8 + j), None, ALU.mult, None, r=rd, w=[cacc.u(tb)])
                                stt(o, chf[:, b0 + 1:b0 + 1 + TB], pcol(P_CW + lp * 12 + 4 + j), o, ALU.mult, ALU.add, r=rd + [cacc.u(tb)], w=[cacc.u(tb)])
                                stt(o, chf[:, b0:b0 + TB], pcol(P_CW + lp * 12 + j), o, ALU.mult, ALU.add, r=rd + [cacc.u(tb)], w=[cacc.u(tb)])
                                mc = (4 + j) * T + tb * TB
                                tt(mixT[:, mc:mc + TB], o, Bs[:, tb * TB:(tb + 1) * TB], ALU.mult, r=[cacc.u(tb), Bs.u(tb)], w=[mixT.u((4 + j, tb))])
                    S.barrier()
                    bp(l, "conv")
                    hts.close()
                    with ExitStack() as os_:
                        mo2 = [sb("mo%d" % i, [128, 8 * 1024], F32, os_) for i in range(2)]
                        tres = sb("tres", [128, TB], F32, os_)

                        def op_mm(th):
                            mo = mo2[th]
                            for i in range(8):
                                wo = load_w(w_out[(l * 8 + i) * 128:(l * 8 + i + 1) * 128, :])
                                for t2 in range(2):
                                    tb = th * 2 + t2
                                    bk = pbank()
                                    for k in range(8):
                                        mm(bk[:, :], wo[:, k * 128:(k + 1) * 128], mixT[:, k * T + tb * TB:k * T + (tb + 1) * TB], k == 0, k == 7,
                                           r=[wo.u(), mixT.u((k, tb))], w=[bk.u()], signal=(k == 7))
                                    mo_ap = mo[:, i * 1024 + t2 * TB:i * 1024 + (t2 + 1) * TB]
                                    act(mo_ap, bk[:, :], AF.Copy, r=[bk.u()], w=[mo.u((i, t2))])
                                    sq = sqr[st["sq"]]
                                    st["sq"] = (st["sq"] + 1) % 2
                                    act(sq[:, :], bk[:, :], AF.Square, r=[bk.u()], w=[sq.u()])
                                    mm(pnorm[t2][:, :], ones_bf, sq[:, :], i == 0, i == 7, r=[sq.u(), cb.u()], w=[pnorm[t2].u()], signal=True)
                            return [finish_rstd(pnorm[t2], 1.0 / D) for t2 in range(2)]

                        def op_res(th, rss):
                            mo = mo2[th]
                            for t2 in range(2):
                                tb = th * 2 + t2
                                rs = rss[t2]
                                for i in range(8):
                                    mo_ap = mo[:, i * 1024 + t2 * TB:i * 1024 + (t2 + 1) * TB]
                                    stt(mo_ap, mo_ap, pcol(P_GQM + 8 * lp + i), rs[:, :], ALU.mult, ALU.mult, r=[mo.u((i, t2)), rs.u(), par.u()], w=[mo.u((i, t2))])
                                for i in range(8):
                                    mo_ap = mo[:, i * 1024 + t2 * TB:i * 1024 + (t2 + 1) * TB]
                                    xa = xT[:, i * T + tb * TB:i * T + (tb + 1) * TB]
                                    tt(xa, xa, mo_ap, ALU.add, r=[xT.u((i, tb)), mo.u((i, t2))], w=[xT.u((i, tb))])

                        rss1 = op_mm(1)
                        op_res(1, rss1)
                        rss0 = op_mm(0)
                        pn = pnorm[1]
                        for k in range(8):
                            sq = sqr[st["sq"]]
                            st["sq"] = (st["sq"] + 1) % 2
                            act(sq[:, 0:2], xT[:, k * T + T - 2:k * T + T], AF.Square, r=[xT.u((k, NTB - 1))], w=[sq.u()])
                            mm(pn[:, 0:2], ones_bf, sq[:, 0:2], k == 0, k == 7, r=[sq.u(), cb.u()], w=[pn.u()], signal=True)
                        act(lnv[0][:, 0:2], pn[:, 0:2], AF.Ln, r=[pn.u()], w=[lnv[0].u()], scale=1.0 / D, bias=EPS)
                        act(lnv[0][:, 0:2], lnv[0][:, 0:2], AF.Exp, r=[lnv[0].u()], w=[lnv[0].u()], scale=-0.5)
                        for k in range(8):
                            stt(xs_h[:, 2 * k:2 * k + 2], xT[:, k * T + T - 2:k * T + T], pcol(P_GPF + 8 * lp + k), lnv[0][:, 0:2], ALU.mult, ALU.mult,
                                r=[xT.u((k, NTB - 1)), lnv[0].u(), par.u()], w=[xs_h.u()])
                        ffn_ag = exchange(xs_h, 16, "hf")
                        op_res(0, rss0)
                    S.barrier()
            S.barrier()
            bp(l, "mix")
            HH = 1024 + 2
            with ExitStack() as fs:
                hF = sb("hF", [128, 8 * HH], BF16, fs)
                ug = sb("ug", [128, 1024 + 2], F32, fs)
                uv = sb("uv", [128, 1024 + 2], F32, fs)
                cg = [sb("cg%d" % i, [128, TB], F32, fs) for i in range(2)]
                cv = [sb("cv%d" % i, [128, TB], F32, fs) for i in range(2)]
                car = sb("car", [128, 4 * NFC], F32, fs)
                aT = sb("aT", [128, NFC * 1024], BF16, fs)
                yT = sb("yT", [128, 8 * 1024], F32, fs)
                tres = sb("tres2", [128, TB], F32, fs)
                ag, go = ffn_ag
                exchange_recv(ag, go, xg_h, 16)
                tsc(hal[:, :], xg_h[:, 0:16], pcol(P_SEL + 0), None, ALU.mult, None, r=[xg_h.u(), par.u()], w=[hal.u()])
                for i in range(1, 4):
                    stt(hal[:, :], xg_h[:, 16 * i:16 * i + 16], pcol(P_SEL + i), hal[:, :], ALU.mult, ALU.add,
                        r=[xg_h.u(), hal.u(), par.u()], w=[hal.u()])
                for th in range(2):
                    if th == 0:
                        for k in range(8):
                            cpy(hF[:, k * HH:k * HH + 2], hal[:, 2 * k:2 * k + 2], r=[hal.u()], w=[hF.u((k, "halo"))])
                    prenorm(hF, HH, P_GPF, [2 * th, 2 * th + 1], 2 * th)
                    for j in range(NFC):
                        wG = load_w(w_up[(l * 44 + j) * 128:(l * 44 + j + 1) * 128, :])
                        wV = load_w(w_up[(l * 44 + 22 + j) * 128:(l * 44 + 23 + j) * 128, :])
                        fo = P_FCW + lp * 132
                        halves = ((wG, ug, 0, j), (wV, uv, 2 * NFC, 22 + j))
                        for (wsl, uF, cofs, fofs) in halves:
                            if th == 0:
                                bh = pbank()
                                for k in range(8):
                                    mm(bh[:, 0:2], wsl[:, k * 128:(k + 1) * 128], hF[:, k * HH:k * HH + 2], k == 0, k == 7,
                                       r=[wsl.u(), hF.u((k, "halo"))], w=[bh.u()], signal=(k == 7))
                                act(uF[:, 0:2], bh[:, 0:2], AF.Copy, r=[bh.u()], w=[uF.u("halo")])
                            else:
                                cpy(uF[:, 0:2], car[:, cofs + 2 * j:cofs + 2 * j + 2], r=[car.u((cofs, j))], w=[uF.u("halo")])
                        for t2 in range(2):
                            cs = (cg[t2], cv[t2])
                            for hi, (wsl, uF, cofs, fofs) in enumerate(halves):
                                bk = pbank()
                                for k in range(8):
                                    c0 = k * HH + 2 + t2 * TB
                                    mm(bk[:, :], wsl[:, k * 128:(k + 1) * 128], hF[:, c0:c0 + TB], k == 0, k == 7,
                                       r=[wsl.u(), hF.u((k, t2))], w=[bk.u()], signal=(k == 7))
                                act(uF[:, 2 + t2 * TB:2 + (t2 + 1) * TB], bk[:, :], AF.Copy, r=[bk.u()], w=[uF.u(t2)])
                                act(cs[hi][:, :], bk[:, :], AF.Copy, r=[bk.u(), par.u()], w=[cs[hi].u()], scale=pcol(fo + 88 + fofs))
                            for tap, sh in ((1, 1), (0, 0)):
                                for hi, (wsl, uF, cofs, fofs) in enumerate(halves):
                                    c_ = cs[hi]
                                    rd = [uF.u(t2), uF.u(t2 - 1) if t2 else uF.u("halo"), par.u(), c_.u()]
                                    stt(c_[:, :], uF[:, sh + t2 * TB:sh + (t2 + 1) * TB], pcol(fo + tap * 44 + fofs), c_[:, :], ALU.mult, ALU.add, r=rd, w=[c_.u()])
                            act(cs[0][:, :], cs[0][:, :], AF.Silu, r=[cs[0].u()], w=[cs[0].u()])
                            ac = j * 1024 + t2 * TB
                            tt(aT[:, ac:ac + TB], cs[0][:, :], cs[1][:, :], ALU.mult, r=[cs[0].u(), cs[1].u()], w=[aT.u((j, t2))])
                        if th == 0:
                            for (wsl, uF, cofs, fofs) in halves:
                                cpy(car[:, cofs + 2 * j:cofs + 2 * j + 2], uF[:, 1024:1026], r=[uF.u(1)], w=[car.u((cofs, j))])
                    for i in range(8):
                        wd = [load_w(w_dn[(l * 8 + i) * 128:(l * 8 + i + 1) * 128, 0:1024]),
                              load_w(w_dn[(l * 8 + i) * 128:(l * 8 + i + 1) * 128, 1024:2048]),
                              load_w(w_dn[(l * 8 + i) * 128:(l * 8 + i + 1) * 128, 2048:2816])]
                        for t2 in range(2):
                            bk = pbank()
                            for k in range(NFC):
                                wsl = wd[k // 8]
                                kk = k % 8
                                mm(bk[:, :], wsl[:, kk * 128:(kk + 1) * 128], aT[:, k * 1024 + t2 * TB:k * 1024 + (t2 + 1) * TB], k == 0, k == NFC - 1,
                                   r=[wsl.u(), aT.u((k, t2))], w=[bk.u()], signal=(k == NFC - 1))
                            y_ap = yT[:, i * 1024 + t2 * TB:i * 1024 + (t2 + 1) * TB]
                            act(y_ap, bk[:, :], AF.Copy, r=[bk.u()], w=[yT.u((i, t2))])
                            sq = sqr[st["sq"]]
                            st["sq"] = (st["sq"] + 1) % 2
                            act(sq[:, :], bk[:, :], AF.Square, r=[bk.u()], w=[sq.u()])
                            mm(pnorm[t2][:, :], ones_bf, sq[:, :], i == 0, i == 7, r=[sq.u(), cb.u()], w=[pnorm[t2].u()], signal=True)
                    for t2 in range(2):
                        tb = th * 2 + t2
                        rs = finish_rstd(pnorm[t2], 1.0 / D)
                        for i in range(8):
                            y_ap = yT[:, i * 1024 + t2 * TB:i * 1024 + (t2 + 1) * TB]
                            stt(y_ap, y_ap, pcol(P_GQF + 8 * lp + i), rs[:, :], ALU.mult, ALU.mult, r=[yT.u((i, t2)), rs.u(), par.u()], w=[yT.u((i, t2))])
                        for i in range(8):
                            y_ap = yT[:, i * 1024 + t2 * TB:i * 1024 + (t2 + 1) * TB]
                            xa = xT[:, i * T + tb * TB:i * T + (tb + 1) * TB]
                            tt(xa, xa, y_ap, ALU.add, r=[xT.u((i, tb)), yT.u((i, t2))], w=[xT.u((i, tb))])
            S.barrier()
            bp(l, "ffn")

          except _Stop:
            break
        S.enabled = True
        for _i in range(int(os.environ.get("K_DUMMY_PE", "0"))):
            mm(pring[0][:, 0:2], ones_bf, cb[:, 0:2], True, True, r=[cb.u()], w=[pring[0].u()], signal=(_i % 64 == 63))
        for _i in range(int(os.environ.get("K_DUMMY_ACT", "0"))):
            act(lnv[0][:, 0:8], lnv[0][:, 0:8], AF.Copy, r=[lnv[0].u()], w=[lnv[0].u()])
        for _i in range(int(os.environ.get("K_DUMMY_DVE", "0"))):
            cpy(lnv[1][:, 0:8], lnv[1][:, 8:16], r=[lnv[1].u()], w=[lnv[1].u()])
        for _i in range(int(os.environ.get("K_DUMMY_SLOW", "0"))):
            act(xT[:, 0:2048], xT[:, 0:2048], AF.Copy, r=[xT.u((0, 0)), xT.u((0, 1)), xT.u((0, 2)), xT.u((0, 3))],
                w=[xT.u((0, 0)), xT.u((0, 1)), xT.u((0, 2)), xT.u((0, 3))])
        if os.environ.get("K_DUMMY_HI"):
            for _i in range(nlayers * 44 - 48, nlayers * 44):
                load_w(w_up[_i * 128:(_i + 1) * 128, :])
            for _i in range(nlayers * 8 - 8, nlayers * 8):
                load_w(w_dn[_i * 128:(_i + 1) * 128, 0:1024])
                load_w(w_dn[_i * 128:(_i + 1) * 128, 2048:2816])
        for _i in range(int(os.environ.get("K_DUMMY_DMA", "0"))):
            load_w(w_up[(_i % 40) * 128:(_i % 40 + 1) * 128, :])
        for _i in range(int(os.environ.get("K_DUMMY_CC", "0"))):
            ag_, go_ = exchange(xs_h, 16, "dummy")
            exchange_recv(ag_, go_, xg_h, 16)
        sigs = []
        for k in range(8):
            for tb in range(NTB):
                c0 = k * T + tb * TB
                sigs.append(S.dma("sp", lambda e, c0=c0: e.dma_start(out=out[:, c0:c0 + TB], in_=xT[:, c0:c0 + TB]), r=[xT.u((k, tb))]))
        for key, val in sigs:
            S._wait("sp", key, val)
        S.emit()
        print("kernel: recorded ops", S.nops, {e: len(S.q[e]) for e in S.ENG}, "sigcnt", S.cnt, "dma", [c for _, c in S.dsem][:4], "cc", S.cccnt)
    return nc


def _prep_inputs(x, lb_param, w_in, w_out, conv_w, ffn_w_up, ffn_conv_w, ffn_w_down,
                 hgrn_norm_g, pre_mix_g, post_mix_g, pre_ffn_g, post_ffn_g):
    f = np.float32

    def wl(w, nk):
        L, K, N = w.shape
        nj = N // 128
        a = np.asarray(w, f).reshape(L, nk, 128, nj, 128).transpose(0, 3, 2, 1, 4)
        return np.ascontiguousarray(a).reshape(L * nj * 128, nk * 128)

    w_in_r = wl(w_in, 8)
    w_out_r = wl(w_out, 8)
    w_up_r = wl(ffn_w_up, 8)
    w_dn_r = wl(ffn_w_down, NFC)

    def pk(a, n):
        a = np.asarray(a, f).reshape(DEPTH, n, 128).transpose(2, 0, 1)
        return np.ascontiguousarray(a).reshape(128, DEPTH * n)

    def pk3(a, n):
        a = np.asarray(a, f).reshape(DEPTH, 3, n, 128).transpose(3, 0, 1, 2)
        return np.ascontiguousarray(a).reshape(128, DEPTH * 3 * n)

    base = np.concatenate([pk(lb_param, 4), pk(hgrn_norm_g, 4), pk(pre_mix_g, 8), pk(post_mix_g, 8), pk(pre_ffn_g, 8),
                           pk(post_ffn_g, 8), pk3(conv_w, 4), pk3(ffn_conv_w, 44)], axis=1)
    assert base.shape[1] == P_SEL
    cbm = np.zeros((128, NCB), np.float32)
    cbm[:, C_ONES:C_ONES + 128] = 1.0
    cbm[:, C_ID:C_ID + 128] = np.eye(128, dtype=np.float32)
    s = np.arange(128)[:, None]
    t = np.arange(128)[None, :]
    m2 = ((s // 64) == (t // 64)) & (s <= t)
    cbm[:, C_MASK:C_MASK + 512] = np.tile(m2.astype(np.float32), (1, 4))
    rst = np.ones((128, 512), np.float32)
    rst[:, ::64] = 0.0
    cbm[:, C_RST:C_RST + 512] = rst
    cbm[:, C_ONE:C_ONE + 512] = 1.0
    cbm = cbm.astype(ml_dtypes.bfloat16)

    xs = np.asarray(x, f)
    in_maps = []
    for c in range(8):
        b, sgm = c // 4, c % 4
        xc = xs[b, sgm * T:(sgm + 1) * T, :]
        xc = np.ascontiguousarray(xc.T.reshape(8, 128, T).transpose(1, 0, 2)).reshape(128, 8 * T)
        extra = np.zeros((128, 14), np.float32)
        extra[:64, 12] = 1.0
        extra[64:, 13] = 1.0
        for i in range(4):
            extra[:, i] = 1.0 if i == sgm - 1 else 0.0
            extra[:, 4 + i] = 1.0 if i < sgm else 0.0
            extra[:, 8 + i] = 0.0 if i < sgm else 1.0
        par = np.ascontiguousarray(np.concatenate([base, extra], axis=1))
        in_maps.append({"xin": xc, "w_in": w_in_r, "w_out": w_out_r, "w_up": w_up_r, "w_dn": w_dn_r, "par": par, "cb": cbm})
    return in_maps


_NC_CACHE = {}
SPLIT = [(0, 4)]


def _to_fm(xc):
    return np.ascontiguousarray(xc.T.reshape(8, 128, T).transpose(1, 0, 2)).reshape(128, 8 * T)


def kernel(x, lb_param, w_in, w_out, conv_w, ffn_w_up, ffn_conv_w, ffn_w_down,
           hgrn_norm_g, pre_mix_g, post_mix_g, pre_ffn_g, post_ffn_g, _dbg=None, _nlayers=None):
    in_maps = _prep_inputs(x, lb_param, w_in, w_out, conv_w, ffn_w_up, ffn_conv_w, ffn_w_down,
                           hgrn_norm_g, pre_mix_g, post_mix_g, pre_ffn_g, post_ffn_g)
    split = SPLIT if _nlayers is None else [(0, _nlayers)]
    full = {k: in_maps[0][k] for k in ("w_in", "w_out", "w_up", "w_dn")}
    res = None
    for (l0, l1) in split:
        nl = l1 - l0
        key = (_dbg, nl, l0)
        if key not in _NC_CACHE:
            _NC_CACHE[key] = build_program(nl, _dbg, l0)
        nc = _NC_CACHE[key]
        for c, m in enumerate(in_maps):
            m["w_in"] = full["w_in"][l0 * 28 * 128:l1 * 28 * 128]
            m["w_out"] = full["w_out"][l0 * 8 * 128:l1 * 8 * 128]
            m["w_up"] = full["w_up"][l0 * 44 * 128:l1 * 44 * 128]
            m["w_dn"] = full["w_dn"][l0 * 8 * 128:l1 * 8 * 128]
            if res is not None:
                m["xin"] = np.asarray(res.results[c]["out"])
        res = run_bass_kernel_spmd(nc, in_maps, core_ids=list(range(8)))
    outp = np.empty((2, 4 * T, D), np.float32)
    for c in range(8):
        b, sgm = c // 4, c % 4
        o = np.asarray(res.results[c]["out"]).reshape(128, 8, T).transpose(2, 1, 0).reshape(T, D)
        outp[b, sgm * T:(sgm + 1) * T, :] = o
    return outp
```

```python
import os
from contextlib import ExitStack

import numpy as np
import ml_dtypes

import concourse.bass as bass
import concourse.mybir as mybir
from concourse.bass_utils import run_bass_kernel_spmd

F32 = mybir.dt.float32
BF16 = mybir.dt.bfloat16
AF = mybir.ActivationFunctionType
ALU = mybir.AluOpType

D = 1024
T = 2048
TB = 512
NTB = T // TB
DEPTH = 4
NH = 4
DFF = 2816
NFC = DFF // 128
EPS = 1e-6
QSCALE = 128 ** -0.5

P_LBP = 0
P_NG = P_LBP + 16
P_GPM = P_NG + 16
P_GQM = P_GPM + 32
P_GPF = P_GQM + 32
P_GQF = P_GPF + 32
P_CW = P_GQF + 32
P_FCW = P_CW + 48
P_SEL = P_FCW + 528
P_MLT = P_SEL + 4
P_OMM = P_MLT + 4
P_MA = P_OMM + 4
P_MB = P_MA + 1
NPAR = P_MB + 1
C_ONES = 0
C_ID = 128
C_MASK = 256
C_RST = 768
C_ONE = 1280
NCB = 1792

NSLOT = 6


class TT:
    __slots__ = ("w", "r")

    def __init__(self):
        self.w = None
        self.r = {}


class Tile:
    def __init__(self, t):
        self.t = t
        self.units = {}

    def u(self, key=0):
        x = self.units.get(key)
        if x is None:
            x = self.units[key] = TT()
        return x

    def __getitem__(self, idx):
        return self.t[idx]


class Sched:
    ENG = ("pe", "act", "dve", "pool", "sp")

    def __init__(self, nc, es):
        self.nc = nc
        self.q = {e: [] for e in self.ENG}
        self.cnt = {e: 0 for e in self.ENG}
        self.waited = {e: {} for e in self.ENG}
        self.sem = {e: es.enter_context(nc.semaphore("c_" + e)) for e in self.ENG}
        self.dsem = [[es.enter_context(nc.semaphore("d%d" % i)), 0] for i in range(24)]
        self.drr = 0
        self.es = es
        self.cccnt = 0
        self.semh = dict(self.sem)
        for i, (h, _) in enumerate(self.dsem):
            self.semh["d%d" % i] = h
        self.nops = 0
        self.enabled = True
        self.seen = {}
        self.cap = None

    def _wait(self, eng, key, val):
        if eng == "pe" and key == "pe":
            return
        if key in self.cnt:
            assert val <= self.cnt[key], ("wait on future signal", eng, key, val, self.cnt[key])
        if self.waited[eng].get(key, 0) >= val:
            return
        self.waited[eng][key] = val
        if self.seen.get(key, 0) < val:
            self.seen[key] = val
        self.q[eng].append(("w", key, val))

    def _deps(self, eng, reads, writes):
        for t in reads:
            if t.w is not None:
                self._wait(eng, *t.w)
        for t in writes:
            if t.w is not None:
                self._wait(eng, *t.w)
            for k, v in t.r.items():
                self._wait(eng, k, v)

    def _mark(self, sig, reads, writes):
        k, v = sig
        for t in reads:
            if t.r.get(k, 0) < v:
                t.r[k] = v
        for t in writes:
            t.w = sig
            t.r = {}

    def capture(self, f):
        assert self.cap is None
        self.cap = []
        try:
            f()
        finally:
            lst, self.cap = self.cap, None
        return lst

    def op(self, eng, fn, r=(), w=(), signal=True):
        if not self.enabled:
            return
        if self.cap is not None:
            self.cap.append((eng, lambda: self.op(eng, fn, r, w, signal)))
            return
        self._deps(eng, r, w)
        self.nops += 1
        if signal:
            self.cnt[eng] += 1
            sig = (eng, self.cnt[eng])
            self.q[eng].append(("i", fn, eng, 1))
        else:
            sig = (eng, self.cnt[eng] + 1)
            self.q[eng].append(("i", fn, None, 0))
        self._mark(sig, r, w)

    def dma(self, eng, fn, r=(), w=()):
        if not self.enabled:
            return None
        if self.cap is not None:
            self.cap.append((eng, lambda: self.dma(eng, fn, r, w)))
            return None
        self._deps(eng, r, w)
        i = self.drr
        self.drr = (self.drr + 1) % len(self.dsem)
        key = "d%d" % i
        prev = self.dsem[i][1]
        if prev and self.seen.get(key, 0) < prev:
            self._wait(eng, key, prev)
        self.dsem[i][1] = prev + 16
        sig = (key, prev + 16)
        self.q[eng].append(("i", fn, key, 16))
        self._mark(sig, r, w)
        return sig

    def collective(self, fn, r=(), w=()):
        if not self.enabled:
            return
        if self.cap is not None:
            self.cap.append(("pool", lambda: self.collective(fn, r, w)))
            return
        self._deps("pool", r, w)
        self.cccnt += 1
        key = "cc%d" % self.cccnt
        self.semh[key] = self.es.enter_context(self.nc.semaphore(key))
        sig = (key, 1)
        self.q["pool"].append(("i", fn, key, 1))
        self._mark(sig, r, w)

    def barrier(self):
        if not self.enabled:
            return
        cur = [(e, self.cnt[e]) for e in self.ENG if self.cnt[e]]
        cur += [("d%d" % i, c) for i, (_, c) in enumerate(self.dsem) if c]
        cur += [("cc%d" % i, 1) for i in range(1, self.cccnt + 1)]
        for e in self.ENG:
            for k, v in cur:
                if k == e:
                    continue
                self._wait(e, k, v)

    def emit(self):
        nc = self.nc
        engs = {"pe": "tensor", "act": "scalar", "dve": "vector", "pool": "gpsimd", "sp": "sync"}
        with nc.Block() as block:
            for e in self.ENG:
                items = self.q[e]

                def body(eng, items=items):
                    for it in items:
                        if it[0] == "w":
                            eng.wait_ge(self.semh[it[1]], it[2])
                        else:
                            ins = it[1](eng)
                            if it[2] is not None:
                                ins.then_inc(self.semh[it[2]], it[3])

                getattr(block, engs[e])(body)


def build_program(nlayers=DEPTH, dbg=None, lbase=0):
    nc = bass.Bass("TRN2", target_bir_lowering=False)
    xin = nc.dram_tensor("xin", [128, 8 * T], F32, kind="ExternalInput").ap()
    w_in = nc.dram_tensor("w_in", [nlayers * 28 * 128, 1024], F32, kind="ExternalInput").ap()
    w_out = nc.dram_tensor("w_out", [nlayers * 8 * 128, 1024], F32, kind="ExternalInput").ap()
    w_up = nc.dram_tensor("w_up", [nlayers * 44 * 128, 1024], F32, kind="ExternalInput").ap()
    w_dn = nc.dram_tensor("w_dn", [nlayers * 8 * 128, NFC * 128], F32, kind="ExternalInput").ap()
    par_d = nc.dram_tensor("par", [128, NPAR], F32, kind="ExternalInput").ap()
    cb_d = nc.dram_tensor("cb", [128, NCB], BF16, kind="ExternalInput").ap()
    out = nc.dram_tensor("out", [128, 8 * T], F32, kind="ExternalOutput").ap()

    es = ExitStack()
    with es:
        S = Sched(nc, es)

        uid = [0]

        def sb(name, shape, dt, stack=es):
            uid[0] += 1
            return Tile(stack.enter_context(nc.sbuf_tensor("%s_%d" % (name, uid[0]), shape, dt)))

        def ps(name, shape, dt, stack=es):
            uid[0] += 1
            return Tile(stack.enter_context(nc.psum_tensor("%s_%d" % (name, uid[0]), shape, dt)))

        xT = sb("xT", [128, 8 * T], F32)
        par = sb("par", [128, NPAR], F32)
        cb = sb("cb", [128, NCB], BF16)
        lbt = sb("lbt", [128, 64], F32)
        wring = [sb("wr%d" % i, [128, 1024], BF16) for i in range(NSLOT)]
        sqr = [sb("sq%d" % i, [128, TB], BF16) for i in range(2)]
        lnv = [sb("lnv%d" % i, [128, TB], F32) for i in range(1)]
        rstd = [sb("rstd%d" % i, [128, TB], F32) for i in range(2)]
        xs_h = sb("xs_h", [128, 16], F32)
        xg_h = sb("xg_h", [128, 64], F32)
        hal = sb("hal", [128, 16], F32)
        pring = [ps("pb%d" % i, [128, TB], F32) for i in range(5)]
        pnorm = [ps("pn%d" % i, [128, TB], F32) for i in range(2)]
        ptr = ps("ptr", [128, 1024], BF16)
        st = {"wr": 0, "pr": 0, "sq": 0, "ln": 0, "ex": 0}

        ones_bf = cb[:, C_ONES:C_ONES + 128]
        ident_bf = cb[:, C_ID:C_ID + 128]

        def pcol(off):
            return par[:, off:off + 1]

        def wslot():
            i = st["wr"]
            st["wr"] = (i + 1) % NSLOT
            return wring[i]

        def pbank():
            i = st["pr"]
            st["pr"] = (i + 1) % len(pring)
            return pring[i]

        def load_w(src_ap):
            sl = wslot()
            ncol = src_ap.shape[1]
            S.dma("pool", lambda e, sl=sl, src_ap=src_ap, ncol=ncol: e.dma_start(out=sl[:, 0:ncol], in_=src_ap), w=[sl.u()])
            return sl

        def mm(out_ap, lhsT, rhs, start, stop, r, w, signal):
            S.op("pe", lambda e: e.matmul(out_ap, lhsT=lhsT, rhs=rhs, start=start, stop=stop), r=r, w=w, signal=signal)

        def act(out_ap, in_ap, func, r, w, scale=None, bias=None):
            kw = {}
            if scale is not None:
                kw["scale"] = scale
            if bias is not None:
                kw["bias"] = bias
            S.op("act", lambda e: e.activation(out=out_ap, in_=in_ap, func=func, **kw), r=r, w=w)

        def tsc(out_ap, in0, s1, s2, op0, op1, r, w, eng="dve"):
            if s2 is None:
                S.op(eng, lambda e: e.tensor_scalar(out=out_ap, in0=in0, scalar1=s1, scalar2=None, op0=op0), r=r, w=w)
            else:
                S.op(eng, lambda e: e.tensor_scalar(out=out_ap, in0=in0, scalar1=s1, scalar2=s2, op0=op0, op1=op1), r=r, w=w)

        def stt(out_ap, in0, sc, in1, op0, op1, r, w):
            S.op("dve", lambda e: e.scalar_tensor_tensor(out=out_ap, in0=in0, scalar=sc, in1=in1, op0=op0, op1=op1), r=r, w=w)

        def tt(out_ap, in0, in1, op, r, w, eng="dve"):
            S.op(eng, lambda e: e.tensor_tensor(out=out_ap, in0=in0, in1=in1, op=op), r=r, w=w)

        def cpy(out_ap, in_ap, r, w, eng="dve"):
            S.op(eng, lambda e: e.tensor_copy(out_ap, in_ap), r=r, w=w)

        def exchange(src, ncols, tag, runits=None):
            i = st["ex"]
            st["ex"] += 1
            agin = nc.dram_tensor("agin%d" % i, [128, ncols], F32)
            agout = nc.dram_tensor("agout%d" % i, [4 * 128, ncols], F32)
            gi, go = TT(), TT()
            S.dma("sp", lambda e: e.dma_start(out=agin.ap()[:, :], in_=src[:, 0:ncols]), r=(runits or [src.u()]), w=[gi])
            S.collective(lambda e: e.collective_compute("AllGather", ALU.bypass, replica_groups=[[0, 1, 2, 3], [4, 5, 6, 7]],
                                                        ins=[agin.ap().opt()], outs=[agout.ap().opt()]), r=[gi], w=[go])
            return agout, go

        def exchange_recv(agout, go, dst, ncols, nr=4):
            S.dma("sp", lambda e: e.dma_start(out=dst[:, 0:nr * ncols].rearrange("p (r c) -> p r c", r=nr),
                                              in_=agout.ap()[0:nr * 128, :].rearrange("(r p) c -> p r c", p=128)), r=[go], w=[dst.u()])

        def norm_stats(src_fn, nk, tb_key, scale, pn):
            for k in range(nk):
                ap, un = src_fn(k)
                sq = sqr[st["sq"]]
                st["sq"] = (st["sq"] + 1) % 2
                act(sq[:, :], ap, AF.Square, r=un, w=[sq.u()])
                mm(pn[:, :], ones_bf, sq[:, :], k == 0, k == nk - 1, r=[sq.u(), cb.u()], w=[pn.u()], signal=True)
            return finish_rstd(pn, scale)

        def finish_rstd(pn, scale):
            i = st["ln"]
            st["ln"] = (i + 1) % 2
            lv, rs = lnv[0], rstd[i]
            act(lv[:, :], pn[:, :], AF.Ln, r=[pn.u()], w=[lv.u()], scale=scale, bias=EPS)
            act(rs[:, :], lv[:, :], AF.Exp, r=[lv.u()], w=[rs.u()], scale=-0.5)
            return rs

        for k in range(8):
            for tb in range(NTB):
                c0 = k * T + tb * TB
                S.dma("sp", lambda e, c0=c0: e.dma_start(out=xT[:, c0:c0 + TB], in_=xin[:, c0:c0 + TB]), w=[xT.u((k, tb))])
        S.dma("sp", lambda e: e.dma_start(out=par[:, :], in_=par_d[:, :]), w=[par.u()])
        S.dma("sp", lambda e: e.dma_start(out=cb[:, :], in_=cb_d[:, :]), w=[cb.u()])
        L = lambda l: lbt[:, 48 + 0:48 + 4]
        mx = lbt[:, 48:52]
        tt(mx, par[:, P_LBP:P_LBP + 4], par[:, P_LBP + 4:P_LBP + 8], ALU.max, r=[par.u()], w=[lbt.u()])
        tt(mx, mx, par[:, P_LBP + 8:P_LBP + 12], ALU.max, r=[lbt.u()], w=[lbt.u()])
        tt(mx, mx, par[:, P_LBP + 12:P_LBP + 16], ALU.max, r=[lbt.u()], w=[lbt.u()])
        for l in range(4):
            tt(lbt[:, 4 * l:4 * l + 4], par[:, P_LBP + 4 * l:P_LBP + 4 * l + 4], mx, ALU.subtract, r=[par.u(), lbt.u()], w=[lbt.u()])
        act(lbt[:, 0:16], lbt[:, 0:16], AF.Exp, r=[lbt.u()], w=[lbt.u()])
        sm = lbt[:, 52:56]
        tt(sm, lbt[:, 0:4], lbt[:, 4:8], ALU.add, r=[lbt.u()], w=[lbt.u()])
        tt(sm, sm, lbt[:, 8:12], ALU.add, r=[lbt.u()], w=[lbt.u()])
        tt(sm, sm, lbt[:, 12:16], ALU.add, r=[lbt.u()], w=[lbt.u()])
        S.op("dve", lambda e: e.reciprocal(sm, sm), r=[lbt.u()], w=[lbt.u()])
        for l in range(4):
            tt(lbt[:, 4 * l:4 * l + 4], lbt[:, 4 * l:4 * l + 4], sm, ALU.mult, r=[lbt.u()], w=[lbt.u()])
        tt(lbt[:, 8:12], lbt[:, 8:12], lbt[:, 4:8], ALU.add, r=[lbt.u()], w=[lbt.u()])
        tt(lbt[:, 12:16], lbt[:, 12:16], lbt[:, 8:12], ALU.add, r=[lbt.u()], w=[lbt.u()])
        S.op("dve", lambda e: e.memset(lbt[:, 0:4], 0.0), w=[lbt.u()])
        tsc(lbt[:, 16:32], lbt[:, 0:16], -1.0, 1.0, ALU.mult, ALU.add, r=[lbt.u()], w=[lbt.u()])
        tsc(lbt[:, 32:48], lbt[:, 16:32], -1.0, None, ALU.mult, None, r=[lbt.u()], w=[lbt.u()])

        HW = T + 2

        class _Stop(Exception):
            pass

        def bp(l, name):
            if dbg == (l, name):
                S.barrier()
                S.enabled = False

        for l in range(nlayers):
          try:
            lp = l + lbase
            bp(l, "pre")
            with ExitStack() as mixs:
                mixT = sb("mixT", [128, 8 * T], BF16, mixs)
                hts = ExitStack()
                hT = sb("hT", [128, 8 * HW], BF16, hts)

                def prenorm(hT, hw, gofs, tbs, toff):
                    for tb in tbs:
                        rs = norm_stats(lambda k: (xT[:, k * T + tb * TB:k * T + (tb + 1) * TB], [xT.u((k, tb))]), 8, tb, 1.0 / D, pnorm[tb % 2])
                        for k in range(8):
                            c0 = k * hw + 2 + (tb - toff) * TB
                            stt(hT[:, c0:c0 + TB], xT[:, k * T + tb * TB:k * T + (tb + 1) * TB], pcol(gofs + 8 * lp + k), rs[:, :],
                                ALU.mult, ALU.mult, r=[xT.u((k, tb)), rs.u(), par.u()], w=[hT.u((k, tb - toff))])

                def halo_exchange(hT, hw, last_c0, last_key):
                    for k in range(8):
                        cpy(xs_h[:, 2 * k:2 * k + 2], hT[:, k * hw + last_c0:k * hw + last_c0 + 2], r=[hT.u((k, last_key))], w=[xs_h.u()])
                    ag, go = exchange(xs_h, 16, "h")
                    exchange_recv(ag, go, xg_h, 16)
                    tsc(hal[:, :], xg_h[:, 0:16], pcol(P_SEL + 0), None, ALU.mult, None, r=[xg_h.u(), par.u()], w=[hal.u()])
                    for i in range(1, 4):
                        stt(hal[:, :], xg_h[:, 16 * i:16 * i + 16], pcol(P_SEL + i), hal[:, :], ALU.mult, ALU.add,
                            r=[xg_h.u(), hal.u(), par.u()], w=[hal.u()])
                    for k in range(8):
                        cpy(hT[:, k * hw:k * hw + 2], hal[:, 2 * k:2 * k + 2], r=[hal.u()], w=[hT.u((k, "halo"))])

                prenorm(hT, HW, P_GPM, range(NTB), 0)
                bp(l, "norm")
                halo_exchange(hT, HW, 2 + T - 2, NTB - 1)
                bp(l, "halo")

                def hrhs(k, tb):
                    c0 = k * HW + 2 + tb * TB
                    return hT[:, c0:c0 + TB]

                def inproj_fm(wsl, tb, bank):
                    for k in range(8):
                        mm(bank[:, :], wsl[:, k * 128:(k + 1) * 128], hrhs(k, tb), k == 0, k == 7,
                           r=[wsl.u(), hT.u((k, tb))], w=[bank.u()], signal=(k == 7))

                def inproj_halo(wsl, bank, col):
                    for k in range(8):
                        mm(bank[:, col:col + 2], wsl[:, k * 128:(k + 1) * 128], hT[:, k * HW:k * HW + 2], k == 0, k == 7,
                           r=[wsl.u(), hT.u((k, "halo"))], w=[bank.u()], signal=(k == 7))

                with ExitStack() as mxs:
                    with ExitStack() as hs:
                        tq = sb("tq", [128, TB], F32, hs)
                        tsig = sb("tsig", [128, TB], F32, hs)
                        tk = sb("tk", [128, TB], F32, hs)
                        tb_ = sb("tb_", [128, TB], F32, hs)
                        tB = sb("tB", [128, TB], F32, hs)
                        tbm = sb("tbm", [128, TB], F32, hs)
                        tEp = sb("tEp", [128, TB], F32, hs)
                        qt = [sb("qt%d" % i, [128, TB], BF16, hs) for i in range(3)]
                        kt = [sb("kt%d" % i, [128, TB], BF16, hs) for i in range(2)]
                        kh = [sb("kh%d" % i, [128, TB], BF16, hs) for i in range(2)]
                        vtok = [sb("vtok%d" % i, [128, TB], BF16, hs) for i in range(3)]
                        ktokA = sb("ktokA", [128, TB], BF16, hs)
                        ktokB = sb("ktokB", [128, TB], BF16, hs)
                        PT = [sb("PT%d" % i, [128, TB], BF16, hs) for i in range(2)]
                        Stl = [sb("Stl%d" % i, [128, 8 * 128], BF16, hs) for i in range(2)]
                        scal = [sb("scal%d" % i, [128, 24], F32, hs) for i in range(2)]
                        oloc2 = [sb("oloc%d" % i, [128, T], F32, hs) for i in range(2)]
                        Sst2 = [sb("Sst%d" % i, [128, 128], F32, hs) for i in range(2)]
                        Bprev = sb("Bprev", [128, 1], F32, hs)
                        xs_s = sb("xs_s", [128, 132], F32, hs)
                        xg_s = sb("xg_s", [128, 3 * 132], F32, hs)
                        acc = sb("acc", [128, 128], F32, hs)
                        avec = sb("avec", [128, 4], F32, hs)
                        Sin = sb("Sin", [128, 128], BF16, hs)

                        stages = {}
                        def head_setup(h):
                            wq = load_w(w_in[(l * 28 + 0 + h) * 128:(l * 28 + 1 + h) * 128, :])
                            wf = load_w(w_in[(l * 28 + 4 + h) * 128:(l * 28 + 5 + h) * 128, :])
                            wi = load_w(w_in[(l * 28 + 8 + h) * 128:(l * 28 + 9 + h) * 128, :])
                            wg = load_w(w_in[(l * 28 + 12 + h) * 128:(l * 28 + 13 + h) * 128, :])
                            lb_c = lbt[:, 4 * lp + h:4 * lp + h + 1]
                            oml_c = lbt[:, 16 + 4 * lp + h:16 + 4 * lp + h + 1]
                            noml_c = lbt[:, 32 + 4 * lp + h:32 + 4 * lp + h + 1]

                            def stageA(tb, h=h, wq=wq, wf=wf, wi=wi, wg=wg, lb_c=lb_c, oml_c=oml_c, noml_c=noml_c):
                                g_ = h * NTB + tb
                                q_, k_, v_, sc_ = qt[g_ % 3], kt[g_ % 2], vtok[g_ % 3], scal[g_ % 2]
                                if tb == 0:
                                    S.op("dve", lambda e: e.memset(Bprev[:, :], 0.0), w=[Bprev.u()])
                                bf, bq, bg, bv = pbank(), pbank(), pbank(), pbank()
                                inproj_fm(wf, tb, bf)
                                inproj_fm(wq, tb, bq)
                                inproj_fm(wg, tb, bg)
                                for tt_ in range(4):
                                    for k in range(8):
                                        c0 = k * HW + 2 + tb * TB + tt_ * 128
                                        mm(bv[:, tt_ * 128:(tt_ + 1) * 128], hT[:, c0:c0 + 128], wi[:, k * 128:(k + 1) * 128], k == 0, k == 7,
                                           r=[wi.u(), hT.u((k, tb))], w=[bv.u()], signal=(k == 7))
                                act(tsig[:, :], bf[:, :], AF.Sigmoid, r=[bf.u()], w=[tsig.u()])
                                act(tEp[:, :], tsig[:, :], AF.Ln, r=[tsig.u(), lbt.u()], w=[tEp.u()], scale=oml_c, bias=lb_c)
                                act(tq[:, :], bq[:, :], AF.Silu, r=[bq.u()], w=[tq.u()])
                                act(mixT[:, h * T + tb * TB:h * T + (tb + 1) * TB], bg[:, :], AF.Silu, r=[bg.u()], w=[mixT.u((h, tb))])
                                S.op("dve", lambda e: e.tensor_tensor_scan(out=tb_[:, :], data0=cb[:, C_RST:C_RST + TB], data1=tEp[:, :], initial=0.0,
                                                                           op0=ALU.mult, op1=ALU.add), r=[cb.u(), tEp.u()], w=[tb_.u()])
                                b3 = tb_[:, :].rearrange("p (c t) -> p c t", t=64)
                                bm3 = tbm[:, :].rearrange("p (c t) -> p c t", t=64)
                                tt(bm3, b3, b3[:, :, 31:32].to_broadcast([128, 8, 64]), ALU.subtract, r=[tb_.u()], w=[tbm.u()])
                                S.op("dve", lambda e: e.tensor_tensor_scan(out=tB[:, :], data0=cb[:, C_ONE:C_ONE + TB], data1=tEp[:, :], initial=Bprev[:, 0:1],
                                                                           op0=ALU.mult, op1=ALU.add), r=[cb.u(), tEp.u(), Bprev.u()], w=[tB.u()])
                                cpy(Bprev[:, :], tB[:, TB - 1:TB], r=[tB.u()], w=[Bprev.u()])
                                tsc(tk[:, :], tsig[:, :], noml_c, oml_c, ALU.mult, ALU.add, r=[tsig.u(), lbt.u()], w=[tk.u()])
                                act(sc_[:, 0:8].rearrange("p (c o) -> p c o", o=1), b3[:, :, 63:64], AF.Exp, r=[tb_.u()], w=[sc_.u()])
                                act(sc_[:, 8:16].rearrange("p (c o) -> p c o", o=1), bm3[:, :, 63:64], AF.Exp, r=[tbm.u()], w=[sc_.u()])
                                act(sc_[:, 16:24].rearrange("p (c o) -> p c o", o=1), b3[:, :, 31:32], AF.Exp, r=[tb_.u()], w=[sc_.u()])
                                act(tEp[:, :], tbm[:, :], AF.Exp, r=[tbm.u()], w=[tEp.u()])
                                act(tbm[:, :], tbm[:, :], AF.Exp, r=[tbm.u()], w=[tbm.u()], scale=-1.0)
                                act(tB[:, :], tB[:, :], AF.Exp, r=[tB.u()], w=[tB.u()])
                                stt(q_[:, :], tq[:, :], QSCALE, tEp[:, :], ALU.mult, ALU.mult, r=[tq.u(), tEp.u()], w=[q_.u()])
                                tt(tk[:, :], tk[:, :], tbm[:, :], ALU.mult, r=[tk.u(), tbm.u()], w=[tk.u()])
                                cpy(k_[:, :], tk[:, :], r=[tk.u()], w=[k_.u()])
                                kh_ = kh[g_ % 2]
                                tt(kh_[:, :].rearrange("p (c t) -> p c t", t=64), tk[:, :].rearrange("p (c t) -> p c t", t=64),
                                   sc_[:, 8:16].rearrange("p (c o) -> p c o", o=1).to_broadcast([128, 8, 64]), ALU.mult,
                                   r=[tk.u(), sc_.u()], w=[kh_.u()])
                                stt(mixT[:, (4 + h) * T + tb * TB:(4 + h) * T + (tb + 1) * TB], tq[:, :], QSCALE, tB[:, :], ALU.mult, ALU.mult, r=[tq.u(), tB.u()], w=[mixT.u((4 + h, tb))])
                                if tb == NTB - 1:
                                    cpy(xs_s[:, 128:129], tB[:, TB - 1:TB], r=[tB.u()], w=[xs_s.u("d")])
                                cpy(v_[:, :], bv[:, :], r=[bv.u()], w=[v_.u()])

                            def stageB(tb, h=h):
                                g_ = h * NTB + tb
                                q_, k_, v_, sc_ = qt[g_ % 3], kt[g_ % 2], vtok[g_ % 3], scal[g_ % 2]
                                PT_, Stl_ = PT[g_ % 2], Stl[g_ % 2]
                                if tb == 0:
                                    S.op("dve", lambda e: e.memset(Sst2[0][:, :], 0.0), w=[Sst2[0].u()])
                                kh_ = kh[g_ % 2]
                                for j in range(4):
                                    S.op("pe", lambda e, j=j, kh_=kh_: e.transpose(ptr[:, j * 128:(j + 1) * 128], kh_[:, j * 128:(j + 1) * 128], ident_bf),
                                         r=[kh_.u(), cb.u()], w=[ptr.u()], signal=(j == 3))
                                tsc(ktokA[:, :], ptr[:, 0:TB], pcol(P_MA), None, ALU.mult, None, r=[ptr.u(), par.u()], w=[ktokA.u()])
                                tsc(ktokB[:, :], ptr[:, 0:TB], pcol(P_MB), None, ALU.mult, None, r=[ptr.u(), par.u()], w=[ktokB.u()])
                                bs = pbank()
                                for j in range(4):
                                    mm(bs[:, j * 128:(j + 1) * 128], k_[:, j * 128:(j + 1) * 128], q_[:, j * 128:(j + 1) * 128], True, True,
                                       r=[k_.u(), q_.u()], w=[bs.u()], signal=(j == 3))
                                tt(PT_[:, :], bs[:, :], cb[:, C_MASK:C_MASK + TB], ALU.mult, r=[bs.u(), cb.u()], w=[PT_.u()])
                                bu = [pbank(), pbank()]
                                for c in range(8):
                                    jj = c // 2
                                    b_ = bu[c // 4]
                                    kx = ktokB if c % 2 else ktokA
                                    mm(b_[:, (c % 4) * 128:(c % 4 + 1) * 128], kx[:, jj * 128:(jj + 1) * 128], v_[:, jj * 128:(jj + 1) * 128],
                                       True, True, r=[kx.u(), v_.u()], w=[b_.u()], signal=(c % 4 == 3))
                                for c in range(8):
                                    b_ = bu[c // 4]
                                    Sa, Sb = Sst2[c % 2], Sst2[(c + 1) % 2]
                                    stt(Sb[:, :], Sa[:, :], sc_[:, c:c + 1], b_[:, (c % 4) * 128:(c % 4 + 1) * 128], ALU.mult, ALU.add,
                                        r=[Sa.u(), sc_.u(), b_.u()], w=[Sb.u()])
                                    act(Stl_[:, c * 128:(c + 1) * 128], Sa[:, :], AF.Copy, r=[Sa.u(), sc_.u()], w=[Stl_.u(c)], scale=sc_[:, 16 + c:17 + c])
                                if tb == NTB - 1:
                                    cpy(xs_s[:, 0:128], Sst2[0][:, :], r=[Sst2[0].u()], w=[xs_s.u("s")])
                                    stages[("ag", h)] = exchange(xs_s, 132, "s", [xs_s.u("d"), xs_s.u("s")])

                            def stageC(tb, h=h):
                                g_ = h * NTB + tb
                                oloc = oloc2[h % 2]
                                q_, v_ = qt[g_ % 3], vtok[g_ % 3]
                                PT_, Stl_ = PT[g_ % 2], Stl[g_ % 2]
                                bo = pbank()
                                for c in range(8):
                                    jj = c // 2
                                    mm(bo[:, c * 64:(c + 1) * 64], v_[:, jj * 128:(jj + 1) * 128], PT_[:, c * 64:(c + 1) * 64], True, False,
                                       r=[v_.u(), PT_.u()], w=[bo.u()], signal=False)
                                    mm(bo[:, c * 64:(c + 1) * 64], Stl_[:, c * 128:(c + 1) * 128], q_[:, c * 64:(c + 1) * 64], False, True,
                                       r=[Stl_.u(c), q_.u()], w=[bo.u()], signal=(c == 7))
                                cpy(oloc[:, tb * TB:(tb + 1) * TB], bo[:, :], r=[bo.u()], w=[oloc.u(tb)])

                            def epilogue(h=h):
                                oloc = oloc2[h % 2]
                                ag, go = stages[("ag", h)]
                                exchange_recv(ag, go, xg_s, 132, 3)
                                G = lambda i, a, b: xg_s[:, 132 * i + a:132 * i + b]
                                ru = [xg_s.u(), par.u()]
                                tsc(acc[:, :], G(0, 0, 128), pcol(P_MLT + 0), None, ALU.mult, None, r=ru, w=[acc.u()])
                                for i in (1, 2):
                                    tsc(avec[:, i:i + 1], G(i, 128, 129), pcol(P_MLT + i), pcol(P_OMM + i), ALU.mult, ALU.add, r=ru, w=[avec.u()])
                                    tsc(acc[:, :], acc[:, :], avec[:, i:i + 1], None, ALU.mult, None, r=[acc.u(), avec.u()], w=[acc.u()])
                                    stt(acc[:, :], G(i, 0, 128), pcol(P_MLT + i), acc[:, :], ALU.mult, ALU.add, r=ru + [acc.u()], w=[acc.u()])
                                cpy(Sin[:, :], acc[:, :], r=[acc.u()], w=[Sin.u()])
                                for tb0 in (0, 2):
                                    tbs = (tb0, tb0 + 1)
                                    bcs, os_ap, sqs, rss = {}, {}, {}, {}
                                    for tb in tbs:
                                        bcs[tb] = pbank()
                                        mm(bcs[tb][:, :], Sin[:, :], mixT[:, (4 + h) * T + tb * TB:(4 + h) * T + (tb + 1) * TB], True, True,
                                           r=[Sin.u(), mixT.u((4 + h, tb))], w=[bcs[tb].u()], signal=True)
                                    for tb in tbs:
                                        os_ap[tb] = oloc[:, tb * TB:(tb + 1) * TB]
                                        tt(os_ap[tb], os_ap[tb], bcs[tb][:, :], ALU.add, r=[oloc.u(tb), bcs[tb].u()], w=[oloc.u(tb)])
                                    for tb in tbs:
                                        sq = sqr[st["sq"]]
                                        st["sq"] = (st["sq"] + 1) % 2
                                        sqs[tb] = sq
                                        act(sq[:, :], os_ap[tb], AF.Square, r=[oloc.u(tb)], w=[sq.u()])
                                    for tb in tbs:
                                        pn = pnorm[tb % 2]
                                        mm(pn[:, :], ones_bf, sqs[tb][:, :], True, True, r=[sqs[tb].u(), cb.u()], w=[pn.u()], signal=True)
                                    for tb in tbs:
                                        rss[tb] = finish_rstd(pnorm[tb % 2], 1.0 / 128)
                                    for tb in tbs:
                                        stt(os_ap[tb], os_ap[tb], pcol(P_NG + 4 * lp + h), rss[tb][:, :], ALU.mult, ALU.mult,
                                            r=[oloc.u(tb), rss[tb].u(), par.u()], w=[oloc.u(tb)])
                                    for tb in tbs:
                                        mc = h * T + tb * TB
                                        tt(mixT[:, mc:mc + TB], os_ap[tb], mixT[:, mc:mc + TB], ALU.mult, r=[oloc.u(tb), mixT.u((h, tb))], w=[mixT.u((h, tb))])
                            return stageA, stageB, stageC, epilogue

                        NG = NH * NTB
                        hfun = {}
                        for s_ in range(NG + 2):
                            LA, LB, LC = [], [], []
                            if s_ < NG:
                                if s_ % NTB == 0:
                                    hfun[s_ // NTB] = head_setup(s_ // NTB)
                                LA = S.capture(lambda: hfun[s_ // NTB][0](s_ % NTB))
                            if 0 <= s_ - 1 < NG:
                                LB = S.capture(lambda: hfun[(s_ - 1) // NTB][1]((s_ - 1) % NTB))
                            if 0 <= s_ - 2 < NG:
                                LC = S.capture(lambda: hfun[(s_ - 2) // NTB][2]((s_ - 2) % NTB))
                            for e_, th_ in LA:
                                if e_ == "pe":
                                    th_()
                            ep = [x for x in LA if x[0] != "pe"]
                            i_, j_ = 0, 0
                            while i_ < len(ep) or j_ < len(LB):
                                if i_ < len(ep):
                                    ep[i_][1]()
                                    i_ += 1
                                if j_ < len(LB):
                                    LB[j_][1]()
                                    j_ += 1
                            for _, th_ in LC:
                                th_()
                            if s_ >= 6 and (s_ - 6) % NTB == 0 and (s_ - 6) // NTB < NH - 1:
                                hfun[(s_ - 6) // NTB][3]()
                        hfun[NH - 1][3]()
                    S.barrier()
                    bp(l, "hgrn")
                    with ExitStack() as cs:
                        chf = sb("chf", [128, T + 2], F32, cs)
                        cacc = sb("cacc", [128, T], F32, cs)
                        Bs = sb("Bs", [128, T], F32, cs)
                        cC = [sb("cC%d" % i, [128, TB], F32, cs) for i in range(2)]
                        for j in range(4):
                            wB = load_w(w_in[(l * 28 + 16 + j) * 128:(l * 28 + 17 + j) * 128, :])
                            wC = load_w(w_in[(l * 28 + 20 + j) * 128:(l * 28 + 21 + j) * 128, :])
                            wH = load_w(w_in[(l * 28 + 24 + j) * 128:(l * 28 + 25 + j) * 128, :])
                            bk = pbank()
                            inproj_halo(wC, bk, 0)
                            bk2 = pbank()
                            inproj_halo(wH, bk2, 0)
                            act(cC[0][:, 0:2], bk[:, 0:2], AF.Copy, r=[bk.u()], w=[cC[0].u()])
                            tt(chf[:, 0:2], cC[0][:, 0:2], bk2[:, 0:2], ALU.mult, r=[cC[0].u(), bk2.u()], w=[chf.u("halo")])
                            for tb in range(NTB):
                                bC, bH, bB = pbank(), pbank(), pbank()
                                inproj_fm(wC, tb, bC)
                                inproj_fm(wH, tb, bH)
                                inproj_fm(wB, tb, bB)
                                c = cC[tb % 2]
                                act(c[:, :], bC[:, :], AF.Copy, r=[bC.u()], w=[c.u()])
                                tt(chf[:, 2 + tb * TB:2 + (tb + 1) * TB], c[:, :], bH[:, :], ALU.mult, r=[c.u(), bH.u()], w=[chf.u(tb)])
                                act(Bs[:, tb * TB:(tb + 1) * TB], bB[:, :], AF.Copy, r=[bB.u()], w=[Bs.u(tb)])
                            def rd_(tb):
                                return [chf.u(tb), par.u()] + ([chf.u(tb - 1)] if tb else [chf.u("halo")])
                            for tb in range(NTB):
                                b0 = tb * TB
                                tsc(cacc[:, b0:b0 + TB], chf[:, b0 + 2:b0 + 2 + TB], pcol(P_CW + lp * 12 + 8 + j), None, ALU.mult, None, r=rd_(tb), w=[cacc.u(tb)])
                            for tb in range(NTB):
                                b0 = tb * TB
                                o = cacc[:, b0:b0 + TB]
                                stt(o, chf[:, b0 + 1:b0 + 1 + TB], pcol(P_CW + lp * 12 + 4 + j), o, ALU.mult, ALU.add, r=rd_(tb) + [cacc.u(tb)], w=[cacc.u(tb)])
                            for tb in range(NTB):
                                b0 = tb * TB
                                o = cacc[:, b0:b0 + TB]
                                stt(o, chf[:, b0:b0 + TB], pcol(P_CW + lp * 12 + j), o, ALU.mult, ALU.add, r=rd_(tb) + [cacc.u(tb)], w=[cacc.u(tb)])
                            for tb in range(NTB):
                                b0 = tb * TB
                                mc = (4 + j) * T + tb * TB
                                tt(mixT[:, mc:mc + TB], cacc[:, b0:b0 + TB], Bs[:, b0:b0 + TB], ALU.mult, r=[cacc.u(tb), Bs.u(tb)], w=[mixT.u((4 + j, tb))])
                    S.barrier()
                    bp(l, "conv")
                    hts.close()
                    with ExitStack() as os_:
                        mo2 = [sb("mo%d" % i, [128, 8 * 1024], F32, os_) for i in range(2)]
                        tres = sb("tres", [128, TB], F32, os_)

                        def op_mm(th):
                            mo = mo2[th]
                            for i in range(8):
                                wo = load_w(w_out[(l * 8 + i) * 128:(l * 8 + i + 1) * 128, :])
                                for t2 in range(2):
                                    tb = th * 2 + t2
                                    bk = pbank()
                                    for k in range(8):
                                        mm(bk[:, :], wo[:, k * 128:(k + 1) * 128], mixT[:, k * T + tb * TB:k * T + (tb + 1) * TB], k == 0, k == 7,
                                           r=[wo.u(), mixT.u((k, tb))], w=[bk.u()], signal=(k == 7))
                                    mo_ap = mo[:, i * 1024 + t2 * TB:i * 1024 + (t2 + 1) * TB]
                                    act(mo_ap, bk[:, :], AF.Copy, r=[bk.u()], w=[mo.u((i, t2))])
                                    sq = sqr[st["sq"]]
                                    st["sq"] = (st["sq"] + 1) % 2
                                    act(sq[:, :], bk[:, :], AF.Square, r=[bk.u()], w=[sq.u()])
                                    mm(pnorm[t2][:, :], ones_bf, sq[:, :], i == 0, i == 7, r=[sq.u(), cb.u()], w=[pnorm[t2].u()], signal=True)
                            return [finish_rstd(pnorm[t2], 1.0 / D) for t2 in range(2)]

                        def op_res(th, rss):
                            mo = mo2[th]
                            for t2 in range(2):
                                tb = th * 2 + t2
                                rs = rss[t2]
                                for i in range(8):
                                    mo_ap = mo[:, i * 1024 + t2 * TB:i * 1024 + (t2 + 1) * TB]
                                    stt(mo_ap, mo_ap, pcol(P_GQM + 8 * lp + i), rs[:, :], ALU.mult, ALU.mult, r=[mo.u((i, t2)), rs.u(), par.u()], w=[mo.u((i, t2))])
                                for i in range(8):
                                    mo_ap = mo[:, i * 1024 + t2 * TB:i * 1024 + (t2 + 1) * TB]
                                    xa = xT[:, i * T + tb * TB:i * T + (tb + 1) * TB]
                                    tt(xa, xa, mo_ap, ALU.add, r=[xT.u((i, tb)), mo.u((i, t2))], w=[xT.u((i, tb))])

                        rss1 = op_mm(1)
                        op_res(1, rss1)
                        rss0 = op_mm(0)
                        pn = pnorm[1]
                        for k in range(8):
                            sq = sqr[st["sq"]]
                            st["sq"] = (st["sq"] + 1) % 2
                            act(sq[:, 0:2], xT[:, k * T + T - 2:k * T + T], AF.Square, r=[xT.u((k, NTB - 1))], w=[sq.u()])
                            mm(pn[:, 0:2], ones_bf, sq[:, 0:2], k == 0, k == 7, r=[sq.u(), cb.u()], w=[pn.u()], signal=True)
                        act(lnv[0][:, 0:2], pn[:, 0:2], AF.Ln, r=[pn.u()], w=[lnv[0].u()], scale=1.0 / D, bias=EPS)
                        act(lnv[0][:, 0:2], lnv[0][:, 0:2], AF.Exp, r=[lnv[0].u()], w=[lnv[0].u()], scale=-0.5)
                        for k in range(8):
                            stt(xs_h[:, 2 * k:2 * k + 2], xT[:, k * T + T - 2:k * T + T], pcol(P_GPF + 8 * lp + k), lnv[0][:, 0:2], ALU.mult, ALU.mult,
                                r=[xT.u((k, NTB - 1)), lnv[0].u(), par.u()], w=[xs_h.u()])
                        ffn_ag = exchange(xs_h, 16, "hf")
                        op_res(0, rss0)
                    S.barrier()
            S.barrier()
            bp(l, "mix")
            HH = 1024 + 2
            with ExitStack() as fs:
                hF = sb("hF", [128, 8 * HH], BF16, fs)
                ug = sb("ug", [128, 1024 + 2], F32, fs)
                uv = sb("uv", [128, 1024 + 2], F32, fs)
                cg = [sb("cg%d" % i, [128, TB], F32, fs) for i in range(2)]
                cv = [sb("cv%d" % i, [128, TB], F32, fs) for i in range(2)]
                car = sb("car", [128, 4 * NFC], F32, fs)
                aT = sb("aT", [128, NFC * 1024], BF16, fs)
                yT = sb("yT", [128, 8 * 1024], F32, fs)
                tres = sb("tres2", [128, TB], F32, fs)
                ag, go = ffn_ag
                exchange_recv(ag, go, xg_h, 16)
                tsc(hal[:, :], xg_h[:, 0:16], pcol(P_SEL + 0), None, ALU.mult, None, r=[xg_h.u(), par.u()], w=[hal.u()])
                for i in range(1, 4):
                    stt(hal[:, :], xg_h[:, 16 * i:16 * i + 16], pcol(P_SEL + i), hal[:, :], ALU.mult, ALU.add,
                        r=[xg_h.u(), hal.u(), par.u()], w=[hal.u()])
                for th in range(2):
                    if th == 0:
                        for k in range(8):
                            cpy(hF[:, k * HH:k * HH + 2], hal[:, 2 * k:2 * k + 2], r=[hal.u()], w=[hF.u((k, "halo"))])
                    prenorm(hF, HH, P_GPF, [2 * th, 2 * th + 1], 2 * th)
                    for j in range(NFC):
                        wG = load_w(w_up[(l * 44 + j) * 128:(l * 44 + j + 1) * 128, :])
                        wV = load_w(w_up[(l * 44 + 22 + j) * 128:(l * 44 + 23 + j) * 128, :])
                        fo = P_FCW + lp * 132
                        halves = ((wG, ug, 0, j), (wV, uv, 2 * NFC, 22 + j))
                        for (wsl, uF, cofs, fofs) in halves:
                            if th == 0:
                                bh = pbank()
                                for k in range(8):
                                    mm(bh[:, 0:2], wsl[:, k * 128:(k + 1) * 128], hF[:, k * HH:k * HH + 2], k == 0, k == 7,
                                       r=[wsl.u(), hF.u((k, "halo"))], w=[bh.u()], signal=(k == 7))
                                act(uF[:, 0:2], bh[:, 0:2], AF.Copy, r=[bh.u()], w=[uF.u("halo")])
                            else:
                                cpy(uF[:, 0:2], car[:, cofs + 2 * j:cofs + 2 * j + 2], r=[car.u((cofs, j))], w=[uF.u("halo")])
                        for t2 in range(2):
                            cs = (cg[t2], cv[t2])
                            for hi, (wsl, uF, cofs, fofs) in enumerate(halves):
                                bk = pbank()
                                for k in range(8):
                                    c0 = k * HH + 2 + t2 * TB
                                    mm(bk[:, :], wsl[:, k * 128:(k + 1) * 128], hF[:, c0:c0 + TB], k == 0, k == 7,
                                       r=[wsl.u(), hF.u((k, t2))], w=[bk.u()], signal=(k == 7))
                                act(uF[:, 2 + t2 * TB:2 + (t2 + 1) * TB], bk[:, :], AF.Copy, r=[bk.u()], w=[uF.u(t2)])
                                act(cs[hi][:, :], bk[:, :], AF.Copy, r=[bk.u(), par.u()], w=[cs[hi].u()], scale=pcol(fo + 88 + fofs))
                            for tap, sh in ((1, 1), (0, 0)):
                                for hi, (wsl, uF, cofs, fofs) in enumerate(halves):
                                    c_ = cs[hi]
                                    rd = [uF.u(t2), uF.u(t2 - 1) if t2 else uF.u("halo"), par.u(), c_.u()]
                                    stt(c_[:, :], uF[:, sh + t2 * TB:sh + (t2 + 1) * TB], pcol(fo + tap * 44 + fofs), c_[:, :], ALU.mult, ALU.add, r=rd, w=[c_.u()])
                            act(cs[0][:, :], cs[0][:, :], AF.Silu, r=[cs[0].u()], w=[cs[0].u()])
                            ac = j * 1024 + t2 * TB
                            tt(aT[:, ac:ac + TB], cs[0][:, :], cs[1][:, :], ALU.mult, r=[cs[0].u(), cs[1].u()], w=[aT.u((j, t2))])
                        if th == 0:
                            for (wsl, uF, cofs, fofs) in halves:
                                cpy(car[:, cofs + 2 * j:cofs + 2 * j + 2], uF[:, 1024:1026], r=[uF.u(1)], w=[car.u((cofs, j))])
                    for i in range(8):
                        wd = [load_w(w_dn[(l * 8 + i) * 128:(l * 8 + i + 1) * 128, 0:1024]),
                              load_w(w_dn[(l * 8 + i) * 128:(l * 8 + i + 1) * 128, 1024:2048]),
                              load_w(w_dn[(l * 8 + i) * 128:(l * 8 + i + 1) * 128, 2048:2816])]
                        for t2 in range(2):
                            bk = pbank()
                            for k in range(NFC):
                                wsl = wd[k // 8]
                                kk = k % 8
                                mm(bk[:, :], wsl[:, kk * 128:(kk + 1) * 128], aT[:, k * 1024 + t2 * TB:k * 1024 + (t2 + 1) * TB], k == 0, k == NFC - 1,
                                   r=[wsl.u(), aT.u((k, t2))], w=[bk.u()], signal=(k == NFC - 1))
                            y_ap = yT[:, i * 1024 + t2 * TB:i * 1024 + (t2 + 1) * TB]
                            act(y_ap, bk[:, :], AF.Copy, r=[bk.u()], w=[yT.u((i, t2))])
                            sq = sqr[st["sq"]]
                            st["sq"] = (st["sq"] + 1) % 2
                            act(sq[:, :], bk[:, :], AF.Square, r=[bk.u()], w=[sq.u()])
                            mm(pnorm[t2][:, :], ones_bf, sq[:, :], i == 0, i == 7, r=[sq.u(), cb.u()], w=[pnorm[t2].u()], signal=True)
                    for t2 in range(2):
                        tb = th * 2 + t2
                        rs = finish_rstd(pnorm[t2], 1.0 / D)
                        for i in range(8):
                            y_ap = yT[:, i * 1024 + t2 * TB:i * 1024 + (t2 + 1) * TB]
                            stt(y_ap, y_ap, pcol(P_GQF + 8 * lp + i), rs[:, :], ALU.mult, ALU.mult, r=[yT.u((i, t2)), rs.u(), par.u()], w=[yT.u((i, t2))])
                        for i in range(8):
                            y_ap = yT[:, i * 1024 + t2 * TB:i * 1024 + (t2 + 1) * TB]
                            xa = xT[:, i * T + tb * TB:i * T + (tb + 1) * TB]
                            tt(xa, xa, y_ap, ALU.add, r=[xT.u((i, tb)), yT.u((i, t2))], w=[xT.u((i, tb))])
            S.barrier()
            bp(l, "ffn")

          except _Stop:
            break
        S.enabled = True
        for _i in range(int(os.environ.get("K_DUMMY_PE", "0"))):
            mm(pring[0][:, 0:2], ones_bf, cb[:, 0:2], True, True, r=[cb.u()], w=[pring[0].u()], signal=(_i % 64 == 63))
        for _i in range(int(os.environ.get("K_DUMMY_ACT", "0"))):
            act(lnv[0][:, 0:8], lnv[0][:, 0:8], AF.Copy, r=[lnv[0].u()], w=[lnv[0].u()])
        for _i in range(int(os.environ.get("K_DUMMY_DVE", "0"))):
            cpy(lnv[1][:, 0:8], lnv[1][:, 8:16], r=[lnv[1].u()], w=[lnv[1].u()])
        for _i in range(int(os.environ.get("K_DUMMY_SLOW", "0"))):
            act(xT[:, 0:2048], xT[:, 0:2048], AF.Copy, r=[xT.u((0, 0)), xT.u((0, 1)), xT.u((0, 2)), xT.u((0, 3))],
                w=[xT.u((0, 0)), xT.u((0, 1)), xT.u((0, 2)), xT.u((0, 3))])
        if os.environ.get("K_DUMMY_HI"):
            for _i in range(nlayers * 44 - 48, nlayers * 44):
                load_w(w_up[_i * 128:(_i + 1) * 128, :])
            for _i in range(nlayers * 8 - 8, nlayers * 8):
                load_w(w_dn[_i * 128:(_i + 1) * 128, 0:1024])
                load_w(w_dn[_i * 128:(_i + 1) * 128, 2048:2816])
        for _i in range(int(os.environ.get("K_DUMMY_DMA", "0"))):
            load_w(w_up[(_i % 40) * 128:(_i % 40 + 1) * 128, :])
        for _i in range(int(os.environ.get("K_DUMMY_CC", "0"))):
            ag_, go_ = exchange(xs_h, 16, "dummy")
            exchange_recv(ag_, go_, xg_h, 16)
        sigs = []
        for k in range(8):
            for tb in range(NTB):
                c0 = k * T + tb * TB
                sigs.append(S.dma("sp", lambda e, c0=c0: e.dma_start(out=out[:, c0:c0 + TB], in_=xT[:, c0:c0 + TB]), r=[xT.u((k, tb))]))
        for key, val in sigs:
            S._wait("sp", key, val)
        S.emit()
        print("kernel: recorded ops", S.nops, {e: len(S.q[e]) for e in S.ENG}, "sigcnt", S.cnt, "dma", [c for _, c in S.dsem][:4], "cc", S.cccnt)
    return nc


def _prep_inputs(x, lb_param, w_in, w_out, conv_w, ffn_w_up, ffn_conv_w, ffn_w_down,
                 hgrn_norm_g, pre_mix_g, post_mix_g, pre_ffn_g, post_ffn_g):
    f = np.float32

    def wl(w, nk):
        L, K, N = w.shape
        nj = N // 128
        a = np.asarray(w, f).reshape(L, nk, 128, nj, 128).transpose(0, 3, 2, 1, 4)
        return np.ascontiguousarray(a).reshape(L * nj * 128, nk * 128)

    w_in_r = wl(w_in, 8)
    w_out_r = wl(w_out, 8)
    w_up_r = wl(ffn_w_up, 8)
    w_dn_r = wl(ffn_w_down, NFC)

    def pk(a, n):
        a = np.asarray(a, f).reshape(DEPTH, n, 128).transpose(2, 0, 1)
        return np.ascontiguousarray(a).reshape(128, DEPTH * n)

    def pk3(a, n):
        a = np.asarray(a, f).reshape(DEPTH, 3, n, 128).transpose(3, 0, 1, 2)
        return np.ascontiguousarray(a).reshape(128, DEPTH * 3 * n)

    base = np.concatenate([pk(lb_param, 4), pk(hgrn_norm_g, 4), pk(pre_mix_g, 8), pk(post_mix_g, 8), pk(pre_ffn_g, 8),
                           pk(post_ffn_g, 8), pk3(conv_w, 4), pk3(ffn_conv_w, 44)], axis=1)
    assert base.shape[1] == P_SEL
    cbm = np.zeros((128, NCB), np.float32)
    cbm[:, C_ONES:C_ONES + 128] = 1.0
    cbm[:, C_ID:C_ID + 128] = np.eye(128, dtype=np.float32)
    s = np.arange(128)[:, None]
    t = np.arange(128)[None, :]
    m2 = ((s // 64) == (t // 64)) & (s <= t)
    cbm[:, C_MASK:C_MASK + 512] = np.tile(m2.astype(np.float32), (1, 4))
    rst = np.ones((128, 512), np.float32)
    rst[:, ::64] = 0.0
    cbm[:, C_RST:C_RST + 512] = rst
    cbm[:, C_ONE:C_ONE + 512] = 1.0
    cbm = cbm.astype(ml_dtypes.bfloat16)

    xs = np.asarray(x, f)
    in_maps = []
    for c in range(8):
        b, sgm = c // 4, c % 4
        xc = xs[b, sgm * T:(sgm + 1) * T, :]
        xc = np.ascontiguousarray(xc.T.reshape(8, 128, T).transpose(1, 0, 2)).reshape(128, 8 * T)
        extra = np.zeros((128, 14), np.float32)
        extra[:64, 12] = 1.0
        extra[64:, 13] = 1.0
        for i in range(4):
            extra[:, i] = 1.0 if i == sgm - 1 else 0.0
            extra[:, 4 + i] = 1.0 if i < sgm else 0.0
            extra[:, 8 + i] = 0.0 if i < sgm else 1.0
        par = np.ascontiguousarray(np.concatenate([base, extra], axis=1))
        in_maps.append({"xin": xc, "w_in": w_in_r, "w_out": w_out_r, "w_up": w_up_r, "w_dn": w_dn_r, "par": par, "cb": cbm})
    return in_maps


_NC_CACHE = {}
SPLIT = [(0, 4)]


def _to_fm(xc):
    return np.ascontiguousarray(xc.T.reshape(8, 128, T).transpose(1, 0, 2)).reshape(128, 8 * T)


def kernel(x, lb_param, w_in, w_out, conv_w, ffn_w_up, ffn_conv_w, ffn_w_down,
           hgrn_norm_g, pre_mix_g, post_mix_g, pre_ffn_g, post_ffn_g, _dbg=None, _nlayers=None):
    in_maps = _prep_inputs(x, lb_param, w_in, w_out, conv_w, ffn_w_up, ffn_conv_w, ffn_w_down,
                           hgrn_norm_g, pre_mix_g, post_mix_g, pre_ffn_g, post_ffn_g)
    split = SPLIT if _nlayers is None else [(0, _nlayers)]
    full = {k: in_maps[0][k] for k in ("w_in", "w_out", "w_up", "w_dn")}
    res = None
    for (l0, l1) in split:
        nl = l1 - l0
        key = (_dbg, nl, l0)
        if key not in _NC_CACHE:
            _NC_CACHE[key] = build_program(nl, _dbg, l0)
        nc = _NC_CACHE[key]
        for c, m in enumerate(in_maps):
            m["w_in"] = full["w_in"][l0 * 28 * 128:l1 * 28 * 128]
            m["w_out"] = full["w_out"][l0 * 8 * 128:l1 * 8 * 128]
            m["w_up"] = full["w_up"][l0 * 44 * 128:l1 * 44 * 128]
            m["w_dn"] = full["w_dn"][l0 * 8 * 128:l1 * 8 * 128]
            if res is not None:
                m["xin"] = np.asarray(res.results[c]["out"])
        res = run_bass_kernel_spmd(nc, in_maps, core_ids=list(range(8)))
    outp = np.empty((2, 4 * T, D), np.float32)
    for c in range(8):
        b, sgm = c // 4, c % 4
        o = np.asarray(res.results[c]["out"]).reshape(128, 8, T).transpose(2, 1, 0).reshape(T, D)
        outp[b, sgm * T:(sgm + 1) * T, :] = o
    return outp
```
